# Optimizing a Trainium2 kernel written in Bass

```python
import math
import jax, jax.numpy as jnp
from jax import lax
import numpy as np

D_MODEL = 1024
BATCH = 2
SEQ = 16384
DEPTH = 2

GRID_W = 64
CTX_LEN = 256
N_MIXERS = 2
EXPAND = 2
E_CONV = EXPAND * D_MODEL
CONV_W = 3
E_SSM = EXPAND * D_MODEL
SSM_GROUP = 16
N_GROUPS = E_SSM // SSM_GROUP
SSM_STATE = 64
SCAN_CHUNK = 128
LN_EPS = 1e-5
DN_ALPHA = (2 * DEPTH) ** 0.25
DN_BETA = (8 * DEPTH) ** -0.25
N_CONV_LAYERS = (DEPTH + 1) // 2
N_SSM_LAYERS = DEPTH // 2

kernel_name = "hybrid_shortconv_s5_prefix_dit"


def _layernorm(x, g, b):
    xf = x.astype(jnp.float32)
    mu = jnp.mean(xf, axis=-1, keepdims=True)
    var = jnp.mean(jnp.square(xf - mu), axis=-1, keepdims=True)
    return ((xf - mu) * lax.rsqrt(var + LN_EPS) * g.astype(jnp.float32) + b.astype(jnp.float32)).astype(x.dtype)


def _ada(cvec, w, b):
    m = jax.nn.silu(cvec) @ w + b
    shift, scale, gate = jnp.split(m, 3, axis=-1)
    return shift[..., None, :], scale[..., None, :], gate[..., None, :]


def _shift_conv(u, w, axis):
    n = u.shape[axis]
    pad = [(0, 0)] * u.ndim
    pad[axis] = (1, 1)
    up = jnp.pad(u, pad)
    sl = lambda k: lax.slice_in_dim(up, k, k + n, axis=axis)
    return sl(0) * w[0] + sl(1) * w[1] + sl(2) * w[2]


def _conv_grid(u, w):
    bsz, L, E = u.shape
    rows = L // GRID_W
    half = E // 2
    ug = u.reshape(bsz, rows, GRID_W, E)
    yh = _shift_conv(ug[..., :half], w[:, :half], axis=2)
    yv = _shift_conv(ug[..., half:], w[:, half:], axis=1)
    return jnp.concatenate([yh, yv], axis=-1).reshape(bsz, L, E)


def _conv_mixer(h, w_in, w_conv, w_out, grid):
    bg, cg, v, z = jnp.split(h @ w_in, 4, axis=-1)
    u = cg * v
    yc = _conv_grid(u, w_conv) if grid else _shift_conv(u, w_conv, axis=1)
    return (bg * yc * jax.nn.silu(z)) @ w_out


def _zoh(lam_re, lam_im, log_step, b_re, b_im):
    dt = jnp.exp(log_step)[:, None]
    mag = jnp.exp(lam_re * dt)
    ar = mag * jnp.cos(lam_im * dt)
    ai = mag * jnp.sin(lam_im * dt)
    qr, qi = ar - 1.0, ai
    den = lam_re * lam_re + lam_im * lam_im
    fr = (qr * lam_re + qi * lam_im) / den
    fi = (qi * lam_re - qr * lam_im) / den
    bbr = fr[..., None] * b_re - fi[..., None] * b_im
    bbi = fr[..., None] * b_im + fi[..., None] * b_re
    return ar, ai, bbr, bbi


def _binop(e1, e2):
    a1r, a1i, b1r, b1i = e1
    a2r, a2i, b2r, b2i = e2
    return (a2r * a1r - a2i * a1i,
            a2r * a1i + a2i * a1r,
            a2r * b1r - a2i * b1i + b2r,
            a2r * b1i + a2i * b1r + b2i)


def _s5_scan(u, h0r, h0i, ar, ai, bbr, bbi, c_re, c_im, with_output):
    bsz, L, _ = u.shape
    n_blk = L // SCAN_CHUNK
    ub = u.reshape(bsz, n_blk, SCAN_CHUNK, N_GROUPS, SSM_GROUP).transpose(1, 0, 2, 3, 4)

    def step(carry, u_blk):
        hr, hi = carry
        bur = jnp.einsum('btgp,gnp->btgn', u_blk, bbr)
        bui = jnp.einsum('btgp,gnp->btgn', u_blk, bbi)
        bur = bur.at[:, 0].add(ar * hr - ai * hi)
        bui = bui.at[:, 0].add(ar * hi + ai * hr)
        a_r = jnp.broadcast_to(ar, bur.shape)
        a_i = jnp.broadcast_to(ai, bur.shape)
        _, _, sr, si = lax.associative_scan(_binop, (a_r, a_i, bur, bui), axis=1)
        new = (sr[:, -1], si[:, -1])
        if with_output:
            y = jnp.einsum('btgn,gpn->btgp', sr, c_re) - jnp.einsum('btgn,gpn->btgp', si, c_im)
            return new, y
        return new, None

    h_final, ys = lax.scan(step, (h0r, h0i), ub)
    if with_output:
        ys = ys.transpose(1, 0, 2, 3, 4).reshape(bsz, L, E_SSM)
    return ys, h_final


def _glu_gate_out(y, z, w_glu, b_glu, w_out):
    g = jax.nn.gelu(y)
    g = g * jax.nn.sigmoid(g @ w_glu + b_glu)
    return (g * jax.nn.silu(z)) @ w_out


def _s5_mixer(h_lat, h_ctx, w_in, lam_re, lam_im, log_step, b_re, b_im, c_re, c_im, d,
              w_glu, b_glu, w_out, ctx_out):
    u_l, z_l = jnp.split(h_lat @ w_in, 2, axis=-1)
    pc = h_ctx @ w_in
    u_c = pc[..., :E_SSM]
    y_l = d * u_l
    y_c = d * u_c if ctx_out else None
    for r in range(2):
        ar, ai, bbr, bbi = _zoh(lam_re[r], lam_im[r], log_step[r], b_re[r], b_im[r])
        seq = (lambda t: t[:, ::-1]) if r == 1 else (lambda t: t)
        dtype = jnp.result_type(u_c.dtype, bbr.dtype)
        h0 = jnp.zeros((u_c.shape[0], N_GROUPS, SSM_STATE), dtype)
        yc, hc = _s5_scan(seq(u_c), h0, h0, ar, ai, bbr, bbi, c_re[r], c_im[r], ctx_out)
        yl, _ = _s5_scan(seq(u_l), hc[0], hc[1], ar, ai, bbr, bbi, c_re[r], c_im[r], True)
        y_l = y_l + seq(yl)
        if ctx_out:
            y_c = y_c + seq(yc)
    out_l = _glu_gate_out(y_l, z_l, w_glu, b_glu, w_out)
    out_c = _glu_gate_out(y_c, pc[..., E_SSM:], w_glu, b_glu, w_out) if ctx_out else None
    return out_l, out_c


def setup_inputs(seed: int = 0) -> dict:
    key = jax.random.key(seed)
    ks = jax.random.split(key, 24)
    f32 = jnp.float32
    nrm = lambda k, shape, s: jax.random.normal(k, shape, f32) * s
    nA, nB = N_CONV_LAYERS, N_SSM_LAYERS
    G, N, P = N_GROUPS, SSM_STATE, SSM_GROUP
    return {
        "x": nrm(ks[0], (BATCH, SEQ, D_MODEL), 1.0),
        "c": nrm(ks[1], (BATCH, D_MODEL), 1.0),
        "ctx": nrm(ks[2], (BATCH, CTX_LEN, D_MODEL), 1.0),
        "c_ctx": nrm(ks[3], (D_MODEL,), 1.0),
        "ada_w": nrm(ks[4], (DEPTH, D_MODEL, 3 * D_MODEL), D_MODEL ** -0.5),
        "ada_b": nrm(ks[5], (DEPTH, 3 * D_MODEL), 0.02),
        "ln_g": 1.0 + nrm(ks[6], (DEPTH, D_MODEL), 0.02),
        "ln_b": nrm(ks[7], (DEPTH, D_MODEL), 0.02),
        "conv_w_in": nrm(ks[8], (nA, D_MODEL, 4 * E_CONV), D_MODEL ** -0.5),
        "conv_w": nrm(ks[9], (nA, CONV_W, E_CONV), CONV_W ** -0.5),
        "conv_w_out": nrm(ks[10], (nA, E_CONV, D_MODEL), DN_BETA * E_CONV ** -0.5),
        "ssm_w_in": nrm(ks[11], (nB, D_MODEL, 2 * E_SSM), D_MODEL ** -0.5),
        "ssm_lam_re": -0.5 * jnp.exp(nrm(ks[12], (nB, 2, G, N), 0.05)),
        "ssm_lam_im": jnp.pi * jnp.arange(N, dtype=f32) + nrm(ks[13], (nB, 2, G, N), 0.05),
        "ssm_log_step": jax.random.uniform(ks[14], (nB, 2, G), f32, math.log(1e-3), math.log(1e-1)),
        "ssm_b_re": nrm(ks[15], (nB, 2, G, N, P), (2 * P) ** -0.5),
        "ssm_b_im": nrm(ks[16], (nB, 2, G, N, P), (2 * P) ** -0.5),
        "ssm_c_re": nrm(ks[17], (nB, 2, G, P, N), N ** -0.5),
        "ssm_c_im": nrm(ks[18], (nB, 2, G, P, N), N ** -0.5),
        "ssm_d": nrm(ks[19], (nB, E_SSM), 1.0),
        "ssm_w_glu": nrm(ks[20], (nB, E_SSM, E_SSM), E_SSM ** -0.5),
        "ssm_b_glu": nrm(ks[21], (nB, E_SSM), 0.02),
        "ssm_w_out": nrm(ks[22], (nB, E_SSM, D_MODEL), DN_BETA * E_SSM ** -0.5),
    }


def reference(x, c, ctx, c_ctx, ada_w, ada_b, ln_g, ln_b, conv_w_in, conv_w, conv_w_out,
              ssm_w_in, ssm_lam_re, ssm_lam_im, ssm_log_step, ssm_b_re, ssm_b_im,
              ssm_c_re, ssm_c_im, ssm_d, ssm_w_glu, ssm_b_glu, ssm_w_out):
    for i in range(DEPTH):
        last = i == DEPTH - 1
        is_conv = (i % N_MIXERS) == 0
        j = i // N_MIXERS
        need_ctx_out = not last
        need_ctx_in = need_ctx_out or not is_conv
        sh, sc, gt = _ada(c, ada_w[i], ada_b[i])
        hx = x * (1.0 + sc) + sh
        if need_ctx_in:
            sh_c, sc_c, gt_c = _ada(c_ctx, ada_w[i], ada_b[i])
            hc = ctx * (1.0 + sc_c) + sh_c
        if is_conv:
            fx = _conv_mixer(hx, conv_w_in[j], conv_w[j], conv_w_out[j], True)
            fc = _conv_mixer(hc, conv_w_in[j], conv_w[j], conv_w_out[j], False) if need_ctx_out else None
        else:
            fx, fc = _s5_mixer(hx, hc, ssm_w_in[j], ssm_lam_re[j], ssm_lam_im[j], ssm_log_step[j],
                               ssm_b_re[j], ssm_b_im[j], ssm_c_re[j], ssm_c_im[j], ssm_d[j],
                               ssm_w_glu[j], ssm_b_glu[j], ssm_w_out[j], need_ctx_out)
        x = _layernorm(DN_ALPHA * x + gt * fx, ln_g[i], ln_b[i])
        if need_ctx_out:
            ctx = _layernorm(DN_ALPHA * ctx + gt_c * fc, ln_g[i], ln_b[i])
    return x
```

```python
from contextlib import ExitStack
import math
import numpy as np
import concourse.bass as bass
import concourse.mybir as mybir
from concourse.bass_utils import run_bass_kernel_spmd

F32 = mybir.dt.float32
BF16 = mybir.dt.bfloat16
ALU = mybir.AluOpType
AF = mybir.ActivationFunctionType
ENGS = ("pe", "act", "dve", "pool", "sp")
NDS = 8

D = 1024
E2 = 2048
NT = 4096
NCX = 256
NTX = NT + NCX
HALO = 64
NXH = NT + 2 * HALO
TCH = 16
NCHL = NT // TCH
NCHX = NCX // TCH
NCHT = NCHL + NCHX
CB = 64
ALPHA = 4.0 ** 0.25
LN_EPS = 1e-5
PI = math.pi
DEBUG = False
MERGED_ZR = True
SAME_ENGINE_SYNC = ("act", "dve", "pool")


class T_:
    __slots__ = ("name", "w", "r")

    def __init__(self, name):
        self.name = name
        self.w = []
        self.r = []


class Prog:
    def __init__(self):
        self.nc = bass.Bass("TRN2", target_bir_lowering=False)
        self.es = ExitStack()
        self.ops = []
        self.dma_rr = {e: 0 for e in ENGS}
        self.last_compute = {}
        self.last_dma = {}
        self.pending_barrier = {}
        self.ntile = 0

    def dram(self, name, shape, dt=F32, kind="Internal"):
        return self.nc.dram_tensor(name, list(shape), dt, kind=kind)

    def sb(self, st, name, shape, dt=F32):
        self.ntile += 1
        return st.enter_context(self.nc.sbuf_tensor(f"sb{self.ntile}_{name}", list(shape), dt))

    def tile(self, name="t"):
        self.ntile += 1
        return T_(f"{name}{self.ntile}")

    def tiles(self, n, name="t"):
        return [self.tile(name) for _ in range(n)]

    def op(self, eng, fn, reads=(), writes=(), dma=False):
        deps = set()
        for t in reads:
            deps.update(t.w)
        for t in writes:
            deps.update(t.w)
            deps.update(t.r)
        if eng in self.pending_barrier:
            deps.update(self.pending_barrier.pop(eng))
        oid = len(self.ops)
        slot = None
        if dma:
            slot = self.dma_rr[eng]
            self.dma_rr[eng] = (slot + 1) % NDS
            prev = self.last_dma.get((eng, slot))
            if prev is not None:
                deps.add(prev)
            self.last_dma[(eng, slot)] = oid
        else:
            self.last_compute[eng] = oid
        self.ops.append([eng, fn, sorted(deps), dma, slot])
        for t in reads:
            t.r.append(oid)
        for t in writes:
            t.w = [oid]
            t.r = []
        return oid

    def barrier(self):
        allp = list(self.last_compute.values()) + list(self.last_dma.values())
        for e in ENGS:
            self.pending_barrier[e] = set(allp) | self.pending_barrier.get(e, set())

    def emit(self):
        nc = self.nc
        ops = self.ops
        n = len(ops)
        needed = [False] * n
        for i, (eng, fn, deps, dma, slot) in enumerate(ops):
            if dma:
                needed[i] = True
            for d in deps:
                de, _, _, ddma, _ = ops[d]
                if de == eng and not ddma and not dma and (de == "pe" or de not in SAME_ENGINE_SYNC):
                    continue
                needed[d] = True
        sem_names = [f"s_{e}" for e in ENGS] + [f"d_{e}_{k}" for e in ENGS for k in range(NDS)]
        sems = {nm: self.es.enter_context(nc.semaphore(nm)) for nm in sem_names}
        cnt = {nm: 0 for nm in sem_names}
        ev = [None] * n
        for i, (eng, fn, deps, dma, slot) in enumerate(ops):
            if not needed[i]:
                continue
            nm = f"d_{eng}_{slot}" if dma else f"s_{eng}"
            cnt[nm] += 16 if dma else 1
            ev[i] = (nm, cnt[nm])
        self.maxsem = max(cnt.values())
        per_eng = {e: [] for e in ENGS}
        for i, o in enumerate(ops):
            per_eng[o[0]].append(i)

        with nc.Block() as block:
            def run(engname, Eh):
                waited = {}
                for i in per_eng[engname]:
                    eng, fn, deps, dma, slot = ops[i]
                    need = {}
                    for d in deps:
                        de, _, _, ddma, _ = ops[d]
                        if de == eng and not ddma and not dma and (de == "pe" or de not in SAME_ENGINE_SYNC):
                            continue
                        nm, v = ev[d]
                        if v > need.get(nm, 0):
                            need[nm] = v
                    for nm, v in need.items():
                        if v > waited.get(nm, 0):
                            Eh.wait_ge(sems[nm], v)
                            waited[nm] = v
                    ins = fn(Eh)
                    if ev[i] is not None:
                        ins.then_inc(sems[ev[i][0]], 16 if dma else 1)
                for nm, c in cnt.items():
                    if nm.startswith(f"d_{engname}_") and c > 0:
                        Eh.wait_ge(sems[nm], c)

            @block.sync
            def _(Eh):
                run("sp", Eh)

            @block.scalar
            def _(Eh):
                run("act", Eh)

            @block.vector
            def _(Eh):
                run("dve", Eh)

            @block.gpsimd
            def _(Eh):
                run("pool", Eh)

            @block.tensor
            def _(Eh):
                run("pe", Eh)
        return nc


def mm(P, out, lhsT, rhs, start, stop, reads, writes, tp=None):
    def f(Eh):
        if tp is not None:
            return Eh.matmul(out, lhsT=lhsT, rhs=rhs, start=start, stop=stop, tile_position=tp)
        return Eh.matmul(out, lhsT=lhsT, rhs=rhs, start=start, stop=stop)
    P.op("pe", f, reads=reads, writes=writes)


def dma(P, q, out, in_, reads=(), writes=()):
    P.op(q, lambda Eh: Eh.dma_start(out=out, in_=in_), reads=reads, writes=writes, dma=True)


def act(P, out, in_, func, reads, writes, scale=1.0, bias=0.0):
    P.op("act", lambda Eh: Eh.activation(out=out, in_=in_, func=func, bias=bias, scale=scale),
         reads=reads, writes=writes)


def tt(P, eng, out, in0, in1, op, reads, writes):
    P.op(eng, lambda Eh: Eh.tensor_tensor(out=out, in0=in0, in1=in1, op=op), reads=reads, writes=writes)


def ts(P, eng, out, in0, s1, op0, reads, writes, s2=None, op1=None):
    if op1 is None:
        P.op(eng, lambda Eh: Eh.tensor_scalar(out=out, in0=in0, scalar1=s1, scalar2=None, op0=op0),
             reads=reads, writes=writes)
    else:
        P.op(eng, lambda Eh: Eh.tensor_scalar(out=out, in0=in0, scalar1=s1, scalar2=s2, op0=op0, op1=op1),
             reads=reads, writes=writes)


def stt(P, eng, out, in0, scalar, in1, op0, op1, reads, writes):
    P.op(eng, lambda Eh: Eh.scalar_tensor_tensor(out=out, in0=in0, scalar=scalar, in1=in1, op0=op0, op1=op1),
         reads=reads, writes=writes)


def cp(P, eng, out, in_, reads, writes):
    if eng == "act":
        P.op("act", lambda Eh: Eh.activation(out=out, in_=in_, func=AF.Copy), reads=reads, writes=writes)
    else:
        P.op(eng, lambda Eh: Eh.tensor_copy(out=out, in_=in_), reads=reads, writes=writes)


def mset(P, eng, ap, val, writes):
    P.op(eng, lambda Eh: Eh.memset(ap, val), writes=writes)


class K:
    pass


def setup_globals(P, k, inputs_decl):
    nc = P.nc
    st = P.es
    k.ps = st.enter_context(nc.psum_tensor("ps_all", [128, 8, 512], F32))
    k.pb = P.tiles(8, "pb")
    k.identf = P.sb(st, "identf", [128, 128], F32)
    k.identb = P.sb(st, "identb", [128, 128], BF16)
    k.t_ident = P.tile("ident")
    k.ones1 = P.sb(st, "ones1", [1, 128], F32)
    k.t_ones = P.tile("ones")
    dma(P, "sp", k.identf[:], inputs_decl["ident"].ap(), writes=[k.t_ident])
    cp(P, "dve", k.identb[:], k.identf[:], [k.t_ident], [k.t_ident])
    mset(P, "dve", k.ones1[:], 1.0, [k.t_ones])
    k.mcol = P.sb(st, "mcol", [128, 24, 2], F32)
    k.t_mcol = P.tile("mcol")
    k.GROW = P.dram("growd", [2, 2, 1024], F32)
    k.t_growd = P.tile("growd")


def ps2(k, b):
    return k.ps[:, b:b + 2, :].rearrange("p a b -> p (a b)")


def ada(P, k, I, layer):
    nc = P.nc
    with ExitStack() as st:
        cv = P.sb(st, "cv", [128, 8, 2], F32)
        scv = P.sb(st, "scv", [128, 8, 2], F32)
        adab = P.sb(st, "adab", [128, 24], F32)
        wch = [P.sb(st, f"adaw{i}", [128, 8, 512], F32) for i in range(2)]
        grow = P.sb(st, "grow", [1, 2, 1024], F32)
        gbrow = P.sb(st, "gbrow", [1, 1024], F32)
        t_cv, t_adab, t_grow, t_gbrow = P.tiles(4, "ada")
        t_w = P.tiles(2, "adaw")
        dma(P, "sp", cv[:], I["cvec"].ap(), writes=[t_cv])
        dma(P, "sp", adab[:], I["adab_col"].ap()[:, layer, :], writes=[t_adab])
        dma(P, "sp", gbrow[:], I["adab_grow"].ap()[:, layer, :], writes=[t_gbrow])
        act(P, scv[:], cv[:], AF.Silu, [t_cv], [t_cv])
        wsrc = I["ada_w"].ap()[layer].rearrange("(dt p) c -> p dt c", p=128)
        pcol = k.ps[:, 0, 0:48].rearrange("p (j c) -> p j c", c=2)
        for jc in range(6):
            w = wch[jc % 2]
            tw = t_w[jc % 2]
            dma(P, "sp", w[:], wsrc[:, :, jc * 512:(jc + 1) * 512], writes=[tw])
            for jt in range(4):
                for dt in range(8):
                    mm(P, pcol[:, jc * 4 + jt, :], w[:, dt, jt * 128:(jt + 1) * 128], scv[:, dt, :],
                       dt == 0, dt == 7, [tw, t_cv], [k.pb[0]])
            if jc >= 4:
                for c in range(2):
                    for dt in range(8):
                        mm(P, k.ps[0:1, 1 + c, 0:512], scv[:, dt, c:c + 1], w[:, dt, :],
                           dt == 0, dt == 7, [tw, t_cv], [k.pb[1 + c]])
                    tt(P, "dve", grow[:, c, (jc - 4) * 512:(jc - 3) * 512], k.ps[0:1, 1 + c, 0:512],
                       gbrow[:, (jc - 4) * 512:(jc - 3) * 512], ALU.add, [k.pb[1 + c], t_gbrow], [t_grow])
        for c in range(2):
            tt(P, "dve", k.mcol[:, :, c], pcol[:, :, c], adab[:], ALU.add, [k.pb[0], t_adab], [k.t_mcol])
        ts(P, "dve", k.mcol[:, 8:16, :], k.mcol[:, 8:16, :], 1.0, ALU.add, [k.t_mcol], [k.t_mcol])
        dma(P, "sp", k.GROW.ap()[layer:layer + 1, :, :], grow[:], reads=[t_grow], writes=[k.t_growd])
    P.barrier()


def load_ln_gate(P, k, I, st, layer):
    k.gateB = P.sb(st, "gateB", [128, 2, 1024], F32)
    k.t_gateB = P.tile("gateB")
    k.lnB = P.sb(st, "lnB", [128, 2, 1024], F32)
    k.t_lnB = P.tile("lnB")
    dma(P, "sp", k.lnB[:, 0, :], I["lngB"].ap()[:, layer, :], writes=[k.t_lnB])
    dma(P, "sp", k.lnB[:, 1, :], I["lnbB"].ap()[:, layer, :], writes=[k.t_lnB])
    for c in range(2):
        dma(P, "sp", k.gateB[:, c, :], k.GROW.ap()[layer, c:c + 1, :].to_broadcast([128, 1024]), reads=[k.t_growd],
            writes=[k.t_gateB])


def ln_residual(P, k, W, psb, xt, t_xt, which, out_ap, t_out, affine=True):
    fx = ps2(k, psb)
    v, t_v = W["v"], W["t_v"]
    sts, mv, t_s = W["st"], W["mv"], W["t_s"]
    tt(P, "dve", v[:], fx, k.gateB[:, which, :], ALU.mult, [k.pb[psb], k.pb[psb + 1], k.t_gateB], [t_v])
    stt(P, "dve", v[:], xt, ALPHA, v[:], ALU.mult, ALU.add, [t_xt, t_v], [t_v])
    P.op("dve", lambda Eh: Eh.bn_stats(out=sts[:, 0:6], in_=v[:, 0:512]), reads=[t_v], writes=[t_s])
    P.op("dve", lambda Eh: Eh.bn_stats(out=sts[:, 6:12], in_=v[:, 512:1024]), reads=[t_v], writes=[t_s])
    P.op("dve", lambda Eh: Eh.bn_aggr(out=mv[:, 0:2], in_=sts[:, 0:12]), reads=[t_s], writes=[t_s])
    act(P, mv[:, 2:3], mv[:, 1:2], AF.Sqrt, [t_s], [t_s], scale=1.0, bias=W["eps"][:, 0:1])
    P.op("dve", lambda Eh: Eh.reciprocal(out=mv[:, 2:3], in_=mv[:, 2:3]), reads=[t_s], writes=[t_s])
    ts(P, "dve", mv[:, 3:4], mv[:, 0:1], mv[:, 2:3], ALU.mult, [t_s], [t_s], s2=-1.0, op1=ALU.mult)
    if not affine:
        act(P, out_ap, v[:], AF.Identity, [t_v, t_s], [t_out], scale=mv[:, 2:3], bias=mv[:, 3:4])
        return
    act(P, v[:], v[:], AF.Identity, [t_v, t_s], [t_v], scale=mv[:, 2:3], bias=mv[:, 3:4])
    tt(P, "pool", v[:], v[:], k.lnB[:, 0, :], ALU.mult, [t_v, k.t_lnB], [t_v])
    tt(P, "pool", out_ap, v[:], k.lnB[:, 1, :], ALU.add, [t_v, k.t_lnB], [t_out])


def ln_work(P, st, n=2):
    Ws = []
    for i in range(n):
        W = {}
        W["v"] = P.sb(st, f"lnv{i}", [128, 1024], F32)
        W["st"] = P.sb(st, f"lnst{i}", [128, 12], F32)
        W["mv"] = P.sb(st, f"lnmv{i}", [128, 4], F32)
        W["eps"] = P.sb(st, f"lneps{i}", [128, 1], F32)
        W["t_v"], W["t_s"], W["t_e"] = P.tiles(3, "lnw")
        mset(P, "dve", W["eps"][:], LN_EPS, [W["t_e"]])
        Ws.append(W)
    return Ws


def layer0(P, k, I, X1, CTX1, G0, slot=None, do_ctx=True, do_ada=True, ln_affine=True):
    sfx = "" if slot is None else str(slot)
    if do_ada:
        ada(P, k, I, 0)
    with ExitStack() as st:
        wout = P.sb(st, "wout", [128, 16, 1024], BF16)
        t_wout = P.tile("wout")
        for q in range(4):
            dma(P, "pool", wout[:, q * 4:(q + 1) * 4, :],
                I["conv_w_out"].ap().rearrange("(j p) d -> p j d", p=128)[:, q * 4:(q + 1) * 4, :], writes=[t_wout])
        with ExitStack() as st2:
            hxT = P.sb(st2, "hxT", [128, 8, NXH], BF16)
            hcT = P.sb(st2, "hcT", [128, 8, NCX], BF16)
            t_hx = P.tiles(8, "hx")
            t_hc = P.tile("hc")
            cw = P.sb(st2, "cw", [128, 16, 3], F32)
            edge = P.sb(st2, "edge", [128, 2], F32)
            t_cw = P.tile("cw")
            dma(P, "sp", cw[:], I["cw"].ap(), writes=[t_cw])
            dma(P, "sp", edge[:], I["edge" + sfx].ap(), writes=[t_cw])
            with ExitStack() as st3:
                xs = [P.sb(st3, f"xs{i}", [128, NXH], F32) for i in range(2)]
                xcs = P.sb(st3, "xcs", [128, 8, NCX], F32)
                t_xs = P.tiles(2, "xs")
                t_xcs = P.tile("xcs")
                if do_ctx:
                    dma(P, "sp", xcs[:], I["ctxT"].ap().rearrange("(dt p) t -> p dt t", p=128), writes=[t_xcs])
                for dt in range(8):
                    s = dt % 2
                    dma(P, "sp", xs[s][:], I["xT" + sfx].ap()[dt * 128:(dt + 1) * 128, :], writes=[t_xs[s]])
                    act(P, hxT[:, dt, :], xs[s][:], AF.Identity, [t_xs[s], k.t_mcol], [t_hx[dt]],
                        scale=k.mcol[:, 8 + dt, 0:1], bias=k.mcol[:, dt, 0:1])
                    if do_ctx:
                        act(P, hcT[:, dt, :], xcs[:, dt, :], AF.Identity, [t_xcs, k.t_mcol], [t_hc],
                            scale=k.mcol[:, 8 + dt, 1:2], bias=k.mcol[:, dt, 1:2])
            P.barrier()
            with ExitStack() as st3:
                wsl = [P.sb(st3, f"wsl{i}", [128, 8, 4, 128], BF16) for i in range(2)]
                t_wsl = P.tiles(2, "wsl")
                gst = [P.sb(st3, f"gst{i}", [128, NTX], BF16) for i in range(2)]
                t_gst = P.tiles(2, "gst")
                NW = 2
                cgs = [P.sb(st3, f"cgs{i}", [128, 640], F32) for i in range(NW)]
                uu = [P.sb(st3, f"uu{i}", [128, 640], F32) for i in range(NW)]
                yc = [P.sb(st3, f"yc{i}", [128, 512], F32) for i in range(NW)]
                sz = [P.sb(st3, f"sz{i}", [128, 512], F32) for i in range(NW)]
                t1 = [P.sb(st3, f"t1{i}", [128, 512], F32) for i in range(NW)]
                t_cgs, t_uu, t_yc, t_sz, t_t1 = (P.tiles(NW, "w") for _ in range(5))
                wsrc = I["conv_w_in"].ap().rearrange("(dt p) c -> p dt c", p=128)

                def load_w(j):
                    for part in range(4):
                        dma(P, "pool", wsl[j % 2][:, :, part, :],
                            wsrc[:, :, part * 2048 + j * 128: part * 2048 + (j + 1) * 128], writes=[t_wsl[j % 2]])
                load_w(0)
                it = 0
                for j in range(16):
                    if j + 1 < 16:
                        load_w(j + 1)
                    w = wsl[j % 2]
                    tw = t_wsl[j % 2]
                    vert = j >= 8
                    g = gst[j % 2]
                    tg = t_gst[j % 2]
                    for blk in range(9 if do_ctx else 8):
                        ws_ = it % NW
                        it += 1
                        isctx = blk == 8
                        n = NCX if isctx else 512
                        ext = vert and not isctx
                        def src(dt, c0, c1):
                            if isctx:
                                return hcT[:, dt, c0:c1], [t_hc]
                            return hxT[:, dt, c0:c1], [t_hx[dt]]
                        base = 0 if isctx else HALO + blk * 512
                        for part, bank in ((1, 0), (2, 2)):
                            if ext:
                                for dt in range(8):
                                    r, tr = src(dt, base - 64, base + 448)
                                    mm(P, k.ps[:, bank, :], w[:, dt, part, :], r, dt == 0, dt == 7, [tw] + tr, [k.pb[bank]])
                                for dt in range(8):
                                    r, tr = src(dt, base + 448, base + 576)
                                    mm(P, k.ps[:, bank + 1, 0:128], w[:, dt, part, :], r, dt == 0, dt == 7, [tw] + tr,
                                       [k.pb[bank + 1]])
                            else:
                                for dt in range(8):
                                    r, tr = src(dt, base, base + n)
                                    mm(P, k.ps[:, bank, 0:n], w[:, dt, part, :], r, dt == 0, dt == 7, [tw] + tr, [k.pb[bank]])
                        bz, bbg = 5 + 2 * (it % 2), 4 + 2 * (it % 2)
                        for part, bank in ((3, bz), (0, bbg)):
                            for dt in range(8):
                                r, tr = src(dt, base, base + n)
                                mm(P, k.ps[:, bank, 0:n], w[:, dt, part, :], r, dt == 0, dt == 7, [tw] + tr, [k.pb[bank]])
                        ne = 640 if ext else n
                        pcg = ps2(k, 0)[:, 0:ne]
                        pv = ps2(k, 2)[:, 0:ne]
                        cp(P, "act", cgs[ws_][:, 0:ne], pcg, [k.pb[0], k.pb[1]], [t_cgs[ws_]])
                        tt(P, "dve", uu[ws_][:, 0:ne], cgs[ws_][:, 0:ne], pv, ALU.mult, [t_cgs[ws_], k.pb[2], k.pb[3]], [t_uu[ws_]])
                        u_ = uu[ws_]
                        y_ = yc[ws_]
                        tu, ty = t_uu[ws_], t_yc[ws_]
                        if ext:
                            if blk == 0:
                                ts(P, "dve", u_[:, 0:64], u_[:, 0:64], edge[:, 0:1], ALU.mult, [tu, t_cw], [tu])
                            if blk == 7:
                                ts(P, "dve", u_[:, 576:640], u_[:, 576:640], edge[:, 1:2], ALU.mult, [tu, t_cw], [tu])
                            ts(P, "dve", y_[:, 0:512], u_[:, 64:576], cw[:, j, 1:2], ALU.mult, [tu, t_cw], [ty])
                            stt(P, "dve", y_[:, 0:512], u_[:, 0:512], cw[:, j, 0:1], y_[:, 0:512], ALU.mult, ALU.add, [tu, t_cw, ty], [ty])
                            stt(P, "dve", y_[:, 0:512], u_[:, 128:640], cw[:, j, 2:3], y_[:, 0:512], ALU.mult, ALU.add, [tu, t_cw, ty], [ty])
                        else:
                            rl = NCX if isctx else 64
                            u3 = u_[:, 0:n].rearrange("p (r c) -> p r c", c=rl)
                            y3 = y_[:, 0:n].rearrange("p (r c) -> p r c", c=rl)
                            ts(P, "dve", y_[:, 0:n], u_[:, 0:n], cw[:, j, 1:2], ALU.mult, [tu, t_cw], [ty])
                            stt(P, "dve", y3[:, :, 1:rl], u3[:, :, 0:rl - 1], cw[:, j, 0:1], y3[:, :, 1:rl], ALU.mult, ALU.add,
                                [tu, t_cw, ty], [ty])
                            stt(P, "dve", y3[:, :, 0:rl - 1], u3[:, :, 1:rl], cw[:, j, 2:3], y3[:, :, 0:rl - 1], ALU.mult, ALU.add,
                                [tu, t_cw, ty], [ty])
                        act(P, sz[ws_][:, 0:n], k.ps[:, bz, 0:n], AF.Silu, [k.pb[bz]], [t_sz[ws_]])
                        tt(P, "dve", t1[ws_][:, 0:n], k.ps[:, bbg, 0:n], y_[:, 0:n], ALU.mult, [k.pb[bbg], ty], [t_t1[ws_]])
                        c0 = NT if isctx else blk * 512
                        tt(P, "pool", g[:, c0:c0 + n], t1[ws_][:, 0:n], sz[ws_][:, 0:n], ALU.mult, [t_t1[ws_], t_sz[ws_]], [tg])
                    dma(P, "pool", G0.ap()[j * 128:(j + 1) * 128, 0:(NTX if do_ctx else NT)], g[:, 0:(NTX if do_ctx else NT)], reads=[tg])
        P.barrier()
        with ExitStack() as st2:
            gb = [P.sb(st2, f"gb{i}", [128, 16, 512], BF16) for i in range(2)]
            t_gb = P.tiles(2, "gb")
            xt = [P.sb(st2, f"xt{i}", [128, 1024], F32) for i in range(2)]
            t_xt = P.tiles(2, "xt")
            ot = [P.sb(st2, f"ot{i}", [128, 1024], F32) for i in range(2)]
            t_ot = P.tiles(2, "ot")
            Ws = ln_work(P, st2)
            load_ln_gate(P, k, I, st2, 0)
            gsrc = G0.ap().rearrange("(j p) t -> p j t", p=128)

            def load_g(b):
                n = NCX if b == 8 else 512
                dma(P, "sp", gb[b % 2][:, :, 0:n], gsrc[:, :, b * 512:b * 512 + n], writes=[t_gb[b % 2]])
            load_g(0)
            it = 0
            nblk0 = 9 if do_ctx else 8
            for b in range(nblk0):
                if b + 1 < nblk0:
                    load_g(b + 1)
                ntile = 2 if b == 8 else 4
                for tl in range(ntile):
                    s = it % 2
                    it += 1
                    psb = 2 * s
                    isctx = b == 8
                    if isctx:
                        xsrc = I["ctxtok"].ap()[tl * 128:(tl + 1) * 128, :]
                        dst = CTX1.ap()[tl * 128:(tl + 1) * 128, :]
                    else:
                        r0 = b * 512 + tl * 128
                        xsrc = I["xtok" + sfx].ap()[r0:r0 + 128, :]
                        dst = X1.ap()[r0:r0 + 128, :]
                    dma(P, "sp", xt[s][:], xsrc, writes=[t_xt[s]])
                    for h in range(2):
                        for j in range(16):
                            mm(P, k.ps[:, psb + h, :], gb[b % 2][:, j, tl * 128:(tl + 1) * 128], wout[:, j, h * 512:(h + 1) * 512],
                               j == 0, j == 15, [t_gb[b % 2], t_wout], [k.pb[psb + h]])
                    ln_residual(P, k, Ws[s], psb, xt[s][:], t_xt[s], 1 if isctx else 0, ot[s][:], t_ot[s], affine=ln_affine)
                    dma(P, "pool", dst, ot[s][:], reads=[t_ot[s]])
    P.barrier()


def bc_last(ap, shape):
    return ap.unsqueeze(len(shape) - 1).to_broadcast(shape)


def s5_tables(P, k, I, st):
    S = K()
    S.apr = P.sb(st, "apr", [128, 17, 128], F32)
    S.api = P.sb(st, "api", [128, 17, 128], F32)
    S.bbr = P.sb(st, "bbr", [128, 128, 16], F32)
    S.bbi = P.sb(st, "bbi", [128, 128, 16], F32)
    S.a4r = P.sb(st, "a4r", [128, 128], F32)
    S.a4i = P.sb(st, "a4i", [128, 128], F32)
    S.rpr = P.sb(st, "rpr", [128, 8, 128], F32)
    S.rpi = P.sb(st, "rpi", [128, 8, 128], F32)
    S.mgi = P.sb(st, "mgi", [128, 2], F32)
    S.sel = P.sb(st, "sel", [128, 4], F32)
    S.t_tab = P.tile("s5tab")
    tb = S.t_tab
    dma(P, "sp", S.mgi[:], I["mgi"].ap(), writes=[tb])
    dma(P, "sp", S.sel[:], I["sel"].ap(), writes=[tb])
    with ExitStack() as s2:
        lr = P.sb(s2, "lr", [128, 128], F32)
        li = P.sb(s2, "li", [128, 128], F32)
        ls = P.sb(s2, "ls", [128, 128], F32)
        bre = P.sb(s2, "bre", [128, 128, 16], F32)
        bim = P.sb(s2, "bim", [128, 128, 16], F32)
        w = [P.sb(s2, f"zw{i}", [128, 128], F32) for i in range(8)]
        big1 = P.sb(s2, "zb1", [128, 128, 16], F32)
        big2 = P.sb(s2, "zb2", [128, 128, 16], F32)
        tz = P.tile("zoh")
        dma(P, "sp", lr[:], I["lamre_A"].ap(), writes=[tz])
        dma(P, "sp", li[:], I["lamim_A"].ap(), writes=[tz])
        dma(P, "sp", ls[:], I["logstep_A"].ap(), writes=[tz])
        dma(P, "sp", bre[:], I["bre_A"].ap(), writes=[tz])
        dma(P, "sp", bim[:], I["bim_A"].ap(), writes=[tz])
        R, Wt = [tz, tb], [tz, tb]
        dtt, mag, th, cc, ss, t1, t2, t3 = w
        act(P, dtt[:], ls[:], AF.Exp, R, Wt)
        tt(P, "dve", mag[:], lr[:], dtt[:], ALU.mult, R, Wt)
        act(P, mag[:], mag[:], AF.Exp, R, Wt)
        tt(P, "dve", th[:], li[:], dtt[:], ALU.mult, R, Wt)
        act(P, ss[:], th[:], AF.Sin, R, Wt, scale=1.0 / 64.0)
        ts(P, "dve", t1[:], th[:], 1.0 / 64.0, ALU.mult, R, Wt, s2=PI / 2, op1=ALU.add)
        act(P, cc[:], t1[:], AF.Sin, R, Wt)
        for _ in range(6):
            tt(P, "dve", t1[:], cc[:], cc[:], ALU.mult, R, Wt)
            tt(P, "dve", t2[:], ss[:], ss[:], ALU.mult, R, Wt)
            tt(P, "dve", t3[:], cc[:], ss[:], ALU.mult, R, Wt)
            tt(P, "dve", cc[:], t1[:], t2[:], ALU.subtract, R, Wt)
            ts(P, "dve", ss[:], t3[:], 2.0, ALU.mult, R, Wt)
        ar, ai = S.apr[:, 1, :], S.api[:, 1, :]
        tt(P, "dve", ar, mag[:], cc[:], ALU.mult, R, Wt)
        tt(P, "dve", ai, mag[:], ss[:], ALU.mult, R, Wt)
        mset(P, "dve", S.apr[:, 0, :], 1.0, Wt)
        mset(P, "dve", S.api[:, 0, :], 0.0, Wt)
        tt(P, "dve", t1[:], lr[:], lr[:], ALU.mult, R, Wt)
        tt(P, "dve", t2[:], li[:], li[:], ALU.mult, R, Wt)
        tt(P, "dve", t1[:], t1[:], t2[:], ALU.add, R, Wt)
        P.op("dve", lambda Eh: Eh.reciprocal(out=t1[:], in_=t1[:]), reads=R, writes=Wt)
        ts(P, "dve", t2[:], ar, -1.0, ALU.add, R, Wt)
        tt(P, "dve", t3[:], t2[:], lr[:], ALU.mult, R, Wt)
        tt(P, "dve", cc[:], ai, li[:], ALU.mult, R, Wt)
        tt(P, "dve", t3[:], t3[:], cc[:], ALU.add, R, Wt)
        tt(P, "dve", t3[:], t3[:], t1[:], ALU.mult, R, Wt)
        tt(P, "dve", cc[:], ai, lr[:], ALU.mult, R, Wt)
        tt(P, "dve", ss[:], t2[:], li[:], ALU.mult, R, Wt)
        tt(P, "dve", cc[:], cc[:], ss[:], ALU.subtract, R, Wt)
        tt(P, "dve", cc[:], cc[:], t1[:], ALU.mult, R, Wt)
        frb = bc_last(t3[:], [128, 128, 16])
        fib = bc_last(cc[:], [128, 128, 16])
        tt(P, "dve", big1[:], bre[:], frb, ALU.mult, R, Wt)
        tt(P, "dve", big2[:], bim[:], fib, ALU.mult, R, Wt)
        tt(P, "dve", S.bbr[:], big1[:], big2[:], ALU.subtract, R, Wt)
        tt(P, "dve", big1[:], bim[:], frb, ALU.mult, R, Wt)
        tt(P, "dve", big2[:], bre[:], fib, ALU.mult, R, Wt)
        tt(P, "dve", S.bbi[:], big1[:], big2[:], ALU.add, R, Wt)
        for kk in range(1, 16):
            pr, pi_ = S.apr[:, kk, :], S.api[:, kk, :]
            tt(P, "dve", t1[:], pr, ar, ALU.mult, R, Wt)
            tt(P, "dve", t2[:], pi_, ai, ALU.mult, R, Wt)
            tt(P, "dve", S.apr[:, kk + 1, :], t1[:], t2[:], ALU.subtract, R, Wt)
            tt(P, "dve", t1[:], pr, ai, ALU.mult, R, Wt)
            tt(P, "dve", t2[:], pi_, ar, ALU.mult, R, Wt)
            tt(P, "dve", S.api[:, kk + 1, :], t1[:], t2[:], ALU.add, R, Wt)
        cp(P, "dve", S.a4r[:], S.apr[:, 16, :], R, Wt)
        cp(P, "dve", S.a4i[:], S.api[:, 16, :], R, Wt)
        for l_ in range(8):
            cp(P, "dve", S.rpr[:, l_, :], S.a4r[:], R, Wt)
            cp(P, "dve", S.rpi[:, l_, :], S.a4i[:], R, Wt)
            tt(P, "dve", t1[:], S.a4r[:], S.a4r[:], ALU.mult, R, Wt)
            tt(P, "dve", t2[:], S.a4i[:], S.a4i[:], ALU.mult, R, Wt)
            tt(P, "dve", t3[:], S.a4r[:], S.a4i[:], ALU.mult, R, Wt)
            tt(P, "dve", S.a4r[:], t1[:], t2[:], ALU.subtract, R, Wt)
            ts(P, "dve", S.a4i[:], t3[:], 2.0, ALU.mult, R, Wt)
    P.barrier()
    return S


def l1_inproj(P, k, I, X1src, CTX1src, UD, HXD, do_ctx=True, store_hx=True, fold=False):
    with ExitStack() as st:
        wu = P.sb(st, "wu", [128, 8, E2], BF16)
        t_wu = P.tile("wu")
        wsrc = I["ssm_w_in"].ap().rearrange("(dt p) c -> p dt c", p=128)
        for q in range(4):
            dma(P, "pool", wu[:, :, q * 512:(q + 1) * 512], wsrc[:, :, q * 512:(q + 1) * 512], writes=[t_wu])
        xt = [P.sb(st, f"x1t{i}", [128, 1024], F32) for i in range(2)]
        t_xt = P.tiles(2, "x1t")
        hxb = [P.sb(st, f"hxb{i}", [128, 8, 512], BF16) for i in range(2)]
        t_hxb = P.tiles(2, "hxb")
        ub = [P.sb(st, f"ub{i}", [128, 16, 512], BF16) for i in range(2)]
        t_ub = P.tiles(2, "ub")
        usrc = UD.ap().rearrange("(j p) t -> p j t", p=128)
        nblk = 9 if do_ctx else 8
        cnt = [0]

        def prep(b):
            isctx = b == 8
            n = NCX if isctx else 512
            hb, thb = hxb[b % 2], t_hxb[b % 2]
            for tl in range(n // 128):
                s = cnt[0] % 2
                cnt[0] += 1
                src = CTX1src.ap()[tl * 128:(tl + 1) * 128, :] if isctx else X1src.ap()[b * 512 + tl * 128: b * 512 + (tl + 1) * 128, :]
                dma(P, "sp", xt[s][:], src, writes=[t_xt[s]])
                for h in range(2):
                    bank = 2 * s + h
                    for q in range(4):
                        dt = h * 4 + q
                        P.op("pe", lambda Eh, o=k.ps[:, bank, q * 128:(q + 1) * 128], i_=xt[s][:, dt * 128:(dt + 1) * 128]:
                             Eh.transpose(o, i_, k.identf[:]), reads=[t_xt[s], k.t_ident], writes=[k.pb[bank]])
                    for q in range(4):
                        dt = h * 4 + q
                        if fold:
                            act(P, hb[:, dt, tl * 128:(tl + 1) * 128], k.ps[:, bank, q * 128:(q + 1) * 128], AF.Identity,
                                [k.pb[bank], k.t_mcol2], [thb], scale=k.mcol2[:, 8 + dt:9 + dt], bias=k.mcol2[:, dt:dt + 1])
                        else:
                            act(P, hb[:, dt, tl * 128:(tl + 1) * 128], k.ps[:, bank, q * 128:(q + 1) * 128], AF.Identity,
                                [k.pb[bank], k.t_mcol], [thb], scale=k.mcol[:, 8 + dt, (1 if isctx else 0):(2 if isctx else 1)],
                                bias=k.mcol[:, dt, (1 if isctx else 0):(2 if isctx else 1)])
            c0 = NT if isctx else b * 512
            if store_hx:
                dma(P, "act", HXD.ap()[:, :, c0:c0 + n], hb[:, :, 0:n], reads=[thb])

        prep(0)
        for b in range(nblk):
            if b + 1 < nblk:
                prep(b + 1)
            isctx = b == 8
            n = NCX if isctx else 512
            hb, thb = hxb[b % 2], t_hxb[b % 2]
            c0 = NT if isctx else b * 512
            u_, tu = ub[b % 2], t_ub[b % 2]
            for j in range(16):
                bank = 4 + (j % 4)
                for dt in range(8):
                    mm(P, k.ps[:, bank, 0:n], wu[:, dt, j * 128:(j + 1) * 128], hb[:, dt, 0:n], dt == 0, dt == 7,
                       [t_wu, thb], [k.pb[bank]])
                cp(P, "dve", u_[:, j, 0:n], k.ps[:, bank, 0:n], [k.pb[bank]], [tu])
            dma(P, "pool", usrc[:, :, c0:c0 + n], u_[:, :, 0:n], reads=[tu])
    P.barrier()


def build_cmp(P, S, W, src_r, src_i, k0, nk, rM0, sign_neg_part1):
    shp = [128, nk, 4, 16]
    A_r = bc_last(S.apr[:, k0:k0 + nk, rM0:rM0 + 4], shp)
    A_i = bc_last(S.api[:, k0:k0 + nk, rM0:rM0 + 4], shp)
    B_r = src_r[:, rM0:rM0 + 4, :].unsqueeze(1).to_broadcast(shp)
    B_i = src_i[:, rM0:rM0 + 4, :].unsqueeze(1).to_broadcast(shp)
    cmp_, t_cmp = W["cmp"], W["t_cmp"]
    ta, tb_ = W["tmp1"], W["tmp2"]
    R = [S.t_tab, W["t_tmp"]]
    Wr = [W["t_tmp"]]
    tt(P, "dve", ta[:, 0:nk], B_r, A_r, ALU.mult, R, Wr)
    tt(P, "pool", tb_[:, 0:nk], B_i, A_i, ALU.mult, [S.t_tab, W["t_tmp2"]], [W["t_tmp2"]])
    tt(P, "dve", cmp_[:, 0:nk, 0, :, :], ta[:, 0:nk], tb_[:, 0:nk], ALU.subtract, [W["t_tmp"], W["t_tmp2"]], [t_cmp])
    tt(P, "dve", ta[:, 0:nk], B_i, A_r, ALU.mult, R + [W["t_tmp2"]], Wr)
    tt(P, "pool", tb_[:, 0:nk], B_r, A_i, ALU.mult, [S.t_tab, W["t_tmp2"], W["t_tmp"]], [W["t_tmp2"]])
    if sign_neg_part1:
        stt(P, "dve", cmp_[:, 0:nk, 1, :, :], ta[:, 0:nk], -1.0, tb_[:, 0:nk], ALU.mult, ALU.subtract,
            [W["t_tmp"], W["t_tmp2"]], [t_cmp])
    else:
        tt(P, "dve", cmp_[:, 0:nk, 1, :, :], ta[:, 0:nk], tb_[:, 0:nk], ALU.add, [W["t_tmp"], W["t_tmp2"]], [t_cmp])


def pad_cmp(P, S, W, dst, t_dst, nk):
    cmp_, t_cmp = W["cmp"], W["t_cmp"]
    i = 0
    for gi in range(2):
        for m in range(4):
            c0 = (2 * m + gi) * 16
            if gi == 0:
                ts(P, "dve", dst[:, 0:nk, :, m, c0:c0 + 16], cmp_[:, 0:nk, :, m, :], S.mgi[:, gi:gi + 1], ALU.mult,
                   [t_cmp, S.t_tab], [t_dst])
            else:
                act(P, dst[:, 0:nk, :, m, c0:c0 + 16], cmp_[:, 0:nk, :, m, :], AF.Copy, [t_cmp, S.t_tab], [t_dst],
                    scale=S.mgi[:, gi:gi + 1])
            i += 1


def table_work(P, st):
    W = {}
    W["cmp"] = P.sb(st, "cmp", [128, 17, 2, 4, 16], F32)
    W["tmp1"] = P.sb(st, "tmp1", [128, 17, 4, 16], F32)
    W["tmp2"] = P.sb(st, "tmp2", [128, 17, 4, 16], F32)
    W["t_cmp"], W["t_tmp"], W["t_tmp2"] = P.tiles(3, "tw")
    return W


def l1_summaries(P, k, I, S, UD, SD, do_ctx=True):
    with ExitStack() as st:
        W = table_work(P, st)
        bk = P.sb(st, "bkpad", [128, 16, 2, 4, 128], BF16)
        t_bk = P.tile("bk")
        mset(P, "pool", bk[:].rearrange("p a b c d -> p (a b c d)"), 0.0, [t_bk])
        wrt = [P.sb(st, f"wrt{i}", [128, 16, 2, 128], BF16) for i in range(2)]
        t_wrt = P.tiles(2, "wrt")
        uj = [P.sb(st, f"uj{i}", [128, NTX], BF16) for i in range(2)]
        t_uj = P.tiles(2, "uj")
        ssb = [P.sb(st, f"ssb{i}", [128, 4, 2, 256], F32) for i in range(2)]
        t_ssb = P.tiles(2, "ssb")
        ssc = [P.sb(st, f"ssc{i}", [128, 4, 2, 16], F32) for i in range(2)]
        t_ssc = P.tiles(2, "ssc")

        nld = NTX if do_ctx else NT

        def load_u(j):
            dma(P, "sp", uj[j % 2][:, 0:nld], UD.ap()[j * 128:(j + 1) * 128, 0:nld], writes=[t_uj[j % 2]])
        load_u(0)
        it = 0
        for j in range(16):
            if j + 1 < 16:
                load_u(j + 1)
            u_, tu = uj[j % 2], t_uj[j % 2]
            u3 = u_[:, 0:NT].rearrange("p (c t) -> p c t", t=TCH)
            u3c = u_[:, NT:NTX].rearrange("p (c t) -> p c t", t=TCH)
            for r in range(2):
                s = it % 2
                it += 1
                rM0 = r * 64 + 4 * j
                build_cmp(P, S, W, S.bbr, S.bbi, 0, 16, rM0, False)
                pad_cmp(P, S, W, bk, t_bk, 16)
                wr, twr = wrt[s], t_wrt[s]
                wflat = wr[:].rearrange("p a b c -> p (a b c)")
                for rd in range(8):
                    bank = 4 + (rd % 2)
                    for q in range(4):
                        kp = rd * 4 + q
                        kk, part = divmod(kp, 2)
                        for m in range(4):
                            mm(P, k.ps[:, bank, q * 128:(q + 1) * 128], bk[:, kk, part, m, :], k.identb[:], m == 0, m == 3,
                               [t_bk, k.t_ident], [k.pb[bank]])
                    cp(P, "act", wflat[:, rd * 512:(rd + 1) * 512], k.ps[:, bank, :], [k.pb[bank]], [twr])
                for (uv, nch, dst, tdst, dcol) in ((u3, NCHL, ssb[s], t_ssb[s], 0), (u3c, NCHX, ssc[s], t_ssc[s], NCHL))[0:(2 if do_ctx else 1)]:
                    for part in range(2):
                        for kk in range(TCH):
                            sidx = (TCH - 1 - kk) if r == 0 else kk
                            for m in range(4):
                                mm(P, k.ps[:, m, part * 256: part * 256 + nch], wr[32 * m:32 * m + 32, kk, part, :],
                                   uv[32 * m:32 * m + 32, :, sidx], kk == 0, kk == TCH - 1, [twr, tu], [k.pb[m]], tp=(32 * m, 0))
                    for m in range(4):
                        cp(P, "dve" if m % 2 == 0 else "act", dst[:, m, :, :],
                           k.ps[:, m, :].rearrange("p (a b) -> p a b", a=2)[:, :, 0:nch], [k.pb[m]], [tdst])
                    dma(P, "sp", SD[r].ap()[:, 4 * j:4 * j + 4, :, dcol:dcol + nch], dst[:], reads=[tdst])
    P.barrier()


def l1_carry(P, k, S, SD, HD, init, final, do_ctx, do_lat, store):
    CBK = 32
    eng = "dve"
    with ExitStack() as st:
        sblk = [[P.sb(st, f"sblk{r}{i}", [128, 64, 2, CBK], F32) for i in range(2)] for r in range(2)]
        t_sblk = [P.tiles(2, "sblk") for r in range(2)]
        hseq = [P.sb(st, f"hseq{r}", [128, 64, 2, CBK + 1], F32) for r in range(2)]
        t_hseq = P.tiles(2, "hseq")
        hbf = [P.sb(st, f"hbf{r}", [128, 64, 2, CBK], BF16) for r in range(2)]
        t_hbf = P.tiles(2, "hbf")
        tmp = [[P.sb(st, f"ctmp{r}{i}", [128, 64], F32) for i in range(4)] for r in range(2)]
        t_tmp = [P.tiles(4, "ctmp") for r in range(2)]
        blocks = [[], []]
        for r in range(2):
            if do_ctx:
                blocks[r].append((NCHL, NCHX))
            if do_lat:
                lat = [(c0, CBK) for c0 in range(0, NCHL, CBK)]
                blocks[r] += lat if r == 0 else lat[::-1]
        Rr = [S.apr[:, 16, r * 64:(r + 1) * 64] for r in range(2)]
        Ri = [S.api[:, 16, r * 64:(r + 1) * 64] for r in range(2)]

        def load_blk(r, bi):
            c0, nb = blocks[r][bi]
            dma(P, "sp", sblk[r][bi % 2][:, :, :, 0:nb], SD[r].ap()[:, :, :, c0:c0 + nb], writes=[t_sblk[r][bi % 2]])
        nblk = len(blocks[0])
        for r in range(2):
            load_blk(r, 0)
        for bi in range(nblk):
            nb = blocks[0][bi][1]
            for r in range(2):
                if bi + 1 < nblk:
                    load_blk(r, bi + 1)
                hs, ths = hseq[r], t_hseq[r]
                e_in = 0 if r == 0 else nb
                if bi == 0:
                    if init[r] is None:
                        mset(P, eng, hs[:, :, :, e_in], 0.0, [ths])
                    else:
                        cp(P, eng, hs[:, :, 0, e_in], init[r][0], [init[r][2]], [ths])
                        cp(P, eng, hs[:, :, 1, e_in], init[r][1], [init[r][2]], [ths])
                else:
                    pnb = blocks[r][bi - 1][1]
                    e_prev = pnb if r == 0 else 0
                    if e_prev != e_in:
                        cp(P, eng, hs[:, :, :, e_in], hs[:, :, :, e_prev], [ths], [ths])
            for i in range(nb):
                cc = [i, nb - 1 - i]
                ei = [cc[0], cc[1] + 1]
                eo = [cc[0] + 1, cc[1]]
                for r in range(2):
                    hs, ths, tm, ttm = hseq[r], t_hseq[r], tmp[r], t_tmp[r]
                    hr, hi = hs[:, :, 0, ei[r]], hs[:, :, 1, ei[r]]
                    tt(P, eng, tm[0][:], Rr[r], hr, ALU.mult, [S.t_tab, ths], [ttm[0]])
                    tt(P, eng, tm[1][:], Ri[r], hi, ALU.mult, [S.t_tab, ths], [ttm[1]])
                    tt(P, eng, tm[2][:], Ri[r], hr, ALU.mult, [S.t_tab, ths], [ttm[2]])
                    tt(P, eng, tm[3][:], Rr[r], hi, ALU.mult, [S.t_tab, ths], [ttm[3]])
                for r in range(2):
                    tm, ttm = tmp[r], t_tmp[r]
                    tt(P, eng, tm[0][:], tm[0][:], tm[1][:], ALU.subtract, [ttm[0], ttm[1]], [ttm[0]])
                    tt(P, eng, tm[2][:], tm[2][:], tm[3][:], ALU.add, [ttm[2], ttm[3]], [ttm[2]])
                for r in range(2):
                    hs, ths, tm, ttm = hseq[r], t_hseq[r], tmp[r], t_tmp[r]
                    sb_, tsb = sblk[r][bi % 2], t_sblk[r][bi % 2]
                    tt(P, eng, hs[:, :, 0, eo[r]], tm[0][:], sb_[:, :, 0, cc[r]], ALU.add, [ttm[0], tsb], [ths])
                    tt(P, eng, hs[:, :, 1, eo[r]], tm[2][:], sb_[:, :, 1, cc[r]], ALU.add, [ttm[2], tsb], [ths])
            for r in range(2):
                hs, ths = hseq[r], t_hseq[r]
                c0 = blocks[r][bi][0]
                if store and c0 < NCHL:
                    lo = 0 if r == 0 else 1
                    cp(P, "act", hbf[r][:, :, :, 0:nb], hs[:, :, :, lo:lo + nb], [ths], [t_hbf[r]])
                    dma(P, "act", HD[r].ap()[:, :, :, c0:c0 + nb], hbf[r][:, :, :, 0:nb], reads=[t_hbf[r]])
                e_out = nb if r == 0 else 0
                if final[r] is not None and bi == nblk - 1:
                    cp(P, eng, final[r][0], hs[:, :, 0, e_out], [ths], [final[r][2]])
                    cp(P, eng, final[r][1], hs[:, :, 1, e_out], [ths], [final[r][2]])
    P.barrier()


def l1_summaries_multi(P, k, I, S, UDs, SDs):
    with ExitStack() as st:
        W = table_work(P, st)
        bk = P.sb(st, "bkpad", [128, 16, 2, 4, 128], BF16)
        t_bk = P.tile("bk")
        mset(P, "pool", bk[:].rearrange("p a b c d -> p (a b c d)"), 0.0, [t_bk])
        wrt = [[P.sb(st, f"wrt{r}{i}", [128, 16, 2, 128], BF16) for i in range(2)] for r in range(2)]
        t_wrt = [P.tiles(2, "wrt") for r in range(2)]
        uj = [P.sb(st, f"uj{i}", [128, NTX], BF16) for i in range(2)]
        t_uj = P.tiles(2, "uj")
        ssb = [P.sb(st, f"ssb{i}", [128, 4, 2, 256], F32) for i in range(2)]
        t_ssb = P.tiles(2, "ssb")
        ssc = [P.sb(st, f"ssc{i}", [128, 4, 2, 16], F32) for i in range(2)]
        t_ssc = P.tiles(2, "ssc")
        seq = [(j, sl) for j in range(16) for sl in range(4)]

        def load_u(i):
            j, sl = seq[i]
            nld = NTX if sl == 0 else NT
            dma(P, "sp", uj[i % 2][:, 0:nld], UDs[sl].ap()[j * 128:(j + 1) * 128, 0:nld], writes=[t_uj[i % 2]])
        load_u(0)
        it = 0
        for j in range(16):
            for r in range(2):
                rM0 = r * 64 + 4 * j
                build_cmp(P, S, W, S.bbr, S.bbi, 0, 16, rM0, False)
                pad_cmp(P, S, W, bk, t_bk, 16)
                wr, twr = wrt[r][j % 2], t_wrt[r][j % 2]
                wflat = wr[:].rearrange("p a b c -> p (a b c)")
                for rd in range(8):
                    bank = 4 + (rd % 2)
                    for q in range(4):
                        kp = rd * 4 + q
                        kk, part = divmod(kp, 2)
                        for m in range(4):
                            mm(P, k.ps[:, bank, q * 128:(q + 1) * 128], bk[:, kk, part, m, :], k.identb[:], m == 0, m == 3,
                               [t_bk, k.t_ident], [k.pb[bank]])
                    cp(P, "act", wflat[:, rd * 512:(rd + 1) * 512], k.ps[:, bank, :], [k.pb[bank]], [twr])
            for sl in range(4):
                i = j * 4 + sl
                if i + 1 < len(seq):
                    load_u(i + 1)
                u_, tu = uj[i % 2], t_uj[i % 2]
                u3 = u_[:, 0:NT].rearrange("p (c t) -> p c t", t=TCH)
                u3c = u_[:, NT:NTX].rearrange("p (c t) -> p c t", t=TCH)
                for r in range(2):
                    s_ = it % 2
                    it += 1
                    wr, twr = wrt[r][j % 2], t_wrt[r][j % 2]
                    jobs = ((u3, NCHL, ssb[s_], t_ssb[s_], 0), (u3c, NCHX, ssc[s_], t_ssc[s_], NCHL))
                    for (uv, nch, dst, tdst, dcol) in jobs[0:(2 if sl == 0 else 1)]:
                        for part in range(2):
                            for kk in range(TCH):
                                sidx = (TCH - 1 - kk) if r == 0 else kk
                                for m in range(4):
                                    mm(P, k.ps[:, m, part * 256: part * 256 + nch], wr[32 * m:32 * m + 32, kk, part, :],
                                       uv[32 * m:32 * m + 32, :, sidx], kk == 0, kk == TCH - 1, [twr, tu], [k.pb[m]], tp=(32 * m, 0))
                        for m in range(4):
                            cp(P, "dve" if m % 2 == 0 else "act", dst[:, m, :, :],
                               k.ps[:, m, :].rearrange("p (a b) -> p a b", a=2)[:, :, 0:nch], [k.pb[m]], [tdst])
                        dma(P, "sp", SDs[sl][r].ap()[:, 4 * j:4 * j + 4, :, dcol:dcol + nch], dst[:], reads=[tdst])
    P.barrier()


def l1_carry_foreign(P, k, S, SDs, zs, t_zs):
    CBK = 16
    eng = "dve"
    NS = 3
    with ExitStack() as st:
        sblk = [P.sb(st, f"fsblk{r}", [128, NS, 64, 2, CBK], F32) for r in range(2)]
        t_sblk = P.tiles(2, "fsblk")
        hseq = [P.sb(st, f"fhseq{r}", [128, NS, 64, 2, 2], F32) for r in range(2)]
        t_hseq = P.tiles(2, "fhseq")
        tmp = [[P.sb(st, f"fctmp{r}{i}", [128, NS, 64], F32) for i in range(4)] for r in range(2)]
        t_tmp = [P.tiles(4, "fctmp") for r in range(2)]
        shp = [128, NS, 64]
        Rr = [S.apr[:, 16, r * 64:(r + 1) * 64].unsqueeze(1).to_broadcast(shp) for r in range(2)]
        Ri = [S.api[:, 16, r * 64:(r + 1) * 64].unsqueeze(1).to_broadcast(shp) for r in range(2)]
        lat = [c0 for c0 in range(0, NCHL, CBK)]
        blocks = [lat, lat[::-1]]
        for r in range(2):
            mset(P, eng, hseq[r][:, :, :, :, 0], 0.0, [t_hseq[r]])
        cur = 0
        for bi in range(len(lat)):
            for r in range(2):
                c0 = blocks[r][bi]
                for sl in range(NS):
                    dma(P, "sp", sblk[r][:, sl], SDs[sl + 1][r].ap()[:, :, :, c0:c0 + CBK], writes=[t_sblk[r]])
            for i in range(CBK):
                cc = [i, CBK - 1 - i]
                nxt = 1 - cur
                for r in range(2):
                    hs, ths, tm, ttm = hseq[r], t_hseq[r], tmp[r], t_tmp[r]
                    hr, hi = hs[:, :, :, 0, cur], hs[:, :, :, 1, cur]
                    tt(P, eng, tm[0][:], Rr[r], hr, ALU.mult, [S.t_tab, ths], [ttm[0]])
                    tt(P, eng, tm[1][:], Ri[r], hi, ALU.mult, [S.t_tab, ths], [ttm[1]])
                    tt(P, eng, tm[2][:], Ri[r], hr, ALU.mult, [S.t_tab, ths], [ttm[2]])
                    tt(P, eng, tm[3][:], Rr[r], hi, ALU.mult, [S.t_tab, ths], [ttm[3]])
                for r in range(2):
                    tm, ttm = tmp[r], t_tmp[r]
                    tt(P, eng, tm[0][:], tm[0][:], tm[1][:], ALU.subtract, [ttm[0], ttm[1]], [ttm[0]])
                    tt(P, eng, tm[2][:], tm[2][:], tm[3][:], ALU.add, [ttm[2], ttm[3]], [ttm[2]])
                for r in range(2):
                    hs, ths, tm, ttm = hseq[r], t_hseq[r], tmp[r], t_tmp[r]
                    tt(P, eng, hs[:, :, :, 0, nxt], tm[0][:], sblk[r][:, :, :, 0, cc[r]], ALU.add, [ttm[0], t_sblk[r]], [ths])
                    tt(P, eng, hs[:, :, :, 1, nxt], tm[2][:], sblk[r][:, :, :, 1, cc[r]], ALU.add, [ttm[2], t_sblk[r]], [ths])
                cur = nxt
        for r in range(2):
            for part in range(2):
                cp(P, eng, zs[:, 1:4, r, part, :], hseq[r][:, :, :, part, cur], [t_hseq[r]], [t_zs])
    P.barrier()


def l1_zreduce_foreign(P, k, S, SDs, zs, t_zs):
    NS = 3
    with ExitStack() as st:
        XA = [P.sb(st, f"zxa{r}", [128, NS, 4, 2, 256], F32) for r in range(2)]
        XB = [P.sb(st, f"zxb{r}", [128, NS, 4, 2, 128], F32) for r in range(2)]
        tm = [[P.sb(st, f"zt{r}{i}", [128, NS, 4, 128], F32) for i in range(3)] for r in range(2)]
        t_xa, t_xb = P.tiles(2, "zxa"), P.tiles(2, "zxb")
        t_tm = [P.tiles(3, "zt") for r in range(2)]
        eng = "dve"
        for j in range(16):
            for r in range(2):
                for sl in range(NS):
                    dma(P, "sp", XA[r][:, sl], SDs[sl + 1][r].ap()[:, 4 * j:4 * j + 4, :, 0:NCHL], writes=[t_xa[r]])
            for l_ in range(8):
                n = 128 >> l_
                for r in range(2):
                    src, tsrc = (XA[r], t_xa[r]) if l_ % 2 == 0 else (XB[r], t_xb[r])
                    dst, tdst = (XB[r], t_xb[r]) if l_ % 2 == 0 else (XA[r], t_xa[r])
                    shp = [128, NS, 4, n]
                    c0 = r * 64 + 4 * j
                    Rr = S.rpr[:, l_, c0:c0 + 4].unsqueeze(1).unsqueeze(3).to_broadcast(shp)
                    Ri = S.rpi[:, l_, c0:c0 + 4].unsqueeze(1).unsqueeze(3).to_broadcast(shp)
                    ia, ib = (0, 1) if r == 0 else (1, 0)
                    Ar, Ai = src[:, :, :, 0, ia:2 * n:2], src[:, :, :, 1, ia:2 * n:2]
                    Br, Bi = src[:, :, :, 0, ib:2 * n:2], src[:, :, :, 1, ib:2 * n:2]
                    t1, t2, t3 = (tm[r][i][:, :, :, 0:n] for i in range(3))
                    tt(P, eng, t1, Ar, Rr, ALU.mult, [tsrc, S.t_tab], [t_tm[r][0]])
                    tt(P, eng, t2, Ai, Ri, ALU.mult, [tsrc, S.t_tab], [t_tm[r][1]])
                    tt(P, eng, t3, Ai, Rr, ALU.mult, [tsrc, S.t_tab], [t_tm[r][2]])
                for r in range(2):
                    src, tsrc = (XA[r], t_xa[r]) if l_ % 2 == 0 else (XB[r], t_xb[r])
                    shp = [128, NS, 4, n]
                    c0 = r * 64 + 4 * j
                    Ri = S.rpi[:, l_, c0:c0 + 4].unsqueeze(1).unsqueeze(3).to_broadcast(shp)
                    ia = 0 if r == 0 else 1
                    Ar = src[:, :, :, 0, ia:2 * n:2]
                    t1, t2, t3 = (tm[r][i][:, :, :, 0:n] for i in range(3))
                    tt(P, eng, t1, t1, t2, ALU.subtract, [t_tm[r][0], t_tm[r][1]], [t_tm[r][0]])
                    tt(P, eng, t2, Ar, Ri, ALU.mult, [tsrc, S.t_tab, t_tm[r][1]], [t_tm[r][1]])
                for r in range(2):
                    src, tsrc = (XA[r], t_xa[r]) if l_ % 2 == 0 else (XB[r], t_xb[r])
                    dst, tdst = (XB[r], t_xb[r]) if l_ % 2 == 0 else (XA[r], t_xa[r])
                    ib = 1 if r == 0 else 0
                    Br, Bi = src[:, :, :, 0, ib:2 * n:2], src[:, :, :, 1, ib:2 * n:2]
                    t1, t2, t3 = (tm[r][i][:, :, :, 0:n] for i in range(3))
                    tt(P, eng, t3, t3, t2, ALU.add, [t_tm[r][2], t_tm[r][1]], [t_tm[r][2]])
                    tt(P, eng, dst[:, :, :, 0, 0:n], t1, Br, ALU.add, [t_tm[r][0], tsrc], [tdst])
                    tt(P, eng, dst[:, :, :, 1, 0:n], t3, Bi, ALU.add, [t_tm[r][2], tsrc], [tdst])
            for r in range(2):
                for part in range(2):
                    cp(P, "act", zs[:, 1:4, r, part, 4 * j:4 * j + 4], XA[r][:, :, :, part, 0], [t_xa[r]], [t_zs])
    P.barrier()


def l1_summaries_zr(P, k, I, S, UDs, SD, zs, t_zs):
    NS = 3
    with ExitStack() as st:
        W = table_work(P, st)
        bk = P.sb(st, "bkpad", [128, 16, 2, 4, 128], BF16)
        t_bk = P.tile("bk")
        mset(P, "pool", bk[:].rearrange("p a b c d -> p (a b c d)"), 0.0, [t_bk])
        wrt = [P.sb(st, f"wrt{r}", [128, 16, 2, 128], BF16) for r in range(2)]
        t_wrt = P.tiles(2, "wrt")
        uj = [P.sb(st, f"uj{i}", [128, NTX], BF16) for i in range(2)]
        t_uj = P.tiles(2, "uj")
        ssb = P.sb(st, "ssb", [128, 4, 2, 256], F32)
        t_ssb = P.tile("ssb")
        ssc = P.sb(st, "ssc", [128, 4, 2, 16], F32)
        t_ssc = P.tile("ssc")
        XB = [P.sb(st, f"zxb{r}", [128, NS, 4, 2, 128], F32) for r in range(2)]
        XC = [P.sb(st, f"zxc{r}", [128, NS, 4, 2, 64], F32) for r in range(2)]
        t_xb, t_xc = P.tiles(2, "zxb"), P.tiles(2, "zxc")
        tm0 = [[P.sb(st, f"zl0{a}{i}", [128, 4, 128], F32) for i in range(3)] for a in range(2)]
        t_tm0 = [P.tiles(3, "zl0") for a in range(2)]
        _tm1 = [P.sb(st, f"zl1{i}", [128, NS, 4, 64], F32) for i in range(3)]
        _t_tm1 = P.tiles(3, "zl1")
        tm1 = [_tm1, _tm1]
        t_tm1 = [_t_tm1, _t_tm1]
        seq = [(j, sl) for j in range(16) for sl in range(4)]

        def load_u(i):
            j, sl = seq[i]
            nld = NTX if sl == 0 else NT
            dma(P, "sp", uj[i % 2][:, 0:nld], UDs[sl].ap()[j * 128:(j + 1) * 128, 0:nld], writes=[t_uj[i % 2]])

        def cmuladd(eng, dst_r, dst_i, Ar, Ai, Br, Bi, Rr, Ri, t, tt_, rd):
            tt(P, eng, t[0], Ar, Rr, ALU.mult, rd + [S.t_tab], [tt_[0]])
            tt(P, eng, t[1], Ai, Ri, ALU.mult, rd + [S.t_tab], [tt_[1]])
            tt(P, eng, t[2], Ai, Rr, ALU.mult, rd + [S.t_tab], [tt_[2]])
            tt(P, eng, t[0], t[0], t[1], ALU.subtract, [tt_[0], tt_[1]], [tt_[0]])
            tt(P, eng, t[1], Ar, Ri, ALU.mult, rd + [S.t_tab, tt_[1]], [tt_[1]])
            tt(P, eng, t[2], t[2], t[1], ALU.add, [tt_[2], tt_[1]], [tt_[2]])
            return t[0], t[2]

        load_u(0)
        it = 0
        for j in range(16):
            for r in range(2):
                rM0 = r * 64 + 4 * j
                build_cmp(P, S, W, S.bbr, S.bbi, 0, 16, rM0, False)
                pad_cmp(P, S, W, bk, t_bk, 16)
                wflat = wrt[r][:].rearrange("p a b c -> p (a b c)")
                for rd_ in range(8):
                    bank = 4 + (rd_ % 2)
                    for q in range(4):
                        kp = rd_ * 4 + q
                        kk, part = divmod(kp, 2)
                        for m in range(4):
                            mm(P, k.ps[:, bank, q * 128:(q + 1) * 128], bk[:, kk, part, m, :], k.identb[:], m == 0, m == 3,
                               [t_bk, k.t_ident], [k.pb[bank]])
                    cp(P, "act", wflat[:, rd_ * 512:(rd_ + 1) * 512], k.ps[:, bank, :], [k.pb[bank]], [t_wrt[r]])
            for sl in range(4):
                i = j * 4 + sl
                if i + 1 < len(seq):
                    load_u(i + 1)
                u_, tu = uj[i % 2], t_uj[i % 2]
                u3 = u_[:, 0:NT].rearrange("p (c t) -> p c t", t=TCH)
                u3c = u_[:, NT:NTX].rearrange("p (c t) -> p c t", t=TCH)
                for r in range(2):
                    b0 = 4 * (it % 2)
                    a_ = it % 2
                    it += 1
                    wr, twr = wrt[r], t_wrt[r]
                    pbs = [k.pb[b0 + m] for m in range(4)]

                    def smm(uv, nch):
                        for part in range(2):
                            for kk in range(TCH):
                                sidx = (TCH - 1 - kk) if r == 0 else kk
                                for m in range(4):
                                    mm(P, k.ps[:, b0 + m, part * 256: part * 256 + nch], wr[32 * m:32 * m + 32, kk, part, :],
                                       uv[32 * m:32 * m + 32, :, sidx], kk == 0, kk == TCH - 1, [twr, tu], [pbs[m]], tp=(32 * m, 0))
                    smm(u3, NCHL)
                    if sl == 0:
                        for m in range(4):
                            cp(P, "dve" if m % 2 == 0 else "act", ssb[:, m, :, :],
                               k.ps[:, b0 + m, :].rearrange("p (a b) -> p a b", a=2), [pbs[m]], [t_ssb])
                        dma(P, "sp", SD[r].ap()[:, 4 * j:4 * j + 4, :, 0:NCHL], ssb[:], reads=[t_ssb])
                        smm(u3c, NCHX)
                        for m in range(4):
                            cp(P, "dve" if m % 2 == 0 else "act", ssc[:, m, :, :],
                               k.ps[:, b0 + m, :].rearrange("p (a b) -> p a b", a=2)[:, :, 0:NCHX], [pbs[m]], [t_ssc])
                        dma(P, "sp", SD[r].ap()[:, 4 * j:4 * j + 4, :, NCHL:NCHT], ssc[:], reads=[t_ssc])
                    else:
                        n = 128
                        shp = [128, 4, n]
                        c0 = r * 64 + 4 * j
                        Rr = S.rpr[:, 0, c0:c0 + 4].unsqueeze(2).to_broadcast(shp)
                        Ri = S.rpi[:, 0, c0:c0 + 4].unsqueeze(2).to_broadcast(shp)
                        ia, ib = (0, 1) if r == 0 else (1, 0)
                        pv = k.ps[:, b0:b0 + 4, :]
                        Ar, Ai = pv[:, :, ia:256:2], pv[:, :, 256 + ia:512:2]
                        Br, Bi = pv[:, :, ib:256:2], pv[:, :, 256 + ib:512:2]
                        t = [x[:] for x in tm0[a_]]
                        o_r, o_i = cmuladd("dve", None, None, Ar, Ai, Br, Bi, Rr, Ri, t, t_tm0[a_], pbs)
                        tt(P, "dve", XB[r][:, sl - 1, :, 0, :], o_r, Br, ALU.add, [t_tm0[a_][0]] + pbs, [t_xb[r]])
                        tt(P, "dve", XB[r][:, sl - 1, :, 1, :], o_i, Bi, ALU.add, [t_tm0[a_][2]] + pbs, [t_xb[r]])
            for r in range(2):
                for l_ in range(1, 8):
                    n = 128 >> l_
                    src, tsrc = (XB[r], t_xb[r]) if l_ % 2 == 1 else (XC[r], t_xc[r])
                    dst, tdst = (XC[r], t_xc[r]) if l_ % 2 == 1 else (XB[r], t_xb[r])
                    shp = [128, NS, 4, n]
                    c0 = r * 64 + 4 * j
                    Rr = S.rpr[:, l_, c0:c0 + 4].unsqueeze(1).unsqueeze(3).to_broadcast(shp)
                    Ri = S.rpi[:, l_, c0:c0 + 4].unsqueeze(1).unsqueeze(3).to_broadcast(shp)
                    ia, ib = (0, 1) if r == 0 else (1, 0)
                    Ar, Ai = src[:, :, :, 0, ia:2 * n:2], src[:, :, :, 1, ia:2 * n:2]
                    Br, Bi = src[:, :, :, 0, ib:2 * n:2], src[:, :, :, 1, ib:2 * n:2]
                    t = [x[:, :, :, 0:n] for x in tm1[r]]
                    o_r, o_i = cmuladd("dve", None, None, Ar, Ai, Br, Bi, Rr, Ri, t, t_tm1[r], [tsrc])
                    tt(P, "dve", dst[:, :, :, 0, 0:n], o_r, Br, ALU.add, [t_tm1[r][0], tsrc], [tdst])
                    tt(P, "dve", dst[:, :, :, 1, 0:n], o_i, Bi, ALU.add, [t_tm1[r][2], tsrc], [tdst])
            for r in range(2):
                for part in range(2):
                    cp(P, "act", zs[:, 1:4, r, part, 4 * j:4 * j + 4], XC[r][:, :, :, part, 0], [t_xc[r]], [t_zs])
    P.barrier()


def cmul_add(P, eng, out_r, out_i, a_r, a_i, x_r, x_i, z_r, z_i, tm, R, Wt):
    tt(P, eng, tm[0][:], a_r, x_r, ALU.mult, R, Wt)
    tt(P, eng, tm[1][:], a_i, x_i, ALU.mult, R, Wt)
    tt(P, eng, tm[2][:], a_i, x_r, ALU.mult, R, Wt)
    tt(P, eng, tm[3][:], a_r, x_i, ALU.mult, R, Wt)
    tt(P, eng, tm[0][:], tm[0][:], tm[1][:], ALU.subtract, R, Wt)
    tt(P, eng, tm[2][:], tm[2][:], tm[3][:], ALU.add, R, Wt)
    tt(P, eng, out_r, tm[0][:], z_r, ALU.add, R, Wt)
    tt(P, eng, out_i, tm[2][:], z_i, ALU.add, R, Wt)


def l1_combine(P, k, S, C, zall, tz):
    with ExitStack() as st:
        hk = [P.sb(st, f"hk{i}", [128, 2, 64], F32) for i in range(2)]
        tm = [P.sb(st, f"cbt{i}", [128, 64], F32) for i in range(4)]
        R = [tz, S.t_tab, C.t_c]
        Wt = [C.t_c]
        for r in range(2):
            a_r = S.a4r[:, r * 64:(r + 1) * 64]
            a_i = S.a4i[:, r * 64:(r + 1) * 64]
            order = [0, 1, 2, 3] if r == 0 else [3, 2, 1, 0]
            cur_r, cur_i = C.hctx[:, r, 0, :], C.hctx[:, r, 1, :]
            ts(P, "dve", C.hin[:, r, 0, :], cur_r, S.sel[:, order[0]:order[0] + 1], ALU.mult, R, Wt)
            ts(P, "dve", C.hin[:, r, 1, :], cur_i, S.sel[:, order[0]:order[0] + 1], ALU.mult, R, Wt)
            for idx in range(1, 4):
                kprev, kc = order[idx - 1], order[idx]
                nh = hk[idx % 2]
                cmul_add(P, "dve", nh[:, 0, :], nh[:, 1, :], a_r, a_i, cur_r, cur_i, zall[:, kprev, r, 0, :], zall[:, kprev, r, 1, :],
                         tm, R, Wt)
                cur_r, cur_i = nh[:, 0, :], nh[:, 1, :]
                stt(P, "dve", C.hin[:, r, 0, :], cur_r, S.sel[:, kc:kc + 1], C.hin[:, r, 0, :], ALU.mult, ALU.add, R, Wt)
                stt(P, "dve", C.hin[:, r, 1, :], cur_i, S.sel[:, kc:kc + 1], C.hin[:, r, 1, :], ALU.mult, ALU.add, R, Wt)
    P.barrier()


def l1_outputs(P, k, I, S, UD, HD, GD):
    with ExitStack() as st:
        cr = P.sb(st, "cr", [128, 128, 16], F32)
        ci = P.sb(st, "ci", [128, 128, 16], F32)
        dcol = P.sb(st, "dcol", [128, 16], F32)
        t_c = P.tile("cri")
        dma(P, "sp", cr[:], I["cre_A"].ap(), writes=[t_c])
        dma(P, "sp", ci[:], I["cim_A"].ap(), writes=[t_c])
        dma(P, "sp", dcol[:], I["d_col"].ap(), writes=[t_c])
        W = table_work(P, st)
        bigs = [P.sb(st, f"big{r}", [128, 16, 2, 4, 128], BF16) for r in range(2)]
        t_big = P.tiles(2, "big")
        cpads = [P.sb(st, f"cpad{r}", [128, 1, 2, 4, 128], BF16) for r in range(2)]
        t_cpad = P.tiles(2, "cpad")
        for r in range(2):
            mset(P, "pool", bigs[r][:].rearrange("p a b c d -> p (a b c d)"), 0.0, [t_big[r]])
            mset(P, "pool", cpads[r][:].rearrange("p a b c d -> p (a b c d)"), 0.0, [t_cpad[r]])
        kt = [P.sb(st, f"kt{r}", [128, 16, 128], BF16) for r in range(2)]
        t_kt = P.tiles(2, "kt")
        uj = [P.sb(st, "uj4", [128, NT], BF16)] * 2
        t_uj = [P.tile("uj4")] * 2
        utm = P.sb(st, "utm", [128, TCH, NCHL], BF16)
        t_utm = P.tile("utm")
        hj = [[P.sb(st, f"hj{r}", [128, 4, 2, NCHL], BF16)] * 2 for r in range(2)]
        t_hj = [[P.tile("hj")] * 2 for r in range(2)]
        ysb = P.sb(st, "ysb", [128, NT], F32)
        t_ysb = P.tile("ysb")

        def load_j(j):
            for r in range(2):
                dma(P, "sp", hj[r][j % 2][:], HD[r].ap()[:, 4 * j:4 * j + 4, :, 0:NCHL], writes=[t_hj[r][j % 2]])
        for j in range(16):
            u_, tu = uj[j % 2], t_uj[j % 2]
            dma(P, "sp", u_[:], UD.ap()[j * 128:(j + 1) * 128, 0:NT], writes=[tu])
            load_j(j)
            cp(P, "pool", utm[:], u_[:].rearrange("p (c t) -> p t c", t=TCH), [tu], [t_utm])
            for r in range(2):
                rM0 = r * 64 + 4 * j
                build_cmp(P, S, W, cr, ci, 0, 1, rM0, True)
                pad_cmp(P, S, W, cpads[r], t_cpad[r], 1)
                build_cmp(P, S, W, S.bbr, S.bbi, 0, 16, rM0, False)
                pad_cmp(P, S, W, bigs[r], t_big[r], 16)
                for rd in range(4):
                    bank = 4 + (rd % 2)
                    for q in range(4):
                        kk = rd * 4 + q
                        i_ = 0
                        for m in range(4):
                            for part in range(2):
                                mm(P, k.ps[:, bank, q * 128:(q + 1) * 128], bigs[r][:, kk, part, m, :], cpads[r][:, 0, part, m, :],
                                   i_ == 0, i_ == 7, [t_big[r], t_cpad[r]], [k.pb[bank]])
                                i_ += 1
                    cp(P, "act", kt[r][:, rd * 4:(rd + 1) * 4, :].rearrange("p a b -> p (a b)"), k.ps[:, bank, :], [k.pb[bank]],
                       [t_kt[r]])
            for r in range(2):
                rM0 = r * 64 + 4 * j
                build_cmp(P, S, W, cr, ci, 1, 16, rM0, True)
                pad_cmp(P, S, W, bigs[r], t_big[r], 16)
            ysb3 = ysb[:].rearrange("p (c t) -> p c t", t=TCH)
            for th in range(2):
                t0_ = 8 * th
                mlist = []
                for r in range(2):
                    h_, thj = hj[r][j % 2], t_hj[r][j % 2]
                    for kk in range(TCH):
                        for b in range(4):
                            tb0 = t0_ + 2 * b
                            if r == 0:
                                tlo, thi = max(tb0, kk), tb0 + 2
                                if tlo >= thi:
                                    continue
                                rhs = utm[:, tlo - kk:thi - kk, :]
                            else:
                                tlo, thi = tb0, min(tb0 + 2, TCH - kk)
                                if tlo >= thi:
                                    continue
                                rhs = utm[:, tlo + kk:thi + kk, :]
                            out = k.ps[:, b, (tlo - tb0) * 256:(thi - tb0) * 256].rearrange("p (t c) -> p t c", c=256)
                            mlist.append((b, out, kt[r][:, kk, :], rhs, [t_kt[r], t_utm]))
                    for t in range(t0_, t0_ + 8):
                        b = (t - t0_) // 2
                        col0 = ((t - t0_) % 2) * 256
                        tti = (t + 1) if r == 0 else (TCH - t)
                        for m in range(4):
                            for part in range(2):
                                out = k.ps[:, b, col0:col0 + 256]
                                mlist.append((b, out, bigs[r][:, tti - 1, part, m, :], h_[:, m, part, :], [t_big[r], thj]))
                lastidx = {}
                for i_, e in enumerate(mlist):
                    lastidx[e[0]] = i_
                seen = set()
                for i_, (b, out, lhsT, rhs, rds) in enumerate(mlist):
                    mm(P, out, lhsT, rhs, b not in seen, lastidx[b] == i_, rds, [k.pb[b]])
                    seen.add(b)
                pflat = k.ps[:, 0:4, :].rearrange("p a b -> p (a b)").rearrange("p (t c) -> p c t", c=256)
                cp(P, "act", ysb3[:, :, t0_:t0_ + 8], pflat, [k.pb[0], k.pb[1], k.pb[2], k.pb[3]], [t_ysb])
            stt(P, "dve", ysb[:], u_[:], dcol[:, j:j + 1], ysb[:], ALU.mult, ALU.add, [tu, t_c, t_ysb], [t_ysb])
            act(P, u_[:], ysb[:], AF.Gelu_apprx_tanh, [t_ysb, tu], [tu])
            dma(P, "pool", GD.ap()[j * 128:(j + 1) * 128, :], u_[:], reads=[tu])
    P.barrier()


def l1_glu(P, k, I, GD, G2D):
    with ExitStack() as st:
        wg = P.sb(st, "wg", [128, 16, E2], BF16)
        t_wg = P.tile("wg")
        wsrc = I["ssm_w_glu"].ap().rearrange("(j p) c -> p j c", p=128)
        for q in range(8):
            dma(P, "pool", wg[:, q * 2:(q + 1) * 2, :], wsrc[:, q * 2:(q + 1) * 2, :], writes=[t_wg])
        bg = P.sb(st, "bglu", [128, 16], F32)
        t_bg = P.tile("bglu")
        dma(P, "sp", bg[:], I["bglu_col"].ap(), writes=[t_bg])
        gb = [P.sb(st, f"ggb{i}", [128, 16, 512], BF16) for i in range(2)]
        t_gb = P.tiles(2, "ggb")
        g2 = [P.sb(st, f"gg2{i}", [128, 16, 512], BF16) for i in range(2)]
        t_g2 = P.tiles(2, "gg2")
        sg = [P.sb(st, f"sg{i}", [128, 512], F32) for i in range(2)]
        t_sg = P.tiles(2, "sg")
        gsrc = GD.ap().rearrange("(j p) t -> p j t", p=128)
        gdst = G2D.ap().rearrange("(j p) t -> p j t", p=128)

        def load_g(b):
            dma(P, "sp", gb[b % 2][:], gsrc[:, :, b * 512:(b + 1) * 512], writes=[t_gb[b % 2]])
        load_g(0)
        it = 0
        for b in range(8):
            if b + 1 < 8:
                load_g(b + 1)
            g_, tg = gb[b % 2], t_gb[b % 2]
            o_, to = g2[b % 2], t_g2[b % 2]
            for jo in range(16):
                s = it % 2
                it += 1
                bank = s
                for j in range(16):
                    mm(P, k.ps[:, bank, :], wg[:, j, jo * 128:(jo + 1) * 128], g_[:, j, :], j == 0, j == 15, [t_wg, tg], [k.pb[bank]])
                act(P, sg[s][:], k.ps[:, bank, :], AF.Sigmoid, [k.pb[bank], t_bg], [t_sg[s]], scale=1.0, bias=bg[:, jo:jo + 1])
                tt(P, "dve", o_[:, jo, :], g_[:, jo, :], sg[s][:], ALU.mult, [tg, t_sg[s]], [to])
            dma(P, "pool", gdst[:, :, b * 512:(b + 1) * 512], o_[:], reads=[to])
    P.barrier()


def l1_out(P, k, I, X1src, HXD, G2D, OUT):
    with ExitStack() as st:
        wz = P.sb(st, "wz", [128, 8, E2], BF16)
        t_wz = P.tile("wz")
        wsrc = I["ssm_w_in"].ap().rearrange("(dt p) c -> p dt c", p=128)
        for q in range(4):
            dma(P, "pool", wz[:, :, q * 512:(q + 1) * 512], wsrc[:, :, E2 + q * 512:E2 + (q + 1) * 512], writes=[t_wz])
        wo = P.sb(st, "wo1", [128, 16, 1024], BF16)
        t_wo = P.tile("wo1")
        for q in range(4):
            dma(P, "pool", wo[:, q * 4:(q + 1) * 4, :],
                I["ssm_w_out"].ap().rearrange("(j p) d -> p j d", p=128)[:, q * 4:(q + 1) * 4, :], writes=[t_wo])
        hxb = [P.sb(st, f"hxb5{i}", [128, 8, 512], BF16) for i in range(2)]
        t_hxb = P.tiles(2, "hxb5")
        g2 = [P.sb(st, f"g25{i}", [128, 16, 512], BF16) for i in range(2)]
        t_g2 = P.tiles(2, "g25")
        gat = P.sb(st, "gat", [128, 16, 512], BF16)
        t_gat = P.tile("gat")
        sz = [P.sb(st, f"sz5{i}", [128, 512], F32) for i in range(2)]
        t_sz = P.tiles(2, "sz5")
        xt = [P.sb(st, f"xt5{i}", [128, 1024], F32) for i in range(2)]
        t_xt = P.tiles(2, "xt5")
        ot = [P.sb(st, f"ot5{i}", [128, 1024], F32) for i in range(2)]
        t_ot = P.tiles(2, "ot5")
        Ws = ln_work(P, st)
        load_ln_gate(P, k, I, st, 1)
        gsrc = G2D.ap().rearrange("(j p) t -> p j t", p=128)

        def load_b(b):
            dma(P, "sp", hxb[b % 2][:], HXD.ap()[:, :, b * 512:(b + 1) * 512], writes=[t_hxb[b % 2]])
            dma(P, "sp", g2[b % 2][:], gsrc[:, :, b * 512:(b + 1) * 512], writes=[t_g2[b % 2]])
        load_b(0)
        it = 0
        it2 = 0
        for b in range(8):
            if b + 1 < 8:
                load_b(b + 1)
            hb, thb = hxb[b % 2], t_hxb[b % 2]
            g_, tg = g2[b % 2], t_g2[b % 2]
            for j in range(16):
                s = it % 2
                it += 1
                bank = 4 + s
                for dt in range(8):
                    mm(P, k.ps[:, bank, :], wz[:, dt, j * 128:(j + 1) * 128], hb[:, dt, :], dt == 0, dt == 7, [t_wz, thb], [k.pb[bank]])
                act(P, sz[s][:], k.ps[:, bank, :], AF.Silu, [k.pb[bank]], [t_sz[s]])
                tt(P, "dve", gat[:, j, :], g_[:, j, :], sz[s][:], ALU.mult, [tg, t_sz[s]], [t_gat])
            for tl in range(4):
                s = it2 % 2
                it2 += 1
                psb = 2 * s
                r0 = b * 512 + tl * 128
                dma(P, "sp", xt[s][:], X1src.ap()[r0:r0 + 128, :], writes=[t_xt[s]])
                for h in range(2):
                    for j in range(16):
                        mm(P, k.ps[:, psb + h, :], gat[:, j, tl * 128:(tl + 1) * 128], wo[:, j, h * 512:(h + 1) * 512],
                           j == 0, j == 15, [t_gat, t_wo], [k.pb[psb + h]])
                ln_residual(P, k, Ws[s], psb, xt[s][:], t_xt[s], 0, ot[s][:], t_ot[s])
                dma(P, "pool", OUT.ap()[r0:r0 + 128, :], ot[s][:], reads=[t_ot[s]])
    P.barrier()


def _core_inputs_common(inp, core):
    b, kk = core // 4, core % 4
    t0 = kk * NT
    x = inp["x"]
    f32 = np.float32
    d = {}
    xh = np.zeros((NXH, D), f32)
    lo, hi = t0 - HALO, t0 + NT + HALO
    slo, shi = max(lo, 0), min(hi, x.shape[1])
    xh[slo - lo:shi - lo] = x[b, slo:shi]
    d["xT"] = np.ascontiguousarray(xh.T)
    d["xtok"] = np.ascontiguousarray(x[b, t0:t0 + NT])
    d["ctxT"] = np.ascontiguousarray(inp["ctx"][b].T)
    d["ctxtok"] = np.ascontiguousarray(inp["ctx"][b])
    cv = np.stack([inp["c"][b].reshape(8, 128).T, inp["c_ctx"].reshape(8, 128).T], axis=-1)
    d["cvec"] = np.ascontiguousarray(cv.astype(f32))
    edge = np.ones((128, 2), f32)
    if kk == 0:
        edge[:, 0] = 0.0
    if kk == 3:
        edge[:, 1] = 0.0
    d["edge"] = edge
    sel = np.zeros((128, 4), f32)
    sel[:, kk] = 1.0
    d["sel"] = sel
    return d


def _shared_inputs(inp):
    f32 = np.float32
    s = {}
    s["ident"] = np.eye(128, dtype=f32)
    s["ada_w"] = np.ascontiguousarray(inp["ada_w"])
    ab = inp["ada_b"]
    s["adab_col"] = np.ascontiguousarray(ab.reshape(2, 24, 128).transpose(2, 0, 1))
    s["adab_grow"] = np.ascontiguousarray(ab[None, :, 2048:3072])
    s["lngB"] = np.ascontiguousarray(np.broadcast_to(inp["ln_g"][None], (128, 2, D)))
    s["lncol0"] = np.ascontiguousarray(np.stack([inp["ln_g"][0].reshape(8, 128).T, inp["ln_b"][0].reshape(8, 128).T], axis=1))
    s["lnbB"] = np.ascontiguousarray(np.broadcast_to(inp["ln_b"][None], (128, 2, D)))
    s["conv_w_in"] = np.ascontiguousarray(inp["conv_w_in"][0])
    s["conv_w_out"] = np.ascontiguousarray(inp["conv_w_out"][0])
    s["cw"] = np.ascontiguousarray(inp["conv_w"][0].reshape(3, 16, 128).transpose(2, 1, 0))
    return s


L0_INPUTS = {
    "xT": [D, NXH], "xtok": [NT, D], "ctxT": [D, NCX], "ctxtok": [NCX, D], "cvec": [128, 8, 2],
    "edge": [128, 2], "ident": [128, 128], "ada_w": [2, D, 3 * D], "adab_col": [128, 2, 24],
    "adab_grow": [1, 2, D], "lngB": [128, 2, D], "lnbB": [128, 2, D],
    "conv_w_in": [D, 8192], "conv_w_out": [E2, D], "cw": [128, 16, 3],
}
L1_INPUTS = {
    "cvec": [128, 8, 2], "ident": [128, 128], "ada_w": [2, D, 3 * D], "adab_col": [128, 2, 24],
    "adab_grow": [1, 2, D], "lngB": [128, 2, D], "lnbB": [128, 2, D], "sel": [128, 4], "mgi": [128, 2],
    "ssm_w_in": [D, 2 * E2], "ssm_w_glu": [E2, E2], "ssm_w_out": [E2, D], "d_col": [128, 16], "bglu_col": [128, 16],
    "lamre_A": [128, 128], "lamim_A": [128, 128], "logstep_A": [128, 128],
    "bre_A": [128, 128, 16], "bim_A": [128, 128, 16], "cre_A": [128, 128, 16], "cim_A": [128, 128, 16],
}


def _shared_inputs_l1(inp):
    f32 = np.float32
    s = {}
    s["ssm_w_in"] = np.ascontiguousarray(inp["ssm_w_in"][0])
    s["ssm_w_glu"] = np.ascontiguousarray(inp["ssm_w_glu"][0])
    s["ssm_w_out"] = np.ascontiguousarray(inp["ssm_w_out"][0])
    s["d_col"] = np.ascontiguousarray(inp["ssm_d"][0].reshape(16, 128).T)
    s["bglu_col"] = np.ascontiguousarray(inp["ssm_b_glu"][0].reshape(16, 128).T)

    def lamA(a):
        return np.ascontiguousarray(a.reshape(2, 64, 2, 64).transpose(2, 3, 0, 1).reshape(128, 128))
    s["lamre_A"] = lamA(inp["ssm_lam_re"][0])
    s["lamim_A"] = lamA(inp["ssm_lam_im"][0])
    ls = inp["ssm_log_step"][0].reshape(2, 64, 2).transpose(2, 0, 1)
    s["logstep_A"] = np.ascontiguousarray(np.broadcast_to(ls[:, None], (2, 64, 2, 64)).reshape(128, 128))

    def bA(a):
        return np.ascontiguousarray(a.reshape(2, 64, 2, 64, 16).transpose(2, 3, 0, 1, 4).reshape(128, 128, 16))

    def cA(a):
        return np.ascontiguousarray(a.reshape(2, 64, 2, 16, 64).transpose(2, 4, 0, 1, 3).reshape(128, 128, 16))
    s["bre_A"] = bA(inp["ssm_b_re"][0])
    s["bim_A"] = bA(inp["ssm_b_im"][0])
    s["cre_A"] = cA(inp["ssm_c_re"][0])
    s["cim_A"] = cA(inp["ssm_c_im"][0])
    mgi = np.zeros((128, 2), f32)
    mgi[:64, 0] = 1.0
    mgi[64:, 1] = 1.0
    s["mgi"] = mgi
    return s


def carry_state(P, st):
    C = K()
    C.hctx = P.sb(st, "hctx", [128, 2, 2, 64], F32)
    C.hin = P.sb(st, "hin", [128, 2, 2, 64], F32)
    C.z = P.sb(st, "zloc", [128, 2, 2, 64], F32)
    C.t_c = P.tile("cstate")
    return C


def layer1(P, k, I, mode, X1src, CTX1src, ZOUT, ZALL, OUT, X1F=None):
    dbg = "ExternalOutput" if DEBUG else "Internal"
    UD = P.dram("ud", [E2, NTX], BF16, kind=dbg)
    HXD = P.dram("hxd", [128, 8, NTX], BF16)
    SD = [P.dram(f"sd{r}", [128, 64, 2, NCHT], F32) for r in range(2)]
    HD = [P.dram(f"hd{r}", [128, 64, 2, NCHL], BF16, kind=dbg) for r in range(2)]
    GD = P.dram("gd", [E2, NT], BF16, kind=dbg)
    G2D = P.dram("g2d", [E2, NT], BF16)
    ada(P, k, I, 1)
    with ExitStack() as st:
        S = s5_tables(P, k, I, st)
        C = carry_state(P, st)
        zall = P.sb(st, "zall", [128, 4, 2, 2, 64], F32)
        tz = P.tile("zall")
        l1_inproj(P, k, I, X1src, CTX1src, UD, HXD)
        if mode in ("A", "B"):
            l1_summaries(P, k, I, S, UD, SD)
        if mode == "A":
            fin = [(C.z[:, r, 0, :], C.z[:, r, 1, :], C.t_c) for r in range(2)]
            l1_carry(P, k, S, SD, HD, [None, None], fin, False, True, False)
            dma(P, "sp", ZOUT.ap(), C.z[:], reads=[C.t_c])
            P.barrier()
        if mode == "FUSED":
            UDF = [P.dram(f"udf{i}", [E2, NT], BF16) for i in range(3)]
            SDF = [[P.dram(f"sdf{i}_{r}", [128, 64, 2, NCHL], F32) for r in range(2)] for i in range(3)]
            zs = P.sb(st, "zs", [128, 4, 2, 2, 64], F32)
            perm = P.sb(st, "perm", [128, 16], F32)
            t_zs = P.tile("zs")
            dma(P, "sp", perm[:], I["perm"].ap(), writes=[t_zs])
            mset(P, "dve", zs[:].rearrange("p a b c d -> p (a b c d)"), 0.0, [t_zs])
            lncol = P.sb(st, "lncol", [128, 2, 8], F32)
            k.mcol2 = P.sb(st, "mcol2", [128, 16], F32)
            k.t_mcol2 = P.tile("mcol2")
            dma(P, "sp", lncol[:], I["lncol0"].ap(), writes=[k.t_mcol2])
            tt(P, "dve", k.mcol2[:, 8:16], lncol[:, 0, :], k.mcol[:, 8:16, 0], ALU.mult, [k.t_mcol, k.t_mcol2], [k.t_mcol2])
            tt(P, "dve", k.mcol2[:, 0:8], lncol[:, 1, :], k.mcol[:, 8:16, 0], ALU.mult, [k.t_mcol, k.t_mcol2], [k.t_mcol2])
            tt(P, "dve", k.mcol2[:, 0:8], k.mcol2[:, 0:8], k.mcol[:, 0:8, 0], ALU.add, [k.t_mcol, k.t_mcol2], [k.t_mcol2])
            for slot in range(1, 4):
                l1_inproj(P, k, I, X1F[slot - 1], None, UDF[slot - 1], HXD, do_ctx=False, store_hx=False, fold=True)
            if MERGED_ZR:
                l1_summaries_zr(P, k, I, S, [UD] + UDF, SD, zs, t_zs)
            else:
                l1_summaries_multi(P, k, I, S, [UD] + UDF, [SD] + SDF)
                l1_zreduce_foreign(P, k, S, [SD] + SDF, zs, t_zs)
            zf = zall[:].rearrange("p a b c d -> p a (b c d)")
            zsf = zs[:].rearrange("p a b c d -> p a (b c d)")
            for j in range(4):
                for sl in range(4):
                    if sl == 0:
                        ts(P, "dve", zf[:, j, :], zsf[:, sl, :], perm[:, sl * 4 + j:sl * 4 + j + 1], ALU.mult, [t_zs, tz], [tz])
                    else:
                        stt(P, "dve", zf[:, j, :], zsf[:, sl, :], perm[:, sl * 4 + j:sl * 4 + j + 1], zf[:, j, :], ALU.mult, ALU.add,
                            [t_zs, tz], [tz])
            P.barrier()
        if mode == "B":
            dma(P, "sp", zall[:], ZALL.ap(), writes=[tz])
        if mode in ("B", "FUSED"):
            fin = [(C.hctx[:, r, 0, :], C.hctx[:, r, 1, :], C.t_c) for r in range(2)]
            l1_carry(P, k, S, SD, HD, [None, None], fin, True, False, False)
            l1_combine(P, k, S, C, zall, tz)
            ini = [(C.hin[:, r, 0, :], C.hin[:, r, 1, :], C.t_c) for r in range(2)]
            l1_carry(P, k, S, SD, HD, ini, [None, None], False, True, True)
            l1_outputs(P, k, I, S, UD, HD, GD)
    if mode in ("B", "FUSED"):
        l1_glu(P, k, I, GD, G2D)
        l1_out(P, k, I, X1src, HXD, G2D, OUT)


FUSED_INPUTS = dict(L1_INPUTS)
FUSED_INPUTS.update({kk_: v for kk_, v in L0_INPUTS.items() if kk_ not in ("xT", "xtok", "edge")})
for _s in range(4):
    FUSED_INPUTS[f"xT{_s}"] = [D, NXH]
    FUSED_INPUTS[f"xtok{_s}"] = [NT, D]
    FUSED_INPUTS[f"edge{_s}"] = [128, 2]
FUSED_INPUTS["perm"] = [128, 16]
FUSED_INPUTS["lncol0"] = [128, 2, 8]


def build(mode):
    P = Prog()
    k = K()
    I = {}
    if mode == "L0":
        for nm, shp in L0_INPUTS.items():
            I[nm] = P.dram(nm, shp, F32, kind="ExternalInput")
        X1 = P.dram("x1_out", [NT, D], F32, kind="ExternalOutput")
        CTX1 = P.dram("ctx1_out", [NCX, D], F32, kind="ExternalOutput")
        G0 = P.dram("g0", [E2, NTX], BF16)
        setup_globals(P, k, I)
        layer0(P, k, I, X1, CTX1, G0)
    elif mode in ("A", "B"):
        for nm, shp in L1_INPUTS.items():
            I[nm] = P.dram(nm, shp, F32, kind="ExternalInput")
        X1 = P.dram("x1_in", [NT, D], F32, kind="ExternalInput")
        CTX1 = P.dram("ctx1_in", [NCX, D], F32, kind="ExternalInput")
        ZOUT = ZALL = OUT = None
        if mode == "A":
            ZOUT = P.dram("z_out", [128, 2, 2, 64], F32, kind="ExternalOutput")
        else:
            ZALL = P.dram("z_all", [128, 4, 2, 2, 64], F32, kind="ExternalInput")
            OUT = P.dram("out", [NT, D], F32, kind="ExternalOutput")
        setup_globals(P, k, I)
        layer1(P, k, I, mode, X1, CTX1, ZOUT, ZALL, OUT)
    elif mode == "FUSED":
        for nm, shp in FUSED_INPUTS.items():
            I[nm] = P.dram(nm, shp, F32, kind="ExternalInput")
        OUT = P.dram("out", [NT, D], F32, kind="ExternalOutput")
        X1 = P.dram("x1s", [NT, D], F32)
        X1F = [P.dram(f"x1f{i}", [NT, D], F32) for i in range(3)]
        CTX1 = P.dram("ctx1s", [NCX, D], F32)
        G0 = P.dram("g0", [E2, NTX], BF16)
        setup_globals(P, k, I)
        for slot in range(4):
            layer0(P, k, I, X1 if slot == 0 else X1F[slot - 1], CTX1, G0, slot=slot, do_ctx=(slot == 0), do_ada=(slot == 0),
                   ln_affine=(slot == 0))
        layer1(P, k, I, "FUSED", X1, CTX1, None, None, OUT, X1F=X1F)
    P.emit()
    return P


def _slot_inputs(inp, core):
    b, kk = core // 4, core % 4
    f32 = np.float32
    x = inp["x"]
    order = [kk] + [j for j in range(4) if j != kk]
    d = {}
    perm = np.zeros((128, 16), f32)
    for sl, pos in enumerate(order):
        t0 = pos * NT
        xh = np.zeros((NXH, D), f32)
        lo, hi = t0 - HALO, t0 + NT + HALO
        slo, shi = max(lo, 0), min(hi, x.shape[1])
        xh[slo - lo:shi - lo] = x[b, slo:shi]
        d[f"xT{sl}"] = np.ascontiguousarray(xh.T)
        d[f"xtok{sl}"] = np.ascontiguousarray(x[b, t0:t0 + NT])
        edge = np.ones((128, 2), f32)
        if pos == 0:
            edge[:, 0] = 0.0
        if pos == 3:
            edge[:, 1] = 0.0
        d[f"edge{sl}"] = edge
        perm[:, sl * 4 + pos] = 1.0
    d["perm"] = perm
    return d


def run_fused(inp):
    P = build("FUSED")
    sh = _shared_inputs(inp)
    sh.update(_shared_inputs_l1(inp))
    maps = []
    for core in range(8):
        d = _core_inputs_common(inp, core)
        d.update(_slot_inputs(inp, core))
        maps.append({nm: (d[nm] if nm in d else sh[nm]) for nm in FUSED_INPUTS})
    res = run_bass_kernel_spmd(P.nc, maps, core_ids=list(range(8)))
    return res.results


def run_L0(inp):
    P = build("L0")
    sh = _shared_inputs(inp)
    maps = []
    for core in range(8):
        d = _core_inputs_common(inp, core)
        maps.append({nm: (d[nm] if nm in d else sh[nm]) for nm in L0_INPUTS})
    res = run_bass_kernel_spmd(P.nc, maps, core_ids=list(range(8)))
    return res.results


def run_L1(inp, mode, x1s, ctx1s, zalls=None):
    P = build(mode)
    sh = _shared_inputs(inp)
    sh.update(_shared_inputs_l1(inp))
    maps = []
    for core in range(8):
        d = _core_inputs_common(inp, core)
        m = {nm: (d[nm] if nm in d else sh[nm]) for nm in L1_INPUTS}
        m["x1_in"] = x1s[core]
        m["ctx1_in"] = ctx1s[core]
        if mode == "B":
            m["z_all"] = zalls[core // 4]
        maps.append(m)
    res = run_bass_kernel_spmd(P.nc, maps, core_ids=list(range(8)))
    return res.results


def kernel_unfused(**inputs):
    inp = {k_: np.asarray(v, dtype=np.float32) for k_, v in inputs.items()}
    r0 = run_L0(inp)
    x1s = [np.ascontiguousarray(r0[c]["x1_out"]) for c in range(8)]
    ctx1s = [np.ascontiguousarray(r0[c]["ctx1_out"]) for c in range(8)]
    ra = run_L1(inp, "A", x1s, ctx1s)
    zalls = [np.ascontiguousarray(np.stack([ra[b * 4 + kk]["z_out"] for kk in range(4)], axis=1)) for b in range(2)]
    rb = run_L1(inp, "B", x1s, ctx1s, zalls)
    out = np.empty((2, 4 * NT, D), np.float32)
    for c in range(8):
        out[c // 4, (c % 4) * NT:(c % 4 + 1) * NT] = rb[c]["out"]
    return out


def kernel(**inputs):
    inp = {k_: np.asarray(v, dtype=np.float32) for k_, v in inputs.items()}
    rf = run_fused(inp)
    out = np.empty((2, 4 * NT, D), np.float32)
    for c in range(8):
        out[c // 4, (c % 4) * NT:(c % 4 + 1) * NT] = rf[c]["out"]
    return out
```

```python
from contextlib import ExitStack
import math
import numpy as np
import concourse.bass as bass
import concourse.mybir as mybir
from concourse.bass_utils import run_bass_kernel_spmd

F32 = mybir.dt.float32
BF16 = mybir.dt.bfloat16
ALU = mybir.AluOpType
AF = mybir.ActivationFunctionType
ENGS = ("pe", "act", "dve", "pool", "sp")
NDS = 8

D = 1024
E2 = 2048
NT = 4096
NCX = 256
NTX = NT + NCX
HALO = 64
NXH = NT + 2 * HALO
TCH = 16
NCHL = NT // TCH
NCHX = NCX // TCH
NCHT = NCHL + NCHX
CB = 64
ALPHA = 4.0 ** 0.25
LN_EPS = 1e-5
PI = math.pi
DEBUG = False
MERGED_ZR = True
SAME_ENGINE_SYNC = ("act", "dve", "pool")


class T_:
    __slots__ = ("name", "w", "r")

    def __init__(self, name):
        self.name = name
        self.w = []
        self.r = []


class Prog:
    def __init__(self):
        self.nc = bass.Bass("TRN2", target_bir_lowering=False)
        self.es = ExitStack()
        self.ops = []
        self.dma_rr = {e: 0 for e in ENGS}
        self.last_compute = {}
        self.last_dma = {}
        self.pending_barrier = {}
        self.ntile = 0

    def dram(self, name, shape, dt=F32, kind="Internal"):
        return self.nc.dram_tensor(name, list(shape), dt, kind=kind)

    def sb(self, st, name, shape, dt=F32):
        self.ntile += 1
        return st.enter_context(self.nc.sbuf_tensor(f"sb{self.ntile}_{name}", list(shape), dt))

    def tile(self, name="t"):
        self.ntile += 1
        return T_(f"{name}{self.ntile}")

    def tiles(self, n, name="t"):
        return [self.tile(name) for _ in range(n)]

    def op(self, eng, fn, reads=(), writes=(), dma=False):
        deps = set()
        for t in reads:
            deps.update(t.w)
        for t in writes:
            deps.update(t.w)
            deps.update(t.r)
        if eng in self.pending_barrier:
            deps.update(self.pending_barrier.pop(eng))
        oid = len(self.ops)
        slot = None
        if dma:
            slot = self.dma_rr[eng]
            self.dma_rr[eng] = (slot + 1) % NDS
            prev = self.last_dma.get((eng, slot))
            if prev is not None:
                deps.add(prev)
            self.last_dma[(eng, slot)] = oid
        else:
            self.last_compute[eng] = oid
        self.ops.append([eng, fn, sorted(deps), dma, slot])
        for t in reads:
            t.r.append(oid)
        for t in writes:
            t.w = [oid]
            t.r = []
        return oid

    def barrier(self):
        allp = list(self.last_compute.values()) + list(self.last_dma.values())
        for e in ENGS:
            self.pending_barrier[e] = set(allp) | self.pending_barrier.get(e, set())

    def emit(self):
        nc = self.nc
        ops = self.ops
        n = len(ops)
        needed = [False] * n
        for i, (eng, fn, deps, dma, slot) in enumerate(ops):
            if dma:
                needed[i] = True
            for d in deps:
                de, _, _, ddma, _ = ops[d]
                if de == eng and not ddma and not dma and (de == "pe" or de not in SAME_ENGINE_SYNC):
                    continue
                needed[d] = True
        sem_names = [f"s_{e}" for e in ENGS] + [f"d_{e}_{k}" for e in ENGS for k in range(NDS)]
        sems = {nm: self.es.enter_context(nc.semaphore(nm)) for nm in sem_names}
        cnt = {nm: 0 for nm in sem_names}
        ev = [None] * n
        for i, (eng, fn, deps, dma, slot) in enumerate(ops):
            if not needed[i]:
                continue
            nm = f"d_{eng}_{slot}" if dma else f"s_{eng}"
            cnt[nm] += 16 if dma else 1
            ev[i] = (nm, cnt[nm])
        self.maxsem = max(cnt.values())
        per_eng = {e: [] for e in ENGS}
        for i, o in enumerate(ops):
            per_eng[o[0]].append(i)

        with nc.Block() as block:
            def run(engname, Eh):
                waited = {}
                for i in per_eng[engname]:
                    eng, fn, deps, dma, slot = ops[i]
                    need = {}
                    for d in deps:
                        de, _, _, ddma, _ = ops[d]
                        if de == eng and not ddma and not dma and (de == "pe" or de not in SAME_ENGINE_SYNC):
                            continue
                        nm, v = ev[d]
                        if v > need.get(nm, 0):
                            need[nm] = v
                    for nm, v in need.items():
                        if v > waited.get(nm, 0):
                            Eh.wait_ge(sems[nm], v)
                            waited[nm] = v
                    ins = fn(Eh)
                    if ev[i] is not None:
                        ins.then_inc(sems[ev[i][0]], 16 if dma else 1)
                for nm, c in cnt.items():
                    if nm.startswith(f"d_{engname}_") and c > 0:
                        Eh.wait_ge(sems[nm], c)

            @block.sync
            def _(Eh):
                run("sp", Eh)

            @block.scalar
            def _(Eh):
                run("act", Eh)

            @block.vector
            def _(Eh):
                run("dve", Eh)

            @block.gpsimd
            def _(Eh):
                run("pool", Eh)

            @block.tensor
            def _(Eh):
                run("pe", Eh)
        return nc


def mm(P, out, lhsT, rhs, start, stop, reads, writes, tp=None):
    def f(Eh):
        if tp is not None:
            return Eh.matmul(out, lhsT=lhsT, rhs=rhs, start=start, stop=stop, tile_position=tp)
        return Eh.matmul(out, lhsT=lhsT, rhs=rhs, start=start, stop=stop)
    P.op("pe", f, reads=reads, writes=writes)


def dma(P, q, out, in_, reads=(), writes=()):
    P.op(q, lambda Eh: Eh.dma_start(out=out, in_=in_), reads=reads, writes=writes, dma=True)


def act(P, out, in_, func, reads, writes, scale=1.0, bias=0.0):
    P.op("act", lambda Eh: Eh.activation(out=out, in_=in_, func=func, bias=bias, scale=scale),
         reads=reads, writes=writes)


def tt(P, eng, out, in0, in1, op, reads, writes):
    P.op(eng, lambda Eh: Eh.tensor_tensor(out=out, in0=in0, in1=in1, op=op), reads=reads, writes=writes)


def ts(P, eng, out, in0, s1, op0, reads, writes, s2=None, op1=None):
    if op1 is None:
        P.op(eng, lambda Eh: Eh.tensor_scalar(out=out, in0=in0, scalar1=s1, scalar2=None, op0=op0),
             reads=reads, writes=writes)
    else:
        P.op(eng, lambda Eh: Eh.tensor_scalar(out=out, in0=in0, scalar1=s1, scalar2=s2, op0=op0, op1=op1),
             reads=reads, writes=writes)


def stt(P, eng, out, in0, scalar, in1, op0, op1, reads, writes):
    P.op(eng, lambda Eh: Eh.scalar_tensor_tensor(out=out, in0=in0, scalar=scalar, in1=in1, op0=op0, op1=op1),
         reads=reads, writes=writes)


def cp(P, eng, out, in_, reads, writes):
    if eng == "act":
        P.op("act", lambda Eh: Eh.activation(out=out, in_=in_, func=AF.Copy), reads=reads, writes=writes)
    else:
        P.op(eng, lambda Eh: Eh.tensor_copy(out=out, in_=in_), reads=reads, writes=writes)


def mset(P, eng, ap, val, writes):
    P.op(eng, lambda Eh: Eh.memset(ap, val), writes=writes)


class K:
    pass


def setup_globals(P, k, inputs_decl):
    nc = P.nc
    st = P.es
    k.ps = st.enter_context(nc.psum_tensor("ps_all", [128, 8, 512], F32))
    k.pb = P.tiles(8, "pb")
    k.identf = P.sb(st, "identf", [128, 128], F32)
    k.identb = P.sb(st, "identb", [128, 128], BF16)
    k.t_ident = P.tile("ident")
    k.ones1 = P.sb(st, "ones1", [1, 128], F32)
    k.t_ones = P.tile("ones")
    dma(P, "sp", k.identf[:], inputs_decl["ident"].ap(), writes=[k.t_ident])
    cp(P, "dve", k.identb[:], k.identf[:], [k.t_ident], [k.t_ident])
    mset(P, "dve", k.ones1[:], 1.0, [k.t_ones])
    k.mcol = P.sb(st, "mcol", [128, 24, 2], F32)
    k.t_mcol = P.tile("mcol")
    k.GROW = P.dram("growd", [2, 2, 1024], F32)
    k.t_growd = P.tile("growd")


def ps2(k, b):
    return k.ps[:, b:b + 2, :].rearrange("p a b -> p (a b)")


def ada(P, k, I, layer):
    nc = P.nc
    with ExitStack() as st:
        cv = P.sb(st, "cv", [128, 8, 2], F32)
        scv = P.sb(st, "scv", [128, 8, 2], F32)
        adab = P.sb(st, "adab", [128, 24], F32)
        wch = [P.sb(st, f"adaw{i}", [128, 8, 512], F32) for i in range(2)]
        grow = P.sb(st, "grow", [1, 2, 1024], F32)
        gbrow = P.sb(st, "gbrow", [1, 1024], F32)
        t_cv, t_adab, t_grow, t_gbrow = P.tiles(4, "ada")
        t_w = P.tiles(2, "adaw")
        dma(P, "sp", cv[:], I["cvec"].ap(), writes=[t_cv])
        dma(P, "sp", adab[:], I["adab_col"].ap()[:, layer, :], writes=[t_adab])
        dma(P, "sp", gbrow[:], I["adab_grow"].ap()[:, layer, :], writes=[t_gbrow])
        act(P, scv[:], cv[:], AF.Silu, [t_cv], [t_cv])
        wsrc = I["ada_w"].ap()[layer].rearrange("(dt p) c -> p dt c", p=128)
        pcol = k.ps[:, 0, 0:48].rearrange("p (j c) -> p j c", c=2)
        for jc in range(6):
            w = wch[jc % 2]
            tw = t_w[jc % 2]
            dma(P, "sp", w[:], wsrc[:, :, jc * 512:(jc + 1) * 512], writes=[tw])
            for jt in range(4):
                for dt in range(8):
                    mm(P, pcol[:, jc * 4 + jt, :], w[:, dt, jt * 128:(jt + 1) * 128], scv[:, dt, :],
                       dt == 0, dt == 7, [tw, t_cv], [k.pb[0]])
            if jc >= 4:
                for c in range(2):
                    for dt in range(8):
                        mm(P, k.ps[0:1, 1 + c, 0:512], scv[:, dt, c:c + 1], w[:, dt, :],
                           dt == 0, dt == 7, [tw, t_cv], [k.pb[1 + c]])
                    tt(P, "dve", grow[:, c, (jc - 4) * 512:(jc - 3) * 512], k.ps[0:1, 1 + c, 0:512],
                       gbrow[:, (jc - 4) * 512:(jc - 3) * 512], ALU.add, [k.pb[1 + c], t_gbrow], [t_grow])
        for c in range(2):
            tt(P, "dve", k.mcol[:, :, c], pcol[:, :, c], adab[:], ALU.add, [k.pb[0], t_adab], [k.t_mcol])
        ts(P, "dve", k.mcol[:, 8:16, :], k.mcol[:, 8:16, :], 1.0, ALU.add, [k.t_mcol], [k.t_mcol])
        dma(P, "sp", k.GROW.ap()[layer:layer + 1, :, :], grow[:], reads=[t_grow], writes=[k.t_growd])
    P.barrier()


def load_ln_gate(P, k, I, st, layer):
    k.gateB = P.sb(st, "gateB", [128, 2, 1024], F32)
    k.t_gateB = P.tile("gateB")
    k.lnB = P.sb(st, "lnB", [128, 2, 1024], F32)
    k.t_lnB = P.tile("lnB")
    dma(P, "sp", k.lnB[:, 0, :], I["lngB"].ap()[:, layer, :], writes=[k.t_lnB])
    dma(P, "sp", k.lnB[:, 1, :], I["lnbB"].ap()[:, layer, :], writes=[k.t_lnB])
    for c in range(2):
        dma(P, "sp", k.gateB[:, c, :], k.GROW.ap()[layer, c:c + 1, :].to_broadcast([128, 1024]), reads=[k.t_growd],
            writes=[k.t_gateB])


def ln_residual(P, k, W, psb, xt, t_xt, which, out_ap, t_out, affine=True):
    fx = ps2(k, psb)
    v, t_v = W["v"], W["t_v"]
    sts, mv, t_s = W["st"], W["mv"], W["t_s"]
    tt(P, "dve", v[:], fx, k.gateB[:, which, :], ALU.mult, [k.pb[psb], k.pb[psb + 1], k.t_gateB], [t_v])
    stt(P, "dve", v[:], xt, ALPHA, v[:], ALU.mult, ALU.add, [t_xt, t_v], [t_v])
    P.op("dve", lambda Eh: Eh.bn_stats(out=sts[:, 0:6], in_=v[:, 0:512]), reads=[t_v], writes=[t_s])
    P.op("dve", lambda Eh: Eh.bn_stats(out=sts[:, 6:12], in_=v[:, 512:1024]), reads=[t_v], writes=[t_s])
    P.op("dve", lambda Eh: Eh.bn_aggr(out=mv[:, 0:2], in_=sts[:, 0:12]), reads=[t_s], writes=[t_s])
    act(P, mv[:, 2:3], mv[:, 1:2], AF.Sqrt, [t_s], [t_s], scale=1.0, bias=W["eps"][:, 0:1])
    P.op("dve", lambda Eh: Eh.reciprocal(out=mv[:, 2:3], in_=mv[:, 2:3]), reads=[t_s], writes=[t_s])
    ts(P, "dve", mv[:, 3:4], mv[:, 0:1], mv[:, 2:3], ALU.mult, [t_s], [t_s], s2=-1.0, op1=ALU.mult)
    if not affine:
        act(P, out_ap, v[:], AF.Identity, [t_v, t_s], [t_out], scale=mv[:, 2:3], bias=mv[:, 3:4])
        return
    act(P, v[:], v[:], AF.Identity, [t_v, t_s], [t_v], scale=mv[:, 2:3], bias=mv[:, 3:4])
    tt(P, "pool", v[:], v[:], k.lnB[:, 0, :], ALU.mult, [t_v, k.t_lnB], [t_v])
    tt(P, "pool", out_ap, v[:], k.lnB[:, 1, :], ALU.add, [t_v, k.t_lnB], [t_out])


def ln_work(P, st, n=2):
    Ws = []
    for i in range(n):
        W = {}
        W["v"] = P.sb(st, f"lnv{i}", [128, 1024], F32)
        W["st"] = P.sb(st, f"lnst{i}", [128, 12], F32)
        W["mv"] = P.sb(st, f"lnmv{i}", [128, 4], F32)
        W["eps"] = P.sb(st, f"lneps{i}", [128, 1], F32)
        W["t_v"], W["t_s"], W["t_e"] = P.tiles(3, "lnw")
        mset(P, "dve", W["eps"][:], LN_EPS, [W["t_e"]])
        Ws.append(W)
    return Ws


def layer0(P, k, I, X1, CTX1, G0, slot=None, do_ctx=True, do_ada=True, ln_affine=True):
    sfx = "" if slot is None else str(slot)
    if do_ada:
        ada(P, k, I, 0)
    with ExitStack() as st:
        wout = P.sb(st, "wout", [128, 16, 1024], BF16)
        t_wout = P.tile("wout")
        for q in range(4):
            dma(P, "pool", wout[:, q * 4:(q + 1) * 4, :],
                I["conv_w_out"].ap().rearrange("(j p) d -> p j d", p=128)[:, q * 4:(q + 1) * 4, :], writes=[t_wout])
        with ExitStack() as st2:
            hxT = P.sb(st2, "hxT", [128, 8, NXH], BF16)
            hcT = P.sb(st2, "hcT", [128, 8, NCX], BF16)
            t_hx = P.tiles(8, "hx")
            t_hc = P.tile("hc")
            cw = P.sb(st2, "cw", [128, 16, 3], F32)
            edge = P.sb(st2, "edge", [128, 2], F32)
            t_cw = P.tile("cw")
            dma(P, "sp", cw[:], I["cw"].ap(), writes=[t_cw])
            dma(P, "sp", edge[:], I["edge" + sfx].ap(), writes=[t_cw])
            with ExitStack() as st3:
                xs = [P.sb(st3, f"xs{i}", [128, NXH], F32) for i in range(2)]
                xcs = P.sb(st3, "xcs", [128, 8, NCX], F32)
                t_xs = P.tiles(2, "xs")
                t_xcs = P.tile("xcs")
                if do_ctx:
                    dma(P, "sp", xcs[:], I["ctxT"].ap().rearrange("(dt p) t -> p dt t", p=128), writes=[t_xcs])
                for dt in range(8):
                    s = dt % 2
                    dma(P, "sp", xs[s][:], I["xT" + sfx].ap()[dt * 128:(dt + 1) * 128, :], writes=[t_xs[s]])
                    act(P, hxT[:, dt, :], xs[s][:], AF.Identity, [t_xs[s], k.t_mcol], [t_hx[dt]],
                        scale=k.mcol[:, 8 + dt, 0:1], bias=k.mcol[:, dt, 0:1])
                    if do_ctx:
                        act(P, hcT[:, dt, :], xcs[:, dt, :], AF.Identity, [t_xcs, k.t_mcol], [t_hc],
                            scale=k.mcol[:, 8 + dt, 1:2], bias=k.mcol[:, dt, 1:2])
            P.barrier()
            with ExitStack() as st3:
                wsl = [P.sb(st3, f"wsl{i}", [128, 8, 4, 128], BF16) for i in range(2)]
                t_wsl = P.tiles(2, "wsl")
                gst = [P.sb(st3, f"gst{i}", [128, NTX], BF16) for i in range(2)]
                t_gst = P.tiles(2, "gst")
                NW = 2
                cgs = [P.sb(st3, f"cgs{i}", [128, 640], F32) for i in range(NW)]
                uu = [P.sb(st3, f"uu{i}", [128, 640], F32) for i in range(NW)]
                yc = [P.sb(st3, f"yc{i}", [128, 512], F32) for i in range(NW)]
                sz = [P.sb(st3, f"sz{i}", [128, 512], F32) for i in range(NW)]
                t1 = [P.sb(st3, f"t1{i}", [128, 512], F32) for i in range(NW)]
                t_cgs, t_uu, t_yc, t_sz, t_t1 = (P.tiles(NW, "w") for _ in range(5))
                wsrc = I["conv_w_in"].ap().rearrange("(dt p) c -> p dt c", p=128)

                def load_w(j):
                    for part in range(4):
                        dma(P, "pool", wsl[j % 2][:, :, part, :],
                            wsrc[:, :, part * 2048 + j * 128: part * 2048 + (j + 1) * 128], writes=[t_wsl[j % 2]])
                load_w(0)
                it = 0
                for j in range(16):
                    if j + 1 < 16:
                        load_w(j + 1)
                    w = wsl[j % 2]
                    tw = t_wsl[j % 2]
                    vert = j >= 8
                    g = gst[j % 2]
                    tg = t_gst[j % 2]
                    for blk in range(9 if do_ctx else 8):
                        ws_ = it % NW
                        it += 1
                        isctx = blk == 8
                        n = NCX if isctx else 512
                        ext = vert and not isctx
                        def src(dt, c0, c1):
                            if isctx:
                                return hcT[:, dt, c0:c1], [t_hc]
                            return hxT[:, dt, c0:c1], [t_hx[dt]]
                        base = 0 if isctx else HALO + blk * 512
                        for part, bank in ((1, 0), (2, 2)):
                            if ext:
                                for dt in range(8):
                                    r, tr = src(dt, base - 64, base + 448)
                                    mm(P, k.ps[:, bank, :], w[:, dt, part, :], r, dt == 0, dt == 7, [tw] + tr, [k.pb[bank]])
                                for dt in range(8):
                                    r, tr = src(dt, base + 448, base + 576)
                                    mm(P, k.ps[:, bank + 1, 0:128], w[:, dt, part, :], r, dt == 0, dt == 7, [tw] + tr,
                                       [k.pb[bank + 1]])
                            else:
                                for dt in range(8):
                                    r, tr = src(dt, base, base + n)
                                    mm(P, k.ps[:, bank, 0:n], w[:, dt, part, :], r, dt == 0, dt == 7, [tw] + tr, [k.pb[bank]])
                        bz, bbg = 5 + 2 * (it % 2), 4 + 2 * (it % 2)
                        for part, bank in ((3, bz), (0, bbg)):
                            for dt in range(8):
                                r, tr = src(dt, base, base + n)
                                mm(P, k.ps[:, bank, 0:n], w[:, dt, part, :], r, dt == 0, dt == 7, [tw] + tr, [k.pb[bank]])
                        ne = 640 if ext else n
                        pcg = ps2(k, 0)[:, 0:ne]
                        pv = ps2(k, 2)[:, 0:ne]
                        cp(P, "act", cgs[ws_][:, 0:ne], pcg, [k.pb[0], k.pb[1]], [t_cgs[ws_]])
                        tt(P, "dve", uu[ws_][:, 0:ne], cgs[ws_][:, 0:ne], pv, ALU.mult, [t_cgs[ws_], k.pb[2], k.pb[3]], [t_uu[ws_]])
                        u_ = uu[ws_]
                        y_ = yc[ws_]
                        tu, ty = t_uu[ws_], t_yc[ws_]
                        if ext:
                            if blk == 0:
                                ts(P, "dve", u_[:, 0:64], u_[:, 0:64], edge[:, 0:1], ALU.mult, [tu, t_cw], [tu])
                            if blk == 7:
                                ts(P, "dve", u_[:, 576:640], u_[:, 576:640], edge[:, 1:2], ALU.mult, [tu, t_cw], [tu])
                            act(P, y_[:, 0:512], u_[:, 64:576], AF.Copy, [tu, t_cw], [ty], scale=cw[:, j, 1:2])
                            stt(P, "dve", y_[:, 0:512], u_[:, 0:512], cw[:, j, 0:1], y_[:, 0:512], ALU.mult, ALU.add, [tu, t_cw, ty], [ty])
                            stt(P, "dve", y_[:, 0:512], u_[:, 128:640], cw[:, j, 2:3], y_[:, 0:512], ALU.mult, ALU.add, [tu, t_cw, ty], [ty])
                        else:
                            rl = NCX if isctx else 64
                            u3 = u_[:, 0:n].rearrange("p (r c) -> p r c", c=rl)
                            y3 = y_[:, 0:n].rearrange("p (r c) -> p r c", c=rl)
                            act(P, y_[:, 0:n], u_[:, 0:n], AF.Copy, [tu, t_cw], [ty], scale=cw[:, j, 1:2])
                            stt(P, "dve", y3[:, :, 1:rl], u3[:, :, 0:rl - 1], cw[:, j, 0:1], y3[:, :, 1:rl], ALU.mult, ALU.add,
                                [tu, t_cw, ty], [ty])
                            stt(P, "dve", y3[:, :, 0:rl - 1], u3[:, :, 1:rl], cw[:, j, 2:3], y3[:, :, 0:rl - 1], ALU.mult, ALU.add,
                                [tu, t_cw, ty], [ty])
                        act(P, sz[ws_][:, 0:n], k.ps[:, bz, 0:n], AF.Silu, [k.pb[bz]], [t_sz[ws_]])
                        tt(P, "dve", t1[ws_][:, 0:n], k.ps[:, bbg, 0:n], y_[:, 0:n], ALU.mult, [k.pb[bbg], ty], [t_t1[ws_]])
                        c0 = NT if isctx else blk * 512
                        tt(P, "pool", g[:, c0:c0 + n], t1[ws_][:, 0:n], sz[ws_][:, 0:n], ALU.mult, [t_t1[ws_], t_sz[ws_]], [tg])
                    dma(P, "pool", G0.ap()[j * 128:(j + 1) * 128, 0:(NTX if do_ctx else NT)], g[:, 0:(NTX if do_ctx else NT)], reads=[tg])
        P.barrier()
        with ExitStack() as st2:
            gb = [P.sb(st2, f"gb{i}", [128, 16, 512], BF16) for i in range(2)]
            t_gb = P.tiles(2, "gb")
            xt = [P.sb(st2, f"xt{i}", [128, 1024], F32) for i in range(2)]
            t_xt = P.tiles(2, "xt")
            ot = [P.sb(st2, f"ot{i}", [128, 1024], F32) for i in range(2)]
            t_ot = P.tiles(2, "ot")
            Ws = ln_work(P, st2)
            load_ln_gate(P, k, I, st2, 0)
            gsrc = G0.ap().rearrange("(j p) t -> p j t", p=128)

            def load_g(b):
                n = NCX if b == 8 else 512
                dma(P, "sp", gb[b % 2][:, :, 0:n], gsrc[:, :, b * 512:b * 512 + n], writes=[t_gb[b % 2]])
            load_g(0)
            it = 0
            nblk0 = 9 if do_ctx else 8
            for b in range(nblk0):
                if b + 1 < nblk0:
                    load_g(b + 1)
                ntile = 2 if b == 8 else 4
                for tl in range(ntile):
                    s = it % 2
                    it += 1
                    psb = 2 * s
                    isctx = b == 8
                    if isctx:
                        xsrc = I["ctxtok"].ap()[tl * 128:(tl + 1) * 128, :]
                        dst = CTX1.ap()[tl * 128:(tl + 1) * 128, :]
                    else:
                        r0 = b * 512 + tl * 128
                        xsrc = I["xtok" + sfx].ap()[r0:r0 + 128, :]
                        dst = X1.ap()[r0:r0 + 128, :]
                    dma(P, "sp", xt[s][:], xsrc, writes=[t_xt[s]])
                    for h in range(2):
                        for j in range(16):
                            mm(P, k.ps[:, psb + h, :], gb[b % 2][:, j, tl * 128:(tl + 1) * 128], wout[:, j, h * 512:(h + 1) * 512],
                               j == 0, j == 15, [t_gb[b % 2], t_wout], [k.pb[psb + h]])
                    ln_residual(P, k, Ws[s], psb, xt[s][:], t_xt[s], 1 if isctx else 0, ot[s][:], t_ot[s], affine=ln_affine)
                    dma(P, "pool", dst, ot[s][:], reads=[t_ot[s]])
    P.barrier()


def bc_last(ap, shape):
    return ap.unsqueeze(len(shape) - 1).to_broadcast(shape)


def s5_tables(P, k, I, st):
    S = K()
    S.apr = P.sb(st, "apr", [128, 17, 128], F32)
    S.api = P.sb(st, "api", [128, 17, 128], F32)
    S.bbr = P.sb(st, "bbr", [128, 128, 16], F32)
    S.bbi = P.sb(st, "bbi", [128, 128, 16], F32)
    S.a4r = P.sb(st, "a4r", [128, 128], F32)
    S.a4i = P.sb(st, "a4i", [128, 128], F32)
    S.rpr = P.sb(st, "rpr", [128, 8, 128], F32)
    S.rpi = P.sb(st, "rpi", [128, 8, 128], F32)
    S.mgi = P.sb(st, "mgi", [128, 2], F32)
    S.sel = P.sb(st, "sel", [128, 4], F32)
    S.t_tab = P.tile("s5tab")
    tb = S.t_tab
    dma(P, "sp", S.mgi[:], I["mgi"].ap(), writes=[tb])
    dma(P, "sp", S.sel[:], I["sel"].ap(), writes=[tb])
    with ExitStack() as s2:
        lr = P.sb(s2, "lr", [128, 128], F32)
        li = P.sb(s2, "li", [128, 128], F32)
        ls = P.sb(s2, "ls", [128, 128], F32)
        bre = P.sb(s2, "bre", [128, 128, 16], F32)
        bim = P.sb(s2, "bim", [128, 128, 16], F32)
        w = [P.sb(s2, f"zw{i}", [128, 128], F32) for i in range(8)]
        big1 = P.sb(s2, "zb1", [128, 128, 16], F32)
        big2 = P.sb(s2, "zb2", [128, 128, 16], F32)
        tz = P.tile("zoh")
        dma(P, "sp", lr[:], I["lamre_A"].ap(), writes=[tz])
        dma(P, "sp", li[:], I["lamim_A"].ap(), writes=[tz])
        dma(P, "sp", ls[:], I["logstep_A"].ap(), writes=[tz])
        dma(P, "sp", bre[:], I["bre_A"].ap(), writes=[tz])
        dma(P, "sp", bim[:], I["bim_A"].ap(), writes=[tz])
        R, Wt = [tz, tb], [tz, tb]
        dtt, mag, th, cc, ss, t1, t2, t3 = w
        act(P, dtt[:], ls[:], AF.Exp, R, Wt)
        tt(P, "dve", mag[:], lr[:], dtt[:], ALU.mult, R, Wt)
        act(P, mag[:], mag[:], AF.Exp, R, Wt)
        tt(P, "dve", th[:], li[:], dtt[:], ALU.mult, R, Wt)
        act(P, ss[:], th[:], AF.Sin, R, Wt, scale=1.0 / 64.0)
        ts(P, "dve", t1[:], th[:], 1.0 / 64.0, ALU.mult, R, Wt, s2=PI / 2, op1=ALU.add)
        act(P, cc[:], t1[:], AF.Sin, R, Wt)
        for _ in range(6):
            tt(P, "dve", t1[:], cc[:], cc[:], ALU.mult, R, Wt)
            tt(P, "dve", t2[:], ss[:], ss[:], ALU.mult, R, Wt)
            tt(P, "dve", t3[:], cc[:], ss[:], ALU.mult, R, Wt)
            tt(P, "dve", cc[:], t1[:], t2[:], ALU.subtract, R, Wt)
            ts(P, "dve", ss[:], t3[:], 2.0, ALU.mult, R, Wt)
        ar, ai = S.apr[:, 1, :], S.api[:, 1, :]
        tt(P, "dve", ar, mag[:], cc[:], ALU.mult, R, Wt)
        tt(P, "dve", ai, mag[:], ss[:], ALU.mult, R, Wt)
        mset(P, "dve", S.apr[:, 0, :], 1.0, Wt)
        mset(P, "dve", S.api[:, 0, :], 0.0, Wt)
        tt(P, "dve", t1[:], lr[:], lr[:], ALU.mult, R, Wt)
        tt(P, "dve", t2[:], li[:], li[:], ALU.mult, R, Wt)
        tt(P, "dve", t1[:], t1[:], t2[:], ALU.add, R, Wt)
        P.op("dve", lambda Eh: Eh.reciprocal(out=t1[:], in_=t1[:]), reads=R, writes=Wt)
        ts(P, "dve", t2[:], ar, -1.0, ALU.add, R, Wt)
        tt(P, "dve", t3[:], t2[:], lr[:], ALU.mult, R, Wt)
        tt(P, "dve", cc[:], ai, li[:], ALU.mult, R, Wt)
        tt(P, "dve", t3[:], t3[:], cc[:], ALU.add, R, Wt)
        tt(P, "dve", t3[:], t3[:], t1[:], ALU.mult, R, Wt)
        tt(P, "dve", cc[:], ai, lr[:], ALU.mult, R, Wt)
        tt(P, "dve", ss[:], t2[:], li[:], ALU.mult, R, Wt)
        tt(P, "dve", cc[:], cc[:], ss[:], ALU.subtract, R, Wt)
        tt(P, "dve", cc[:], cc[:], t1[:], ALU.mult, R, Wt)
        frb = bc_last(t3[:], [128, 128, 16])
        fib = bc_last(cc[:], [128, 128, 16])
        tt(P, "dve", big1[:], bre[:], frb, ALU.mult, R, Wt)
        tt(P, "dve", big2[:], bim[:], fib, ALU.mult, R, Wt)
        tt(P, "dve", S.bbr[:], big1[:], big2[:], ALU.subtract, R, Wt)
        tt(P, "dve", big1[:], bim[:], frb, ALU.mult, R, Wt)
        tt(P, "dve", big2[:], bre[:], fib, ALU.mult, R, Wt)
        tt(P, "dve", S.bbi[:], big1[:], big2[:], ALU.add, R, Wt)
        for kk in range(1, 16):
            pr, pi_ = S.apr[:, kk, :], S.api[:, kk, :]
            tt(P, "dve", t1[:], pr, ar, ALU.mult, R, Wt)
            tt(P, "dve", t2[:], pi_, ai, ALU.mult, R, Wt)
            tt(P, "dve", S.apr[:, kk + 1, :], t1[:], t2[:], ALU.subtract, R, Wt)
            tt(P, "dve", t1[:], pr, ai, ALU.mult, R, Wt)
            tt(P, "dve", t2[:], pi_, ar, ALU.mult, R, Wt)
            tt(P, "dve", S.api[:, kk + 1, :], t1[:], t2[:], ALU.add, R, Wt)
        cp(P, "dve", S.a4r[:], S.apr[:, 16, :], R, Wt)
        cp(P, "dve", S.a4i[:], S.api[:, 16, :], R, Wt)
        for l_ in range(8):
            cp(P, "dve", S.rpr[:, l_, :], S.a4r[:], R, Wt)
            cp(P, "dve", S.rpi[:, l_, :], S.a4i[:], R, Wt)
            tt(P, "dve", t1[:], S.a4r[:], S.a4r[:], ALU.mult, R, Wt)
            tt(P, "dve", t2[:], S.a4i[:], S.a4i[:], ALU.mult, R, Wt)
            tt(P, "dve", t3[:], S.a4r[:], S.a4i[:], ALU.mult, R, Wt)
            tt(P, "dve", S.a4r[:], t1[:], t2[:], ALU.subtract, R, Wt)
            ts(P, "dve", S.a4i[:], t3[:], 2.0, ALU.mult, R, Wt)
    P.barrier()
    return S


def l1_inproj(P, k, I, X1src, CTX1src, UD, HXD, do_ctx=True, store_hx=True, fold=False):
    with ExitStack() as st:
        wu = P.sb(st, "wu", [128, 8, E2], BF16)
        t_wu = P.tile("wu")
        wsrc = I["ssm_w_in"].ap().rearrange("(dt p) c -> p dt c", p=128)
        for q in range(4):
            dma(P, "pool", wu[:, :, q * 512:(q + 1) * 512], wsrc[:, :, q * 512:(q + 1) * 512], writes=[t_wu])
        xt = [P.sb(st, f"x1t{i}", [128, 1024], F32) for i in range(2)]
        t_xt = P.tiles(2, "x1t")
        hxb = [P.sb(st, f"hxb{i}", [128, 8, 512], BF16) for i in range(2)]
        t_hxb = P.tiles(2, "hxb")
        ub = [P.sb(st, f"ub{i}", [128, 16, 512], BF16) for i in range(2)]
        t_ub = P.tiles(2, "ub")
        usrc = UD.ap().rearrange("(j p) t -> p j t", p=128)
        nblk = 9 if do_ctx else 8
        cnt = [0]

        def prep(b):
            isctx = b == 8
            n = NCX if isctx else 512
            hb, thb = hxb[b % 2], t_hxb[b % 2]
            for tl in range(n // 128):
                s = cnt[0] % 2
                cnt[0] += 1
                src = CTX1src.ap()[tl * 128:(tl + 1) * 128, :] if isctx else X1src.ap()[b * 512 + tl * 128: b * 512 + (tl + 1) * 128, :]
                dma(P, "sp", xt[s][:], src, writes=[t_xt[s]])
                for h in range(2):
                    bank = 2 * s + h
                    for q in range(4):
                        dt = h * 4 + q
                        P.op("pe", lambda Eh, o=k.ps[:, bank, q * 128:(q + 1) * 128], i_=xt[s][:, dt * 128:(dt + 1) * 128]:
                             Eh.transpose(o, i_, k.identf[:]), reads=[t_xt[s], k.t_ident], writes=[k.pb[bank]])
                    for q in range(4):
                        dt = h * 4 + q
                        if fold:
                            act(P, hb[:, dt, tl * 128:(tl + 1) * 128], k.ps[:, bank, q * 128:(q + 1) * 128], AF.Identity,
                                [k.pb[bank], k.t_mcol2], [thb], scale=k.mcol2[:, 8 + dt:9 + dt], bias=k.mcol2[:, dt:dt + 1])
                        else:
                            act(P, hb[:, dt, tl * 128:(tl + 1) * 128], k.ps[:, bank, q * 128:(q + 1) * 128], AF.Identity,
                                [k.pb[bank], k.t_mcol], [thb], scale=k.mcol[:, 8 + dt, (1 if isctx else 0):(2 if isctx else 1)],
                                bias=k.mcol[:, dt, (1 if isctx else 0):(2 if isctx else 1)])
            c0 = NT if isctx else b * 512
            if store_hx:
                dma(P, "act", HXD.ap()[:, :, c0:c0 + n], hb[:, :, 0:n], reads=[thb])

        prep(0)
        for b in range(nblk):
            if b + 1 < nblk:
                prep(b + 1)
            isctx = b == 8
            n = NCX if isctx else 512
            hb, thb = hxb[b % 2], t_hxb[b % 2]
            c0 = NT if isctx else b * 512
            u_, tu = ub[b % 2], t_ub[b % 2]
            for j in range(16):
                bank = 4 + (j % 4)
                for dt in range(8):
                    mm(P, k.ps[:, bank, 0:n], wu[:, dt, j * 128:(j + 1) * 128], hb[:, dt, 0:n], dt == 0, dt == 7,
                       [t_wu, thb], [k.pb[bank]])
                cp(P, "dve", u_[:, j, 0:n], k.ps[:, bank, 0:n], [k.pb[bank]], [tu])
            dma(P, "pool", usrc[:, :, c0:c0 + n], u_[:, :, 0:n], reads=[tu])
    P.barrier()


def build_cmp(P, S, W, src_r, src_i, k0, nk, rM0, sign_neg_part1):
    shp = [128, nk, 4, 16]
    A_r = bc_last(S.apr[:, k0:k0 + nk, rM0:rM0 + 4], shp)
    A_i = bc_last(S.api[:, k0:k0 + nk, rM0:rM0 + 4], shp)
    B_r = src_r[:, rM0:rM0 + 4, :].unsqueeze(1).to_broadcast(shp)
    B_i = src_i[:, rM0:rM0 + 4, :].unsqueeze(1).to_broadcast(shp)
    cmp_, t_cmp = W["cmp"], W["t_cmp"]
    ta, tb_ = W["tmp1"], W["tmp2"]
    R = [S.t_tab, W["t_tmp"]]
    Wr = [W["t_tmp"]]
    tt(P, "dve", ta[:, 0:nk], B_r, A_r, ALU.mult, R, Wr)
    tt(P, "pool", tb_[:, 0:nk], B_i, A_i, ALU.mult, [S.t_tab, W["t_tmp2"]], [W["t_tmp2"]])
    tt(P, "dve", cmp_[:, 0:nk, 0, :, :], ta[:, 0:nk], tb_[:, 0:nk], ALU.subtract, [W["t_tmp"], W["t_tmp2"]], [t_cmp])
    tt(P, "dve", ta[:, 0:nk], B_i, A_r, ALU.mult, R + [W["t_tmp2"]], Wr)
    tt(P, "pool", tb_[:, 0:nk], B_r, A_i, ALU.mult, [S.t_tab, W["t_tmp2"], W["t_tmp"]], [W["t_tmp2"]])
    if sign_neg_part1:
        stt(P, "dve", cmp_[:, 0:nk, 1, :, :], ta[:, 0:nk], -1.0, tb_[:, 0:nk], ALU.mult, ALU.subtract,
            [W["t_tmp"], W["t_tmp2"]], [t_cmp])
    else:
        tt(P, "dve", cmp_[:, 0:nk, 1, :, :], ta[:, 0:nk], tb_[:, 0:nk], ALU.add, [W["t_tmp"], W["t_tmp2"]], [t_cmp])


def pad_cmp(P, S, W, dst, t_dst, nk):
    cmp_, t_cmp = W["cmp"], W["t_cmp"]
    i = 0
    for gi in range(2):
        for m in range(4):
            c0 = (2 * m + gi) * 16
            if gi == 0:
                ts(P, "dve", dst[:, 0:nk, :, m, c0:c0 + 16], cmp_[:, 0:nk, :, m, :], S.mgi[:, gi:gi + 1], ALU.mult,
                   [t_cmp, S.t_tab], [t_dst])
            else:
                act(P, dst[:, 0:nk, :, m, c0:c0 + 16], cmp_[:, 0:nk, :, m, :], AF.Copy, [t_cmp, S.t_tab], [t_dst],
                    scale=S.mgi[:, gi:gi + 1])
            i += 1


def table_work(P, st):
    W = {}
    W["cmp"] = P.sb(st, "cmp", [128, 17, 2, 4, 16], F32)
    W["tmp1"] = P.sb(st, "tmp1", [128, 17, 4, 16], F32)
    W["tmp2"] = P.sb(st, "tmp2", [128, 17, 4, 16], F32)
    W["t_cmp"], W["t_tmp"], W["t_tmp2"] = P.tiles(3, "tw")
    return W


def l1_summaries(P, k, I, S, UD, SD, do_ctx=True):
    with ExitStack() as st:
        W = table_work(P, st)
        bk = P.sb(st, "bkpad", [128, 16, 2, 4, 128], BF16)
        t_bk = P.tile("bk")
        mset(P, "pool", bk[:].rearrange("p a b c d -> p (a b c d)"), 0.0, [t_bk])
        wrt = [P.sb(st, f"wrt{i}", [128, 16, 2, 128], BF16) for i in range(2)]
        t_wrt = P.tiles(2, "wrt")
        uj = [P.sb(st, f"uj{i}", [128, NTX], BF16) for i in range(2)]
        t_uj = P.tiles(2, "uj")
        ssb = [P.sb(st, f"ssb{i}", [128, 4, 2, 256], F32) for i in range(2)]
        t_ssb = P.tiles(2, "ssb")
        ssc = [P.sb(st, f"ssc{i}", [128, 4, 2, 16], F32) for i in range(2)]
        t_ssc = P.tiles(2, "ssc")

        nld = NTX if do_ctx else NT

        def load_u(j):
            dma(P, "sp", uj[j % 2][:, 0:nld], UD.ap()[j * 128:(j + 1) * 128, 0:nld], writes=[t_uj[j % 2]])
        load_u(0)
        it = 0
        for j in range(16):
            if j + 1 < 16:
                load_u(j + 1)
            u_, tu = uj[j % 2], t_uj[j % 2]
            u3 = u_[:, 0:NT].rearrange("p (c t) -> p c t", t=TCH)
            u3c = u_[:, NT:NTX].rearrange("p (c t) -> p c t", t=TCH)
            for r in range(2):
                s = it % 2
                it += 1
                rM0 = r * 64 + 4 * j
                build_cmp(P, S, W, S.bbr, S.bbi, 0, 16, rM0, False)
                pad_cmp(P, S, W, bk, t_bk, 16)
                wr, twr = wrt[s], t_wrt[s]
                wflat = wr[:].rearrange("p a b c -> p (a b c)")
                for rd in range(8):
                    bank = 4 + (rd % 2)
                    for q in range(4):
                        kp = rd * 4 + q
                        kk, part = divmod(kp, 2)
                        for m in range(4):
                            mm(P, k.ps[:, bank, q * 128:(q + 1) * 128], bk[:, kk, part, m, :], k.identb[:], m == 0, m == 3,
                               [t_bk, k.t_ident], [k.pb[bank]])
                    cp(P, "act", wflat[:, rd * 512:(rd + 1) * 512], k.ps[:, bank, :], [k.pb[bank]], [twr])
                for (uv, nch, dst, tdst, dcol) in ((u3, NCHL, ssb[s], t_ssb[s], 0), (u3c, NCHX, ssc[s], t_ssc[s], NCHL))[0:(2 if do_ctx else 1)]:
                    for part in range(2):
                        for kk in range(TCH):
                            sidx = (TCH - 1 - kk) if r == 0 else kk
                            for m in range(4):
                                mm(P, k.ps[:, m, part * 256: part * 256 + nch], wr[32 * m:32 * m + 32, kk, part, :],
                                   uv[32 * m:32 * m + 32, :, sidx], kk == 0, kk == TCH - 1, [twr, tu], [k.pb[m]], tp=(32 * m, 0))
                    for m in range(4):
                        cp(P, "dve" if m % 2 == 0 else "act", dst[:, m, :, :],
                           k.ps[:, m, :].rearrange("p (a b) -> p a b", a=2)[:, :, 0:nch], [k.pb[m]], [tdst])
                    dma(P, "sp", SD[r].ap()[:, 4 * j:4 * j + 4, :, dcol:dcol + nch], dst[:], reads=[tdst])
    P.barrier()


def l1_carry(P, k, S, SD, HD, init, final, do_ctx, do_lat, store):
    CBK = 32
    eng = "dve"
    with ExitStack() as st:
        sblk = [[P.sb(st, f"sblk{r}{i}", [128, 64, 2, CBK], F32) for i in range(2)] for r in range(2)]
        t_sblk = [P.tiles(2, "sblk") for r in range(2)]
        hseq = [P.sb(st, f"hseq{r}", [128, 64, 2, CBK + 1], F32) for r in range(2)]
        t_hseq = P.tiles(2, "hseq")
        hbf = [P.sb(st, f"hbf{r}", [128, 64, 2, CBK], BF16) for r in range(2)]
        t_hbf = P.tiles(2, "hbf")
        tmp = [[P.sb(st, f"ctmp{r}{i}", [128, 64], F32) for i in range(4)] for r in range(2)]
        t_tmp = [P.tiles(4, "ctmp") for r in range(2)]
        blocks = [[], []]
        for r in range(2):
            if do_ctx:
                blocks[r].append((NCHL, NCHX))
            if do_lat:
                lat = [(c0, CBK) for c0 in range(0, NCHL, CBK)]
                blocks[r] += lat if r == 0 else lat[::-1]
        Rr = [S.apr[:, 16, r * 64:(r + 1) * 64] for r in range(2)]
        Ri = [S.api[:, 16, r * 64:(r + 1) * 64] for r in range(2)]

        def load_blk(r, bi):
            c0, nb = blocks[r][bi]
            dma(P, "sp", sblk[r][bi % 2][:, :, :, 0:nb], SD[r].ap()[:, :, :, c0:c0 + nb], writes=[t_sblk[r][bi % 2]])
        nblk = len(blocks[0])
        for r in range(2):
            load_blk(r, 0)
        for bi in range(nblk):
            nb = blocks[0][bi][1]
            for r in range(2):
                if bi + 1 < nblk:
                    load_blk(r, bi + 1)
                hs, ths = hseq[r], t_hseq[r]
                e_in = 0 if r == 0 else nb
                if bi == 0:
                    if init[r] is None:
                        mset(P, eng, hs[:, :, :, e_in], 0.0, [ths])
                    else:
                        cp(P, eng, hs[:, :, 0, e_in], init[r][0], [init[r][2]], [ths])
                        cp(P, eng, hs[:, :, 1, e_in], init[r][1], [init[r][2]], [ths])
                else:
                    pnb = blocks[r][bi - 1][1]
                    e_prev = pnb if r == 0 else 0
                    if e_prev != e_in:
                        cp(P, eng, hs[:, :, :, e_in], hs[:, :, :, e_prev], [ths], [ths])
            for i in range(nb):
                cc = [i, nb - 1 - i]
                ei = [cc[0], cc[1] + 1]
                eo = [cc[0] + 1, cc[1]]
                for r in range(2):
                    hs, ths, tm, ttm = hseq[r], t_hseq[r], tmp[r], t_tmp[r]
                    hr, hi = hs[:, :, 0, ei[r]], hs[:, :, 1, ei[r]]
                    tt(P, eng, tm[0][:], Rr[r], hr, ALU.mult, [S.t_tab, ths], [ttm[0]])
                    tt(P, eng, tm[1][:], Ri[r], hi, ALU.mult, [S.t_tab, ths], [ttm[1]])
                    tt(P, eng, tm[2][:], Ri[r], hr, ALU.mult, [S.t_tab, ths], [ttm[2]])
                    tt(P, eng, tm[3][:], Rr[r], hi, ALU.mult, [S.t_tab, ths], [ttm[3]])
                for r in range(2):
                    tm, ttm = tmp[r], t_tmp[r]
                    tt(P, eng, tm[0][:], tm[0][:], tm[1][:], ALU.subtract, [ttm[0], ttm[1]], [ttm[0]])
                    tt(P, eng, tm[2][:], tm[2][:], tm[3][:], ALU.add, [ttm[2], ttm[3]], [ttm[2]])
                for r in range(2):
                    hs, ths, tm, ttm = hseq[r], t_hseq[r], tmp[r], t_tmp[r]
                    sb_, tsb = sblk[r][bi % 2], t_sblk[r][bi % 2]
                    tt(P, eng, hs[:, :, 0, eo[r]], tm[0][:], sb_[:, :, 0, cc[r]], ALU.add, [ttm[0], tsb], [ths])
                    tt(P, eng, hs[:, :, 1, eo[r]], tm[2][:], sb_[:, :, 1, cc[r]], ALU.add, [ttm[2], tsb], [ths])
            for r in range(2):
                hs, ths = hseq[r], t_hseq[r]
                c0 = blocks[r][bi][0]
                if store and c0 < NCHL:
                    lo = 0 if r == 0 else 1
                    cp(P, "act", hbf[r][:, :, :, 0:nb], hs[:, :, :, lo:lo + nb], [ths], [t_hbf[r]])
                    dma(P, "act", HD[r].ap()[:, :, :, c0:c0 + nb], hbf[r][:, :, :, 0:nb], reads=[t_hbf[r]])
                e_out = nb if r == 0 else 0
                if final[r] is not None and bi == nblk - 1:
                    cp(P, eng, final[r][0], hs[:, :, 0, e_out], [ths], [final[r][2]])
                    cp(P, eng, final[r][1], hs[:, :, 1, e_out], [ths], [final[r][2]])
    P.barrier()


def l1_summaries_multi(P, k, I, S, UDs, SDs):
    with ExitStack() as st:
        W = table_work(P, st)
        bk = P.sb(st, "bkpad", [128, 16, 2, 4, 128], BF16)
        t_bk = P.tile("bk")
        mset(P, "pool", bk[:].rearrange("p a b c d -> p (a b c d)"), 0.0, [t_bk])
        wrt = [[P.sb(st, f"wrt{r}{i}", [128, 16, 2, 128], BF16) for i in range(2)] for r in range(2)]
        t_wrt = [P.tiles(2, "wrt") for r in range(2)]
        uj = [P.sb(st, f"uj{i}", [128, NTX], BF16) for i in range(2)]
        t_uj = P.tiles(2, "uj")
        ssb = [P.sb(st, f"ssb{i}", [128, 4, 2, 256], F32) for i in range(2)]
        t_ssb = P.tiles(2, "ssb")
        ssc = [P.sb(st, f"ssc{i}", [128, 4, 2, 16], F32) for i in range(2)]
        t_ssc = P.tiles(2, "ssc")
        seq = [(j, sl) for j in range(16) for sl in range(4)]

        def load_u(i):
            j, sl = seq[i]
            nld = NTX if sl == 0 else NT
            dma(P, "sp", uj[i % 2][:, 0:nld], UDs[sl].ap()[j * 128:(j + 1) * 128, 0:nld], writes=[t_uj[i % 2]])
        load_u(0)
        it = 0
        for j in range(16):
            for r in range(2):
                rM0 = r * 64 + 4 * j
                build_cmp(P, S, W, S.bbr, S.bbi, 0, 16, rM0, False)
                pad_cmp(P, S, W, bk, t_bk, 16)
                wr, twr = wrt[r][j % 2], t_wrt[r][j % 2]
                wflat = wr[:].rearrange("p a b c -> p (a b c)")
                for rd in range(8):
                    bank = 4 + (rd % 2)
                    for q in range(4):
                        kp = rd * 4 + q
                        kk, part = divmod(kp, 2)
                        for m in range(4):
                            mm(P, k.ps[:, bank, q * 128:(q + 1) * 128], bk[:, kk, part, m, :], k.identb[:], m == 0, m == 3,
                               [t_bk, k.t_ident], [k.pb[bank]])
                    cp(P, "act", wflat[:, rd * 512:(rd + 1) * 512], k.ps[:, bank, :], [k.pb[bank]], [twr])
            for sl in range(4):
                i = j * 4 + sl
                if i + 1 < len(seq):
                    load_u(i + 1)
                u_, tu = uj[i % 2], t_uj[i % 2]
                u3 = u_[:, 0:NT].rearrange("p (c t) -> p c t", t=TCH)
                u3c = u_[:, NT:NTX].rearrange("p (c t) -> p c t", t=TCH)
                for r in range(2):
                    s_ = it % 2
                    it += 1
                    wr, twr = wrt[r][j % 2], t_wrt[r][j % 2]
                    jobs = ((u3, NCHL, ssb[s_], t_ssb[s_], 0), (u3c, NCHX, ssc[s_], t_ssc[s_], NCHL))
                    for (uv, nch, dst, tdst, dcol) in jobs[0:(2 if sl == 0 else 1)]:
                        for part in range(2):
                            for kk in range(TCH):
                                sidx = (TCH - 1 - kk) if r == 0 else kk
                                for m in range(4):
                                    mm(P, k.ps[:, m, part * 256: part * 256 + nch], wr[32 * m:32 * m + 32, kk, part, :],
                                       uv[32 * m:32 * m + 32, :, sidx], kk == 0, kk == TCH - 1, [twr, tu], [k.pb[m]], tp=(32 * m, 0))
                        for m in range(4):
                            cp(P, "dve" if m % 2 == 0 else "act", dst[:, m, :, :],
                               k.ps[:, m, :].rearrange("p (a b) -> p a b", a=2)[:, :, 0:nch], [k.pb[m]], [tdst])
                        dma(P, "sp", SDs[sl][r].ap()[:, 4 * j:4 * j + 4, :, dcol:dcol + nch], dst[:], reads=[tdst])
    P.barrier()


def l1_carry_foreign(P, k, S, SDs, zs, t_zs):
    CBK = 16
    eng = "dve"
    NS = 3
    with ExitStack() as st:
        sblk = [P.sb(st, f"fsblk{r}", [128, NS, 64, 2, CBK], F32) for r in range(2)]
        t_sblk = P.tiles(2, "fsblk")
        hseq = [P.sb(st, f"fhseq{r}", [128, NS, 64, 2, 2], F32) for r in range(2)]
        t_hseq = P.tiles(2, "fhseq")
        tmp = [[P.sb(st, f"fctmp{r}{i}", [128, NS, 64], F32) for i in range(4)] for r in range(2)]
        t_tmp = [P.tiles(4, "fctmp") for r in range(2)]
        shp = [128, NS, 64]
        Rr = [S.apr[:, 16, r * 64:(r + 1) * 64].unsqueeze(1).to_broadcast(shp) for r in range(2)]
        Ri = [S.api[:, 16, r * 64:(r + 1) * 64].unsqueeze(1).to_broadcast(shp) for r in range(2)]
        lat = [c0 for c0 in range(0, NCHL, CBK)]
        blocks = [lat, lat[::-1]]
        for r in range(2):
            mset(P, eng, hseq[r][:, :, :, :, 0], 0.0, [t_hseq[r]])
        cur = 0
        for bi in range(len(lat)):
            for r in range(2):
                c0 = blocks[r][bi]
                for sl in range(NS):
                    dma(P, "sp", sblk[r][:, sl], SDs[sl + 1][r].ap()[:, :, :, c0:c0 + CBK], writes=[t_sblk[r]])
            for i in range(CBK):
                cc = [i, CBK - 1 - i]
                nxt = 1 - cur
                for r in range(2):
                    hs, ths, tm, ttm = hseq[r], t_hseq[r], tmp[r], t_tmp[r]
                    hr, hi = hs[:, :, :, 0, cur], hs[:, :, :, 1, cur]
                    tt(P, eng, tm[0][:], Rr[r], hr, ALU.mult, [S.t_tab, ths], [ttm[0]])
                    tt(P, eng, tm[1][:], Ri[r], hi, ALU.mult, [S.t_tab, ths], [ttm[1]])
                    tt(P, eng, tm[2][:], Ri[r], hr, ALU.mult, [S.t_tab, ths], [ttm[2]])
                    tt(P, eng, tm[3][:], Rr[r], hi, ALU.mult, [S.t_tab, ths], [ttm[3]])
                for r in range(2):
                    tm, ttm = tmp[r], t_tmp[r]
                    tt(P, eng, tm[0][:], tm[0][:], tm[1][:], ALU.subtract, [ttm[0], ttm[1]], [ttm[0]])
                    tt(P, eng, tm[2][:], tm[2][:], tm[3][:], ALU.add, [ttm[2], ttm[3]], [ttm[2]])
                for r in range(2):
                    hs, ths, tm, ttm = hseq[r], t_hseq[r], tmp[r], t_tmp[r]
                    tt(P, eng, hs[:, :, :, 0, nxt], tm[0][:], sblk[r][:, :, :, 0, cc[r]], ALU.add, [ttm[0], t_sblk[r]], [ths])
                    tt(P, eng, hs[:, :, :, 1, nxt], tm[2][:], sblk[r][:, :, :, 1, cc[r]], ALU.add, [ttm[2], t_sblk[r]], [ths])
                cur = nxt
        for r in range(2):
            for part in range(2):
                cp(P, eng, zs[:, 1:4, r, part, :], hseq[r][:, :, :, part, cur], [t_hseq[r]], [t_zs])
    P.barrier()


def l1_zreduce_foreign(P, k, S, SDs, zs, t_zs):
    NS = 3
    with ExitStack() as st:
        XA = [P.sb(st, f"zxa{r}", [128, NS, 4, 2, 256], F32) for r in range(2)]
        XB = [P.sb(st, f"zxb{r}", [128, NS, 4, 2, 128], F32) for r in range(2)]
        tm = [[P.sb(st, f"zt{r}{i}", [128, NS, 4, 128], F32) for i in range(3)] for r in range(2)]
        t_xa, t_xb = P.tiles(2, "zxa"), P.tiles(2, "zxb")
        t_tm = [P.tiles(3, "zt") for r in range(2)]
        eng = "dve"
        for j in range(16):
            for r in range(2):
                for sl in range(NS):
                    dma(P, "sp", XA[r][:, sl], SDs[sl + 1][r].ap()[:, 4 * j:4 * j + 4, :, 0:NCHL], writes=[t_xa[r]])
            for l_ in range(8):
                n = 128 >> l_
                for r in range(2):
                    src, tsrc = (XA[r], t_xa[r]) if l_ % 2 == 0 else (XB[r], t_xb[r])
                    dst, tdst = (XB[r], t_xb[r]) if l_ % 2 == 0 else (XA[r], t_xa[r])
                    shp = [128, NS, 4, n]
                    c0 = r * 64 + 4 * j
                    Rr = S.rpr[:, l_, c0:c0 + 4].unsqueeze(1).unsqueeze(3).to_broadcast(shp)
                    Ri = S.rpi[:, l_, c0:c0 + 4].unsqueeze(1).unsqueeze(3).to_broadcast(shp)
                    ia, ib = (0, 1) if r == 0 else (1, 0)
                    Ar, Ai = src[:, :, :, 0, ia:2 * n:2], src[:, :, :, 1, ia:2 * n:2]
                    Br, Bi = src[:, :, :, 0, ib:2 * n:2], src[:, :, :, 1, ib:2 * n:2]
                    t1, t2, t3 = (tm[r][i][:, :, :, 0:n] for i in range(3))
                    tt(P, eng, t1, Ar, Rr, ALU.mult, [tsrc, S.t_tab], [t_tm[r][0]])
                    tt(P, eng, t2, Ai, Ri, ALU.mult, [tsrc, S.t_tab], [t_tm[r][1]])
                    tt(P, eng, t3, Ai, Rr, ALU.mult, [tsrc, S.t_tab], [t_tm[r][2]])
                for r in range(2):
                    src, tsrc = (XA[r], t_xa[r]) if l_ % 2 == 0 else (XB[r], t_xb[r])
                    shp = [128, NS, 4, n]
                    c0 = r * 64 + 4 * j
                    Ri = S.rpi[:, l_, c0:c0 + 4].unsqueeze(1).unsqueeze(3).to_broadcast(shp)
                    ia = 0 if r == 0 else 1
                    Ar = src[:, :, :, 0, ia:2 * n:2]
                    t1, t2, t3 = (tm[r][i][:, :, :, 0:n] for i in range(3))
                    tt(P, eng, t1, t1, t2, ALU.subtract, [t_tm[r][0], t_tm[r][1]], [t_tm[r][0]])
                    tt(P, eng, t2, Ar, Ri, ALU.mult, [tsrc, S.t_tab, t_tm[r][1]], [t_tm[r][1]])
                for r in range(2):
                    src, tsrc = (XA[r], t_xa[r]) if l_ % 2 == 0 else (XB[r], t_xb[r])
                    dst, tdst = (XB[r], t_xb[r]) if l_ % 2 == 0 else (XA[r], t_xa[r])
                    ib = 1 if r == 0 else 0
                    Br, Bi = src[:, :, :, 0, ib:2 * n:2], src[:, :, :, 1, ib:2 * n:2]
                    t1, t2, t3 = (tm[r][i][:, :, :, 0:n] for i in range(3))
                    tt(P, eng, t3, t3, t2, ALU.add, [t_tm[r][2], t_tm[r][1]], [t_tm[r][2]])
                    tt(P, eng, dst[:, :, :, 0, 0:n], t1, Br, ALU.add, [t_tm[r][0], tsrc], [tdst])
                    tt(P, eng, dst[:, :, :, 1, 0:n], t3, Bi, ALU.add, [t_tm[r][2], tsrc], [tdst])
            for r in range(2):
                for part in range(2):
                    cp(P, "act", zs[:, 1:4, r, part, 4 * j:4 * j + 4], XA[r][:, :, :, part, 0], [t_xa[r]], [t_zs])
    P.barrier()


def l1_summaries_zr(P, k, I, S, UDs, SD, zs, t_zs):
    NS = 3
    with ExitStack() as st:
        W = table_work(P, st)
        bk = P.sb(st, "bkpad", [128, 16, 2, 4, 128], BF16)
        t_bk = P.tile("bk")
        mset(P, "pool", bk[:].rearrange("p a b c d -> p (a b c d)"), 0.0, [t_bk])
        wrt = [P.sb(st, f"wrt{r}", [128, 16, 2, 128], BF16) for r in range(2)]
        t_wrt = P.tiles(2, "wrt")
        uj = [P.sb(st, f"uj{i}", [128, NTX], BF16) for i in range(2)]
        t_uj = P.tiles(2, "uj")
        ssb = P.sb(st, "ssb", [128, 4, 2, 256], F32)
        t_ssb = P.tile("ssb")
        ssc = P.sb(st, "ssc", [128, 4, 2, 16], F32)
        t_ssc = P.tile("ssc")
        XB = [P.sb(st, f"zxb{r}", [128, NS, 4, 2, 128], F32) for r in range(2)]
        XC = [P.sb(st, f"zxc{r}", [128, NS, 4, 2, 64], F32) for r in range(2)]
        t_xb, t_xc = P.tiles(2, "zxb"), P.tiles(2, "zxc")
        tm0 = [[P.sb(st, f"zl0{a}{i}", [128, 4, 128], F32) for i in range(3)] for a in range(2)]
        t_tm0 = [P.tiles(3, "zl0") for a in range(2)]
        _tm1 = [P.sb(st, f"zl1{i}", [128, NS, 4, 64], F32) for i in range(3)]
        _t_tm1 = P.tiles(3, "zl1")
        tm1 = [_tm1, _tm1]
        t_tm1 = [_t_tm1, _t_tm1]
        seq = [(j, sl) for j in range(16) for sl in range(4)]

        def load_u(i):
            j, sl = seq[i]
            nld = NTX if sl == 0 else NT
            dma(P, "sp", uj[i % 2][:, 0:nld], UDs[sl].ap()[j * 128:(j + 1) * 128, 0:nld], writes=[t_uj[i % 2]])

        def cmuladd(eng, dst_r, dst_i, Ar, Ai, Br, Bi, Rr, Ri, t, tt_, rd):
            tt(P, eng, t[0], Ar, Rr, ALU.mult, rd + [S.t_tab], [tt_[0]])
            tt(P, eng, t[1], Ai, Ri, ALU.mult, rd + [S.t_tab], [tt_[1]])
            tt(P, eng, t[2], Ai, Rr, ALU.mult, rd + [S.t_tab], [tt_[2]])
            tt(P, eng, t[0], t[0], t[1], ALU.subtract, [tt_[0], tt_[1]], [tt_[0]])
            tt(P, eng, t[1], Ar, Ri, ALU.mult, rd + [S.t_tab, tt_[1]], [tt_[1]])
            tt(P, eng, t[2], t[2], t[1], ALU.add, [tt_[2], tt_[1]], [tt_[2]])
            return t[0], t[2]

        load_u(0)
        it = 0
        for j in range(16):
            for r in range(2):
                rM0 = r * 64 + 4 * j
                build_cmp(P, S, W, S.bbr, S.bbi, 0, 16, rM0, False)
                pad_cmp(P, S, W, bk, t_bk, 16)
                wflat = wrt[r][:].rearrange("p a b c -> p (a b c)")
                for rd_ in range(8):
                    bank = 4 + (rd_ % 2)
                    for q in range(4):
                        kp = rd_ * 4 + q
                        kk, part = divmod(kp, 2)
                        for m in range(4):
                            mm(P, k.ps[:, bank, q * 128:(q + 1) * 128], bk[:, kk, part, m, :], k.identb[:], m == 0, m == 3,
                               [t_bk, k.t_ident], [k.pb[bank]])
                    cp(P, "act", wflat[:, rd_ * 512:(rd_ + 1) * 512], k.ps[:, bank, :], [k.pb[bank]], [t_wrt[r]])
            for sl in range(4):
                i = j * 4 + sl
                if i + 1 < len(seq):
                    load_u(i + 1)
                u_, tu = uj[i % 2], t_uj[i % 2]
                u3 = u_[:, 0:NT].rearrange("p (c t) -> p c t", t=TCH)
                u3c = u_[:, NT:NTX].rearrange("p (c t) -> p c t", t=TCH)
                for r in range(2):
                    b0 = 4 * (it % 2)
                    a_ = it % 2
                    it += 1
                    wr, twr = wrt[r], t_wrt[r]
                    pbs = [k.pb[b0 + m] for m in range(4)]

                    def smm(uv, nch):
                        for part in range(2):
                            for kk in range(TCH):
                                sidx = (TCH - 1 - kk) if r == 0 else kk
                                for m in range(4):
                                    mm(P, k.ps[:, b0 + m, part * 256: part * 256 + nch], wr[32 * m:32 * m + 32, kk, part, :],
                                       uv[32 * m:32 * m + 32, :, sidx], kk == 0, kk == TCH - 1, [twr, tu], [pbs[m]], tp=(32 * m, 0))
                    smm(u3, NCHL)
                    if sl == 0:
                        for m in range(4):
                            cp(P, "dve" if m % 2 == 0 else "act", ssb[:, m, :, :],
                               k.ps[:, b0 + m, :].rearrange("p (a b) -> p a b", a=2), [pbs[m]], [t_ssb])
                        dma(P, "sp", SD[r].ap()[:, 4 * j:4 * j + 4, :, 0:NCHL], ssb[:], reads=[t_ssb])
                        smm(u3c, NCHX)
                        for m in range(4):
                            cp(P, "dve" if m % 2 == 0 else "act", ssc[:, m, :, :],
                               k.ps[:, b0 + m, :].rearrange("p (a b) -> p a b", a=2)[:, :, 0:NCHX], [pbs[m]], [t_ssc])
                        dma(P, "sp", SD[r].ap()[:, 4 * j:4 * j + 4, :, NCHL:NCHT], ssc[:], reads=[t_ssc])
                    else:
                        n = 128
                        shp = [128, 4, n]
                        c0 = r * 64 + 4 * j
                        Rr = S.rpr[:, 0, c0:c0 + 4].unsqueeze(2).to_broadcast(shp)
                        Ri = S.rpi[:, 0, c0:c0 + 4].unsqueeze(2).to_broadcast(shp)
                        ia, ib = (0, 1) if r == 0 else (1, 0)
                        pv = k.ps[:, b0:b0 + 4, :]
                        Ar, Ai = pv[:, :, ia:256:2], pv[:, :, 256 + ia:512:2]
                        Br, Bi = pv[:, :, ib:256:2], pv[:, :, 256 + ib:512:2]
                        t = [x[:] for x in tm0[a_]]
                        o_r, o_i = cmuladd("dve", None, None, Ar, Ai, Br, Bi, Rr, Ri, t, t_tm0[a_], pbs)
                        tt(P, "dve", XB[r][:, sl - 1, :, 0, :], o_r, Br, ALU.add, [t_tm0[a_][0]] + pbs, [t_xb[r]])
                        tt(P, "dve", XB[r][:, sl - 1, :, 1, :], o_i, Bi, ALU.add, [t_tm0[a_][2]] + pbs, [t_xb[r]])
            for r in range(2):
                for l_ in range(1, 8):
                    n = 128 >> l_
                    src, tsrc = (XB[r], t_xb[r]) if l_ % 2 == 1 else (XC[r], t_xc[r])
                    dst, tdst = (XC[r], t_xc[r]) if l_ % 2 == 1 else (XB[r], t_xb[r])
                    shp = [128, NS, 4, n]
                    c0 = r * 64 + 4 * j
                    Rr = S.rpr[:, l_, c0:c0 + 4].unsqueeze(1).unsqueeze(3).to_broadcast(shp)
                    Ri = S.rpi[:, l_, c0:c0 + 4].unsqueeze(1).unsqueeze(3).to_broadcast(shp)
                    ia, ib = (0, 1) if r == 0 else (1, 0)
                    Ar, Ai = src[:, :, :, 0, ia:2 * n:2], src[:, :, :, 1, ia:2 * n:2]
                    Br, Bi = src[:, :, :, 0, ib:2 * n:2], src[:, :, :, 1, ib:2 * n:2]
                    t = [x[:, :, :, 0:n] for x in tm1[r]]
                    o_r, o_i = cmuladd("dve", None, None, Ar, Ai, Br, Bi, Rr, Ri, t, t_tm1[r], [tsrc])
                    tt(P, "dve", dst[:, :, :, 0, 0:n], o_r, Br, ALU.add, [t_tm1[r][0], tsrc], [tdst])
                    tt(P, "dve", dst[:, :, :, 1, 0:n], o_i, Bi, ALU.add, [t_tm1[r][2], tsrc], [tdst])
            for r in range(2):
                for part in range(2):
                    cp(P, "act", zs[:, 1:4, r, part, 4 * j:4 * j + 4], XC[r][:, :, :, part, 0], [t_xc[r]], [t_zs])
    P.barrier()


def cmul_add(P, eng, out_r, out_i, a_r, a_i, x_r, x_i, z_r, z_i, tm, R, Wt):
    tt(P, eng, tm[0][:], a_r, x_r, ALU.mult, R, Wt)
    tt(P, eng, tm[1][:], a_i, x_i, ALU.mult, R, Wt)
    tt(P, eng, tm[2][:], a_i, x_r, ALU.mult, R, Wt)
    tt(P, eng, tm[3][:], a_r, x_i, ALU.mult, R, Wt)
    tt(P, eng, tm[0][:], tm[0][:], tm[1][:], ALU.subtract, R, Wt)
    tt(P, eng, tm[2][:], tm[2][:], tm[3][:], ALU.add, R, Wt)
    tt(P, eng, out_r, tm[0][:], z_r, ALU.add, R, Wt)
    tt(P, eng, out_i, tm[2][:], z_i, ALU.add, R, Wt)


def l1_combine(P, k, S, C, zall, tz):
    with ExitStack() as st:
        hk = [P.sb(st, f"hk{i}", [128, 2, 64], F32) for i in range(2)]
        tm = [P.sb(st, f"cbt{i}", [128, 64], F32) for i in range(4)]
        R = [tz, S.t_tab, C.t_c]
        Wt = [C.t_c]
        for r in range(2):
            a_r = S.a4r[:, r * 64:(r + 1) * 64]
            a_i = S.a4i[:, r * 64:(r + 1) * 64]
            order = [0, 1, 2, 3] if r == 0 else [3, 2, 1, 0]
            cur_r, cur_i = C.hctx[:, r, 0, :], C.hctx[:, r, 1, :]
            ts(P, "dve", C.hin[:, r, 0, :], cur_r, S.sel[:, order[0]:order[0] + 1], ALU.mult, R, Wt)
            ts(P, "dve", C.hin[:, r, 1, :], cur_i, S.sel[:, order[0]:order[0] + 1], ALU.mult, R, Wt)
            for idx in range(1, 4):
                kprev, kc = order[idx - 1], order[idx]
                nh = hk[idx % 2]
                cmul_add(P, "dve", nh[:, 0, :], nh[:, 1, :], a_r, a_i, cur_r, cur_i, zall[:, kprev, r, 0, :], zall[:, kprev, r, 1, :],
                         tm, R, Wt)
                cur_r, cur_i = nh[:, 0, :], nh[:, 1, :]
                stt(P, "dve", C.hin[:, r, 0, :], cur_r, S.sel[:, kc:kc + 1], C.hin[:, r, 0, :], ALU.mult, ALU.add, R, Wt)
                stt(P, "dve", C.hin[:, r, 1, :], cur_i, S.sel[:, kc:kc + 1], C.hin[:, r, 1, :], ALU.mult, ALU.add, R, Wt)
    P.barrier()


def l1_outputs(P, k, I, S, UD, HD, GD):
    with ExitStack() as st:
        cr = P.sb(st, "cr", [128, 128, 16], F32)
        ci = P.sb(st, "ci", [128, 128, 16], F32)
        dcol = P.sb(st, "dcol", [128, 16], F32)
        t_c = P.tile("cri")
        dma(P, "sp", cr[:], I["cre_A"].ap(), writes=[t_c])
        dma(P, "sp", ci[:], I["cim_A"].ap(), writes=[t_c])
        dma(P, "sp", dcol[:], I["d_col"].ap(), writes=[t_c])
        W = table_work(P, st)
        bigs = [P.sb(st, f"big{r}", [128, 16, 2, 4, 128], BF16) for r in range(2)]
        t_big = P.tiles(2, "big")
        cpads = [P.sb(st, f"cpad{r}", [128, 1, 2, 4, 128], BF16) for r in range(2)]
        t_cpad = P.tiles(2, "cpad")
        for r in range(2):
            mset(P, "pool", bigs[r][:].rearrange("p a b c d -> p (a b c d)"), 0.0, [t_big[r]])
            mset(P, "pool", cpads[r][:].rearrange("p a b c d -> p (a b c d)"), 0.0, [t_cpad[r]])
        kt = [P.sb(st, f"kt{r}", [128, 16, 128], BF16) for r in range(2)]
        t_kt = P.tiles(2, "kt")
        uj = [P.sb(st, "uj4", [128, NT], BF16)] * 2
        t_uj = [P.tile("uj4")] * 2
        utm = P.sb(st, "utm", [128, TCH, NCHL], BF16)
        t_utm = P.tile("utm")
        hj = [[P.sb(st, f"hj{r}", [128, 4, 2, NCHL], BF16)] * 2 for r in range(2)]
        t_hj = [[P.tile("hj")] * 2 for r in range(2)]
        ysb = P.sb(st, "ysb", [128, 2048], F32)
        t_ysb = P.tile("ysb")
        gst = [P.sb(st, "gst4", [128, NT], BF16)] * 2
        t_gst = [P.tile("gst4")] * 2

        def load_j(j):
            for r in range(2):
                dma(P, "sp", hj[r][j % 2][:], HD[r].ap()[:, 4 * j:4 * j + 4, :, 0:NCHL], writes=[t_hj[r][j % 2]])
        for j in range(16):
            u_, tu = uj[j % 2], t_uj[j % 2]
            dma(P, "sp", u_[:], UD.ap()[j * 128:(j + 1) * 128, 0:NT], writes=[tu])
            load_j(j)
            g_, tg = gst[j % 2], t_gst[j % 2]
            cp(P, "pool", utm[:], u_[:].rearrange("p (c t) -> p t c", t=TCH), [tu], [t_utm])
            for r in range(2):
                rM0 = r * 64 + 4 * j
                build_cmp(P, S, W, cr, ci, 0, 1, rM0, True)
                pad_cmp(P, S, W, cpads[r], t_cpad[r], 1)
                build_cmp(P, S, W, S.bbr, S.bbi, 0, 16, rM0, False)
                pad_cmp(P, S, W, bigs[r], t_big[r], 16)
                for rd in range(4):
                    bank = 4 + (rd % 2)
                    for q in range(4):
                        kk = rd * 4 + q
                        i_ = 0
                        for m in range(4):
                            for part in range(2):
                                mm(P, k.ps[:, bank, q * 128:(q + 1) * 128], bigs[r][:, kk, part, m, :], cpads[r][:, 0, part, m, :],
                                   i_ == 0, i_ == 7, [t_big[r], t_cpad[r]], [k.pb[bank]])
                                i_ += 1
                    cp(P, "act", kt[r][:, rd * 4:(rd + 1) * 4, :].rearrange("p a b -> p (a b)"), k.ps[:, bank, :], [k.pb[bank]],
                       [t_kt[r]])
            for r in range(2):
                rM0 = r * 64 + 4 * j
                build_cmp(P, S, W, cr, ci, 1, 16, rM0, True)
                pad_cmp(P, S, W, bigs[r], t_big[r], 16)
            for hf in range(2):
                cs = slice(hf * 128, (hf + 1) * 128)
                mlist = []
                for r in range(2):
                    h_, th = hj[r][j % 2], t_hj[r][j % 2]
                    for kk in range(TCH):
                        for b in range(4):
                            if r == 0:
                                tlo, thi = max(4 * b, kk), 4 * b + 4
                                if tlo >= thi:
                                    continue
                                rhs = utm[:, tlo - kk:thi - kk, cs]
                            else:
                                tlo, thi = 4 * b, min(4 * b + 4, TCH - kk)
                                if tlo >= thi:
                                    continue
                                rhs = utm[:, tlo + kk:thi + kk, cs]
                            out = k.ps[:, b, (tlo - 4 * b) * 128:(thi - 4 * b) * 128].rearrange("p (t c) -> p t c", c=128)
                            mlist.append((b, out, kt[r][:, kk, :], rhs, [t_kt[r], t_utm]))
                    for t in range(TCH):
                        b = t // 4
                        tti = (t + 1) if r == 0 else (TCH - t)
                        for m in range(4):
                            for part in range(2):
                                out = k.ps[:, b, (t % 4) * 128:(t % 4 + 1) * 128]
                                mlist.append((b, out, bigs[r][:, tti - 1, part, m, :], h_[:, m, part, cs], [t_big[r], th]))
                lastidx = {}
                for i_, e in enumerate(mlist):
                    lastidx[e[0]] = i_
                seen = set()
                for i_, (b, out, lhsT, rhs, rds) in enumerate(mlist):
                    mm(P, out, lhsT, rhs, b not in seen, lastidx[b] == i_, rds, [k.pb[b]])
                    seen.add(b)
                pflat = k.ps[:, 0:4, :].rearrange("p a b -> p (a b)").rearrange("p (t c) -> p c t", c=128)
                cp(P, "act", ysb[:].rearrange("p (c t) -> p c t", t=TCH), pflat, [k.pb[0], k.pb[1], k.pb[2], k.pb[3]], [t_ysb])
                stt(P, "dve", ysb[:], u_[:, hf * 2048:(hf + 1) * 2048], dcol[:, j:j + 1], ysb[:], ALU.mult, ALU.add,
                    [tu, t_c, t_ysb], [t_ysb])
                act(P, g_[:, hf * 2048:(hf + 1) * 2048], ysb[:], AF.Gelu_apprx_tanh, [t_ysb], [tg])
            dma(P, "pool", GD.ap()[j * 128:(j + 1) * 128, :], g_[:], reads=[tg])
    P.barrier()


def l1_glu(P, k, I, GD, G2D):
    with ExitStack() as st:
        wg = P.sb(st, "wg", [128, 16, E2], BF16)
        t_wg = P.tile("wg")
        wsrc = I["ssm_w_glu"].ap().rearrange("(j p) c -> p j c", p=128)
        for q in range(8):
            dma(P, "pool", wg[:, q * 2:(q + 1) * 2, :], wsrc[:, q * 2:(q + 1) * 2, :], writes=[t_wg])
        bg = P.sb(st, "bglu", [128, 16], F32)
        t_bg = P.tile("bglu")
        dma(P, "sp", bg[:], I["bglu_col"].ap(), writes=[t_bg])
        gb = [P.sb(st, f"ggb{i}", [128, 16, 512], BF16) for i in range(2)]
        t_gb = P.tiles(2, "ggb")
        g2 = [P.sb(st, f"gg2{i}", [128, 16, 512], BF16) for i in range(2)]
        t_g2 = P.tiles(2, "gg2")
        sg = [P.sb(st, f"sg{i}", [128, 512], F32) for i in range(2)]
        t_sg = P.tiles(2, "sg")
        gsrc = GD.ap().rearrange("(j p) t -> p j t", p=128)
        gdst = G2D.ap().rearrange("(j p) t -> p j t", p=128)

        def load_g(b):
            dma(P, "sp", gb[b % 2][:], gsrc[:, :, b * 512:(b + 1) * 512], writes=[t_gb[b % 2]])
        load_g(0)
        it = 0
        for b in range(8):
            if b + 1 < 8:
                load_g(b + 1)
            g_, tg = gb[b % 2], t_gb[b % 2]
            o_, to = g2[b % 2], t_g2[b % 2]
            for jo in range(16):
                s = it % 2
                it += 1
                bank = s
                for j in range(16):
                    mm(P, k.ps[:, bank, :], wg[:, j, jo * 128:(jo + 1) * 128], g_[:, j, :], j == 0, j == 15, [t_wg, tg], [k.pb[bank]])
                act(P, sg[s][:], k.ps[:, bank, :], AF.Sigmoid, [k.pb[bank], t_bg], [t_sg[s]], scale=1.0, bias=bg[:, jo:jo + 1])
                tt(P, "dve", o_[:, jo, :], g_[:, jo, :], sg[s][:], ALU.mult, [tg, t_sg[s]], [to])
            dma(P, "pool", gdst[:, :, b * 512:(b + 1) * 512], o_[:], reads=[to])
    P.barrier()


def l1_out(P, k, I, X1src, HXD, G2D, OUT):
    with ExitStack() as st:
        wz = P.sb(st, "wz", [128, 8, E2], BF16)
        t_wz = P.tile("wz")
        wsrc = I["ssm_w_in"].ap().rearrange("(dt p) c -> p dt c", p=128)
        for q in range(4):
            dma(P, "pool", wz[:, :, q * 512:(q + 1) * 512], wsrc[:, :, E2 + q * 512:E2 + (q + 1) * 512], writes=[t_wz])
        wo = P.sb(st, "wo1", [128, 16, 1024], BF16)
        t_wo = P.tile("wo1")
        for q in range(4):
            dma(P, "pool", wo[:, q * 4:(q + 1) * 4, :],
                I["ssm_w_out"].ap().rearrange("(j p) d -> p j d", p=128)[:, q * 4:(q + 1) * 4, :], writes=[t_wo])
        hxb = [P.sb(st, f"hxb5{i}", [128, 8, 512], BF16) for i in range(2)]
        t_hxb = P.tiles(2, "hxb5")
        g2 = [P.sb(st, f"g25{i}", [128, 16, 512], BF16) for i in range(2)]
        t_g2 = P.tiles(2, "g25")
        gat = P.sb(st, "gat", [128, 16, 512], BF16)
        t_gat = P.tile("gat")
        sz = [P.sb(st, f"sz5{i}", [128, 512], F32) for i in range(2)]
        t_sz = P.tiles(2, "sz5")
        xt = [P.sb(st, f"xt5{i}", [128, 1024], F32) for i in range(2)]
        t_xt = P.tiles(2, "xt5")
        ot = [P.sb(st, f"ot5{i}", [128, 1024], F32) for i in range(2)]
        t_ot = P.tiles(2, "ot5")
        Ws = ln_work(P, st)
        load_ln_gate(P, k, I, st, 1)
        gsrc = G2D.ap().rearrange("(j p) t -> p j t", p=128)

        def load_b(b):
            dma(P, "sp", hxb[b % 2][:], HXD.ap()[:, :, b * 512:(b + 1) * 512], writes=[t_hxb[b % 2]])
            dma(P, "sp", g2[b % 2][:], gsrc[:, :, b * 512:(b + 1) * 512], writes=[t_g2[b % 2]])
        load_b(0)
        it = 0
        it2 = 0
        for b in range(8):
            if b + 1 < 8:
                load_b(b + 1)
            hb, thb = hxb[b % 2], t_hxb[b % 2]
            g_, tg = g2[b % 2], t_g2[b % 2]
            for j in range(16):
                s = it % 2
                it += 1
                bank = 4 + s
                for dt in range(8):
                    mm(P, k.ps[:, bank, :], wz[:, dt, j * 128:(j + 1) * 128], hb[:, dt, :], dt == 0, dt == 7, [t_wz, thb], [k.pb[bank]])
                act(P, sz[s][:], k.ps[:, bank, :], AF.Silu, [k.pb[bank]], [t_sz[s]])
                tt(P, "dve", gat[:, j, :], g_[:, j, :], sz[s][:], ALU.mult, [tg, t_sz[s]], [t_gat])
            for tl in range(4):
                s = it2 % 2
                it2 += 1
                psb = 2 * s
                r0 = b * 512 + tl * 128
                dma(P, "sp", xt[s][:], X1src.ap()[r0:r0 + 128, :], writes=[t_xt[s]])
                for h in range(2):
                    for j in range(16):
                        mm(P, k.ps[:, psb + h, :], gat[:, j, tl * 128:(tl + 1) * 128], wo[:, j, h * 512:(h + 1) * 512],
                           j == 0, j == 15, [t_gat, t_wo], [k.pb[psb + h]])
                ln_residual(P, k, Ws[s], psb, xt[s][:], t_xt[s], 0, ot[s][:], t_ot[s])
                dma(P, "pool", OUT.ap()[r0:r0 + 128, :], ot[s][:], reads=[t_ot[s]])
    P.barrier()


def _core_inputs_common(inp, core):
    b, kk = core // 4, core % 4
    t0 = kk * NT
    x = inp["x"]
    f32 = np.float32
    d = {}
    xh = np.zeros((NXH, D), f32)
    lo, hi = t0 - HALO, t0 + NT + HALO
    slo, shi = max(lo, 0), min(hi, x.shape[1])
    xh[slo - lo:shi - lo] = x[b, slo:shi]
    d["xT"] = np.ascontiguousarray(xh.T)
    d["xtok"] = np.ascontiguousarray(x[b, t0:t0 + NT])
    d["ctxT"] = np.ascontiguousarray(inp["ctx"][b].T)
    d["ctxtok"] = np.ascontiguousarray(inp["ctx"][b])
    cv = np.stack([inp["c"][b].reshape(8, 128).T, inp["c_ctx"].reshape(8, 128).T], axis=-1)
    d["cvec"] = np.ascontiguousarray(cv.astype(f32))
    edge = np.ones((128, 2), f32)
    if kk == 0:
        edge[:, 0] = 0.0
    if kk == 3:
        edge[:, 1] = 0.0
    d["edge"] = edge
    sel = np.zeros((128, 4), f32)
    sel[:, kk] = 1.0
    d["sel"] = sel
    return d


def _shared_inputs(inp):
    f32 = np.float32
    s = {}
    s["ident"] = np.eye(128, dtype=f32)
    s["ada_w"] = np.ascontiguousarray(inp["ada_w"])
    ab = inp["ada_b"]
    s["adab_col"] = np.ascontiguousarray(ab.reshape(2, 24, 128).transpose(2, 0, 1))
    s["adab_grow"] = np.ascontiguousarray(ab[None, :, 2048:3072])
    s["lngB"] = np.ascontiguousarray(np.broadcast_to(inp["ln_g"][None], (128, 2, D)))
    s["lncol0"] = np.ascontiguousarray(np.stack([inp["ln_g"][0].reshape(8, 128).T, inp["ln_b"][0].reshape(8, 128).T], axis=1))
    s["lnbB"] = np.ascontiguousarray(np.broadcast_to(inp["ln_b"][None], (128, 2, D)))
    s["conv_w_in"] = np.ascontiguousarray(inp["conv_w_in"][0])
    s["conv_w_out"] = np.ascontiguousarray(inp["conv_w_out"][0])
    s["cw"] = np.ascontiguousarray(inp["conv_w"][0].reshape(3, 16, 128).transpose(2, 1, 0))
    return s


L0_INPUTS = {
    "xT": [D, NXH], "xtok": [NT, D], "ctxT": [D, NCX], "ctxtok": [NCX, D], "cvec": [128, 8, 2],
    "edge": [128, 2], "ident": [128, 128], "ada_w": [2, D, 3 * D], "adab_col": [128, 2, 24],
    "adab_grow": [1, 2, D], "lngB": [128, 2, D], "lnbB": [128, 2, D],
    "conv_w_in": [D, 8192], "conv_w_out": [E2, D], "cw": [128, 16, 3],
}
L1_INPUTS = {
    "cvec": [128, 8, 2], "ident": [128, 128], "ada_w": [2, D, 3 * D], "adab_col": [128, 2, 24],
    "adab_grow": [1, 2, D], "lngB": [128, 2, D], "lnbB": [128, 2, D], "sel": [128, 4], "mgi": [128, 2],
    "ssm_w_in": [D, 2 * E2], "ssm_w_glu": [E2, E2], "ssm_w_out": [E2, D], "d_col": [128, 16], "bglu_col": [128, 16],
    "lamre_A": [128, 128], "lamim_A": [128, 128], "logstep_A": [128, 128],
    "bre_A": [128, 128, 16], "bim_A": [128, 128, 16], "cre_A": [128, 128, 16], "cim_A": [128, 128, 16],
}


def _shared_inputs_l1(inp):
    f32 = np.float32
    s = {}
    s["ssm_w_in"] = np.ascontiguousarray(inp["ssm_w_in"][0])
    s["ssm_w_glu"] = np.ascontiguousarray(inp["ssm_w_glu"][0])
    s["ssm_w_out"] = np.ascontiguousarray(inp["ssm_w_out"][0])
    s["d_col"] = np.ascontiguousarray(inp["ssm_d"][0].reshape(16, 128).T)
    s["bglu_col"] = np.ascontiguousarray(inp["ssm_b_glu"][0].reshape(16, 128).T)

    def lamA(a):
        return np.ascontiguousarray(a.reshape(2, 64, 2, 64).transpose(2, 3, 0, 1).reshape(128, 128))
    s["lamre_A"] = lamA(inp["ssm_lam_re"][0])
    s["lamim_A"] = lamA(inp["ssm_lam_im"][0])
    ls = inp["ssm_log_step"][0].reshape(2, 64, 2).transpose(2, 0, 1)
    s["logstep_A"] = np.ascontiguousarray(np.broadcast_to(ls[:, None], (2, 64, 2, 64)).reshape(128, 128))

    def bA(a):
        return np.ascontiguousarray(a.reshape(2, 64, 2, 64, 16).transpose(2, 3, 0, 1, 4).reshape(128, 128, 16))

    def cA(a):
        return np.ascontiguousarray(a.reshape(2, 64, 2, 16, 64).transpose(2, 4, 0, 1, 3).reshape(128, 128, 16))
    s["bre_A"] = bA(inp["ssm_b_re"][0])
    s["bim_A"] = bA(inp["ssm_b_im"][0])
    s["cre_A"] = cA(inp["ssm_c_re"][0])
    s["cim_A"] = cA(inp["ssm_c_im"][0])
    mgi = np.zeros((128, 2), f32)
    mgi[:64, 0] = 1.0
    mgi[64:, 1] = 1.0
    s["mgi"] = mgi
    return s


def carry_state(P, st):
    C = K()
    C.hctx = P.sb(st, "hctx", [128, 2, 2, 64], F32)
    C.hin = P.sb(st, "hin", [128, 2, 2, 64], F32)
    C.z = P.sb(st, "zloc", [128, 2, 2, 64], F32)
    C.t_c = P.tile("cstate")
    return C


def layer1(P, k, I, mode, X1src, CTX1src, ZOUT, ZALL, OUT, X1F=None):
    dbg = "ExternalOutput" if DEBUG else "Internal"
    UD = P.dram("ud", [E2, NTX], BF16, kind=dbg)
    HXD = P.dram("hxd", [128, 8, NTX], BF16)
    SD = [P.dram(f"sd{r}", [128, 64, 2, NCHT], F32) for r in range(2)]
    HD = [P.dram(f"hd{r}", [128, 64, 2, NCHL], BF16, kind=dbg) for r in range(2)]
    GD = P.dram("gd", [E2, NT], BF16, kind=dbg)
    G2D = P.dram("g2d", [E2, NT], BF16)
    ada(P, k, I, 1)
    with ExitStack() as st:
        S = s5_tables(P, k, I, st)
        C = carry_state(P, st)
        zall = P.sb(st, "zall", [128, 4, 2, 2, 64], F32)
        tz = P.tile("zall")
        l1_inproj(P, k, I, X1src, CTX1src, UD, HXD)
        if mode in ("A", "B"):
            l1_summaries(P, k, I, S, UD, SD)
        if mode == "A":
            fin = [(C.z[:, r, 0, :], C.z[:, r, 1, :], C.t_c) for r in range(2)]
            l1_carry(P, k, S, SD, HD, [None, None], fin, False, True, False)
            dma(P, "sp", ZOUT.ap(), C.z[:], reads=[C.t_c])
            P.barrier()
        if mode == "FUSED":
            UDF = [P.dram(f"udf{i}", [E2, NT], BF16) for i in range(3)]
            SDF = [[P.dram(f"sdf{i}_{r}", [128, 64, 2, NCHL], F32) for r in range(2)] for i in range(3)]
            zs = P.sb(st, "zs", [128, 4, 2, 2, 64], F32)
            perm = P.sb(st, "perm", [128, 16], F32)
            t_zs = P.tile("zs")
            dma(P, "sp", perm[:], I["perm"].ap(), writes=[t_zs])
            mset(P, "dve", zs[:].rearrange("p a b c d -> p (a b c d)"), 0.0, [t_zs])
            lncol = P.sb(st, "lncol", [128, 2, 8], F32)
            k.mcol2 = P.sb(st, "mcol2", [128, 16], F32)
            k.t_mcol2 = P.tile("mcol2")
            dma(P, "sp", lncol[:], I["lncol0"].ap(), writes=[k.t_mcol2])
            tt(P, "dve", k.mcol2[:, 8:16], lncol[:, 0, :], k.mcol[:, 8:16, 0], ALU.mult, [k.t_mcol, k.t_mcol2], [k.t_mcol2])
            tt(P, "dve", k.mcol2[:, 0:8], lncol[:, 1, :], k.mcol[:, 8:16, 0], ALU.mult, [k.t_mcol, k.t_mcol2], [k.t_mcol2])
            tt(P, "dve", k.mcol2[:, 0:8], k.mcol2[:, 0:8], k.mcol[:, 0:8, 0], ALU.add, [k.t_mcol, k.t_mcol2], [k.t_mcol2])
            for slot in range(1, 4):
                l1_inproj(P, k, I, X1F[slot - 1], None, UDF[slot - 1], HXD, do_ctx=False, store_hx=False, fold=True)
            if MERGED_ZR:
                l1_summaries_zr(P, k, I, S, [UD] + UDF, SD, zs, t_zs)
            else:
                l1_summaries_multi(P, k, I, S, [UD] + UDF, [SD] + SDF)
                l1_zreduce_foreign(P, k, S, [SD] + SDF, zs, t_zs)
            zf = zall[:].rearrange("p a b c d -> p a (b c d)")
            zsf = zs[:].rearrange("p a b c d -> p a (b c d)")
            for j in range(4):
                for sl in range(4):
                    if sl == 0:
                        ts(P, "dve", zf[:, j, :], zsf[:, sl, :], perm[:, sl * 4 + j:sl * 4 + j + 1], ALU.mult, [t_zs, tz], [tz])
                    else:
                        stt(P, "dve", zf[:, j, :], zsf[:, sl, :], perm[:, sl * 4 + j:sl * 4 + j + 1], zf[:, j, :], ALU.mult, ALU.add,
                            [t_zs, tz], [tz])
            P.barrier()
        if mode == "B":
            dma(P, "sp", zall[:], ZALL.ap(), writes=[tz])
        if mode in ("B", "FUSED"):
            fin = [(C.hctx[:, r, 0, :], C.hctx[:, r, 1, :], C.t_c) for r in range(2)]
            l1_carry(P, k, S, SD, HD, [None, None], fin, True, False, False)
            l1_combine(P, k, S, C, zall, tz)
            ini = [(C.hin[:, r, 0, :], C.hin[:, r, 1, :], C.t_c) for r in range(2)]
            l1_carry(P, k, S, SD, HD, ini, [None, None], False, True, True)
            l1_outputs(P, k, I, S, UD, HD, GD)
    if mode in ("B", "FUSED"):
        l1_glu(P, k, I, GD, G2D)
        l1_out(P, k, I, X1src, HXD, G2D, OUT)


FUSED_INPUTS = dict(L1_INPUTS)
FUSED_INPUTS.update({kk_: v for kk_, v in L0_INPUTS.items() if kk_ not in ("xT", "xtok", "edge")})
for _s in range(4):
    FUSED_INPUTS[f"xT{_s}"] = [D, NXH]
    FUSED_INPUTS[f"xtok{_s}"] = [NT, D]
    FUSED_INPUTS[f"edge{_s}"] = [128, 2]
FUSED_INPUTS["perm"] = [128, 16]
FUSED_INPUTS["lncol0"] = [128, 2, 8]


def build(mode):
    P = Prog()
    k = K()
    I = {}
    if mode == "L0":
        for nm, shp in L0_INPUTS.items():
            I[nm] = P.dram(nm, shp, F32, kind="ExternalInput")
        X1 = P.dram("x1_out", [NT, D], F32, kind="ExternalOutput")
        CTX1 = P.dram("ctx1_out", [NCX, D], F32, kind="ExternalOutput")
        G0 = P.dram("g0", [E2, NTX], BF16)
        setup_globals(P, k, I)
        layer0(P, k, I, X1, CTX1, G0)
    elif mode in ("A", "B"):
        for nm, shp in L1_INPUTS.items():
            I[nm] = P.dram(nm, shp, F32, kind="ExternalInput")
        X1 = P.dram("x1_in", [NT, D], F32, kind="ExternalInput")
        CTX1 = P.dram("ctx1_in", [NCX, D], F32, kind="ExternalInput")
        ZOUT = ZALL = OUT = None
        if mode == "A":
            ZOUT = P.dram("z_out", [128, 2, 2, 64], F32, kind="ExternalOutput")
        else:
            ZALL = P.dram("z_all", [128, 4, 2, 2, 64], F32, kind="ExternalInput")
            OUT = P.dram("out", [NT, D], F32, kind="ExternalOutput")
        setup_globals(P, k, I)
        layer1(P, k, I, mode, X1, CTX1, ZOUT, ZALL, OUT)
    elif mode == "FUSED":
        for nm, shp in FUSED_INPUTS.items():
            I[nm] = P.dram(nm, shp, F32, kind="ExternalInput")
        OUT = P.dram("out", [NT, D], F32, kind="ExternalOutput")
        X1 = P.dram("x1s", [NT, D], F32)
        X1F = [P.dram(f"x1f{i}", [NT, D], F32) for i in range(3)]
        CTX1 = P.dram("ctx1s", [NCX, D], F32)
        G0 = P.dram("g0", [E2, NTX], BF16)
        setup_globals(P, k, I)
        for slot in range(4):
            layer0(P, k, I, X1 if slot == 0 else X1F[slot - 1], CTX1, G0, slot=slot, do_ctx=(slot == 0), do_ada=(slot == 0),
                   ln_affine=(slot == 0))
        layer1(P, k, I, "FUSED", X1, CTX1, None, None, OUT, X1F=X1F)
    P.emit()
    return P


def _slot_inputs(inp, core):
    b, kk = core // 4, core % 4
    f32 = np.float32
    x = inp["x"]
    order = [kk] + [j for j in range(4) if j != kk]
    d = {}
    perm = np.zeros((128, 16), f32)
    for sl, pos in enumerate(order):
        t0 = pos * NT
        xh = np.zeros((NXH, D), f32)
        lo, hi = t0 - HALO, t0 + NT + HALO
        slo, shi = max(lo, 0), min(hi, x.shape[1])
        xh[slo - lo:shi - lo] = x[b, slo:shi]
        d[f"xT{sl}"] = np.ascontiguousarray(xh.T)
        d[f"xtok{sl}"] = np.ascontiguousarray(x[b, t0:t0 + NT])
        edge = np.ones((128, 2), f32)
        if pos == 0:
            edge[:, 0] = 0.0
        if pos == 3:
            edge[:, 1] = 0.0
        d[f"edge{sl}"] = edge
        perm[:, sl * 4 + pos] = 1.0
    d["perm"] = perm
    return d


def run_fused(inp):
    P = build("FUSED")
    sh = _shared_inputs(inp)
    sh.update(_shared_inputs_l1(inp))
    maps = []
    for core in range(8):
        d = _core_inputs_common(inp, core)
        d.update(_slot_inputs(inp, core))
        maps.append({nm: (d[nm] if nm in d else sh[nm]) for nm in FUSED_INPUTS})
    res = run_bass_kernel_spmd(P.nc, maps, core_ids=list(range(8)))
    return res.results


def run_L0(inp):
    P = build("L0")
    sh = _shared_inputs(inp)
    maps = []
    for core in range(8):
        d = _core_inputs_common(inp, core)
        maps.append({nm: (d[nm] if nm in d else sh[nm]) for nm in L0_INPUTS})
    res = run_bass_kernel_spmd(P.nc, maps, core_ids=list(range(8)))
    return res.results


def run_L1(inp, mode, x1s, ctx1s, zalls=None):
    P = build(mode)
    sh = _shared_inputs(inp)
    sh.update(_shared_inputs_l1(inp))
    maps = []
    for core in range(8):
        d = _core_inputs_common(inp, core)
        m = {nm: (d[nm] if nm in d else sh[nm]) for nm in L1_INPUTS}
        m["x1_in"] = x1s[core]
        m["ctx1_in"] = ctx1s[core]
        if mode == "B":
            m["z_all"] = zalls[core // 4]
        maps.append(m)
    res = run_bass_kernel_spmd(P.nc, maps, core_ids=list(range(8)))
    return res.results


def kernel_unfused(**inputs):
    inp = {k_: np.asarray(v, dtype=np.float32) for k_, v in inputs.items()}
    r0 = run_L0(inp)
    x1s = [np.ascontiguousarray(r0[c]["x1_out"]) for c in range(8)]
    ctx1s = [np.ascontiguousarray(r0[c]["ctx1_out"]) for c in range(8)]
    ra = run_L1(inp, "A", x1s, ctx1s)
    zalls = [np.ascontiguousarray(np.stack([ra[b * 4 + kk]["z_out"] for kk in range(4)], axis=1)) for b in range(2)]
    rb = run_L1(inp, "B", x1s, ctx1s, zalls)
    out = np.empty((2, 4 * NT, D), np.float32)
    for c in range(8):
        out[c // 4, (c % 4) * NT:(c % 4 + 1) * NT] = rb[c]["out"]
    return out


def kernel(**inputs):
    inp = {k_: np.asarray(v, dtype=np.float32) for k_, v in inputs.items()}
    rf = run_fused(inp)
    out = np.empty((2, 4 * NT, D), np.float32)
    for c in range(8):
        out[c // 4, (c % 4) * NT:(c % 4 + 1) * NT] = rf[c]["out"]
    return out
```

```python
from contextlib import ExitStack
import math
import numpy as np
import concourse.bass as bass
import concourse.mybir as mybir
from concourse.bass_utils import run_bass_kernel_spmd

F32 = mybir.dt.float32
BF16 = mybir.dt.bfloat16
ALU = mybir.AluOpType
AF = mybir.ActivationFunctionType
ENGS = ("pe", "act", "dve", "pool", "sp")
NDS = 16

D = 1024
E2 = 2048
NT = 4096
NCX = 256
NTX = NT + NCX
HALO = 64
NXH = NT + 2 * HALO
TCH = 16
NCHL = NT // TCH
NCHX = NCX // TCH
NCHT = NCHL + NCHX
CB = 64
ALPHA = 4.0 ** 0.25
LN_EPS = 1e-5
PI = math.pi
DEBUG = False
MERGED_ZR = True
SAME_ENGINE_SYNC = ("act", "dve", "pool")


class T_:
    __slots__ = ("name", "w", "r")

    def __init__(self, name):
        self.name = name
        self.w = []
        self.r = []


class Prog:
    def __init__(self):
        self.nc = bass.Bass("TRN2", target_bir_lowering=False)
        self.es = ExitStack()
        self.ops = []
        self.dma_rr = {e: 0 for e in ENGS}
        self.last_compute = {}
        self.last_dma = {}
        self.pending_barrier = {}
        self.ntile = 0

    def dram(self, name, shape, dt=F32, kind="Internal"):
        return self.nc.dram_tensor(name, list(shape), dt, kind=kind)

    def sb(self, st, name, shape, dt=F32):
        self.ntile += 1
        return st.enter_context(self.nc.sbuf_tensor(f"sb{self.ntile}_{name}", list(shape), dt))

    def tile(self, name="t"):
        self.ntile += 1
        return T_(f"{name}{self.ntile}")

    def tiles(self, n, name="t"):
        return [self.tile(name) for _ in range(n)]

    def op(self, eng, fn, reads=(), writes=(), dma=False):
        deps = set()
        for t in reads:
            deps.update(t.w)
        for t in writes:
            deps.update(t.w)
            deps.update(t.r)
        if eng in self.pending_barrier:
            deps.update(self.pending_barrier.pop(eng))
        oid = len(self.ops)
        slot = None
        if dma:
            slot = self.dma_rr[eng]
            self.dma_rr[eng] = (slot + 1) % NDS
            prev = self.last_dma.get((eng, slot))
            if prev is not None:
                deps.add(prev)
            self.last_dma[(eng, slot)] = oid
        else:
            self.last_compute[eng] = oid
        self.ops.append([eng, fn, sorted(deps), dma, slot])
        for t in reads:
            t.r.append(oid)
        for t in writes:
            t.w = [oid]
            t.r = []
        return oid

    def barrier(self):
        allp = list(self.last_compute.values()) + list(self.last_dma.values())
        for e in ENGS:
            self.pending_barrier[e] = set(allp) | self.pending_barrier.get(e, set())

    def emit(self):
        nc = self.nc
        ops = self.ops
        n = len(ops)
        needed = [False] * n
        for i, (eng, fn, deps, dma, slot) in enumerate(ops):
            if dma:
                needed[i] = True
            for d in deps:
                de, _, _, ddma, _ = ops[d]
                if de == eng and not ddma and not dma and (de == "pe" or de not in SAME_ENGINE_SYNC):
                    continue
                needed[d] = True
        sem_names = [f"s_{e}" for e in ENGS] + [f"d_{e}_{k}" for e in ENGS for k in range(NDS)]
        sems = {nm: self.es.enter_context(nc.semaphore(nm)) for nm in sem_names}
        cnt = {nm: 0 for nm in sem_names}
        ev = [None] * n
        for i, (eng, fn, deps, dma, slot) in enumerate(ops):
            if not needed[i]:
                continue
            nm = f"d_{eng}_{slot}" if dma else f"s_{eng}"
            cnt[nm] += 16 if dma else 1
            ev[i] = (nm, cnt[nm])
        self.maxsem = max(cnt.values())
        per_eng = {e: [] for e in ENGS}
        for i, o in enumerate(ops):
            per_eng[o[0]].append(i)

        with nc.Block() as block:
            def run(engname, Eh):
                waited = {}
                for i in per_eng[engname]:
                    eng, fn, deps, dma, slot = ops[i]
                    need = {}
                    for d in deps:
                        de, _, _, ddma, _ = ops[d]
                        if de == eng and not ddma and not dma and (de == "pe" or de not in SAME_ENGINE_SYNC):
                            continue
                        nm, v = ev[d]
                        if v > need.get(nm, 0):
                            need[nm] = v
                    for nm, v in need.items():
                        if v > waited.get(nm, 0):
                            Eh.wait_ge(sems[nm], v)
                            waited[nm] = v
                    ins = fn(Eh)
                    if ev[i] is not None:
                        ins.then_inc(sems[ev[i][0]], 16 if dma else 1)
                for nm, c in cnt.items():
                    if nm.startswith(f"d_{engname}_") and c > 0:
                        Eh.wait_ge(sems[nm], c)

            @block.sync
            def _(Eh):
                run("sp", Eh)

            @block.scalar
            def _(Eh):
                run("act", Eh)

            @block.vector
            def _(Eh):
                run("dve", Eh)

            @block.gpsimd
            def _(Eh):
                run("pool", Eh)

            @block.tensor
            def _(Eh):
                run("pe", Eh)
        return nc


def mm(P, out, lhsT, rhs, start, stop, reads, writes, tp=None):
    def f(Eh):
        if tp is not None:
            return Eh.matmul(out, lhsT=lhsT, rhs=rhs, start=start, stop=stop, tile_position=tp)
        return Eh.matmul(out, lhsT=lhsT, rhs=rhs, start=start, stop=stop)
    P.op("pe", f, reads=reads, writes=writes)


def dma(P, q, out, in_, reads=(), writes=()):
    P.op(q, lambda Eh: Eh.dma_start(out=out, in_=in_), reads=reads, writes=writes, dma=True)


def act(P, out, in_, func, reads, writes, scale=1.0, bias=0.0):
    P.op("act", lambda Eh: Eh.activation(out=out, in_=in_, func=func, bias=bias, scale=scale),
         reads=reads, writes=writes)


def tt(P, eng, out, in0, in1, op, reads, writes):
    P.op(eng, lambda Eh: Eh.tensor_tensor(out=out, in0=in0, in1=in1, op=op), reads=reads, writes=writes)


def ts(P, eng, out, in0, s1, op0, reads, writes, s2=None, op1=None):
    if op1 is None:
        P.op(eng, lambda Eh: Eh.tensor_scalar(out=out, in0=in0, scalar1=s1, scalar2=None, op0=op0),
             reads=reads, writes=writes)
    else:
        P.op(eng, lambda Eh: Eh.tensor_scalar(out=out, in0=in0, scalar1=s1, scalar2=s2, op0=op0, op1=op1),
             reads=reads, writes=writes)


def stt(P, eng, out, in0, scalar, in1, op0, op1, reads, writes):
    P.op(eng, lambda Eh: Eh.scalar_tensor_tensor(out=out, in0=in0, scalar=scalar, in1=in1, op0=op0, op1=op1),
         reads=reads, writes=writes)


def cp(P, eng, out, in_, reads, writes):
    if eng == "act":
        P.op("act", lambda Eh: Eh.activation(out=out, in_=in_, func=AF.Copy), reads=reads, writes=writes)
    else:
        P.op(eng, lambda Eh: Eh.tensor_copy(out=out, in_=in_), reads=reads, writes=writes)


def mset(P, eng, ap, val, writes):
    P.op(eng, lambda Eh: Eh.memset(ap, val), writes=writes)


class K:
    pass


def setup_globals(P, k, inputs_decl):
    nc = P.nc
    st = P.es
    k.ps = st.enter_context(nc.psum_tensor("ps_all", [128, 8, 512], F32))
    k.pb = P.tiles(8, "pb")
    k.identf = P.sb(st, "identf", [128, 128], F32)
    k.identb = P.sb(st, "identb", [128, 128], BF16)
    k.t_ident = P.tile("ident")
    k.ones1 = P.sb(st, "ones1", [1, 128], F32)
    k.t_ones = P.tile("ones")
    dma(P, "sp", k.identf[:], inputs_decl["ident"].ap(), writes=[k.t_ident])
    cp(P, "dve", k.identb[:], k.identf[:], [k.t_ident], [k.t_ident])
    mset(P, "dve", k.ones1[:], 1.0, [k.t_ones])
    k.mcol = P.sb(st, "mcol", [128, 24, 2], F32)
    k.t_mcol = P.tile("mcol")
    k.GROW = P.dram("growd", [2, 2, 1024], F32)
    k.t_growd = P.tile("growd")


def ps2(k, b):
    return k.ps[:, b:b + 2, :].rearrange("p a b -> p (a b)")


def ada(P, k, I, layer):
    nc = P.nc
    with ExitStack() as st:
        cv = P.sb(st, "cv", [128, 8, 2], F32)
        scv = P.sb(st, "scv", [128, 8, 2], F32)
        adab = P.sb(st, "adab", [128, 24], F32)
        wch = [P.sb(st, f"adaw{i}", [128, 8, 512], F32) for i in range(2)]
        grow = P.sb(st, "grow", [1, 2, 1024], F32)
        gbrow = P.sb(st, "gbrow", [1, 1024], F32)
        t_cv, t_adab, t_grow, t_gbrow = P.tiles(4, "ada")
        t_w = P.tiles(2, "adaw")
        dma(P, "sp", cv[:], I["cvec"].ap(), writes=[t_cv])
        dma(P, "sp", adab[:], I["adab_col"].ap()[:, layer, :], writes=[t_adab])
        dma(P, "sp", gbrow[:], I["adab_grow"].ap()[:, layer, :], writes=[t_gbrow])
        act(P, scv[:], cv[:], AF.Silu, [t_cv], [t_cv])
        wsrc = I["ada_w"].ap()[layer].rearrange("(dt p) c -> p dt c", p=128)
        pcol = k.ps[:, 0, 0:48].rearrange("p (j c) -> p j c", c=2)
        for jc in range(6):
            w = wch[jc % 2]
            tw = t_w[jc % 2]
            dma(P, "sp", w[:], wsrc[:, :, jc * 512:(jc + 1) * 512], writes=[tw])
            for jt in range(4):
                for dt in range(8):
                    mm(P, pcol[:, jc * 4 + jt, :], w[:, dt, jt * 128:(jt + 1) * 128], scv[:, dt, :],
                       dt == 0, dt == 7, [tw, t_cv], [k.pb[0]])
            if jc >= 4:
                for c in range(2):
                    for dt in range(8):
                        mm(P, k.ps[0:1, 1 + c, 0:512], scv[:, dt, c:c + 1], w[:, dt, :],
                           dt == 0, dt == 7, [tw, t_cv], [k.pb[1 + c]])
                    tt(P, "dve", grow[:, c, (jc - 4) * 512:(jc - 3) * 512], k.ps[0:1, 1 + c, 0:512],
                       gbrow[:, (jc - 4) * 512:(jc - 3) * 512], ALU.add, [k.pb[1 + c], t_gbrow], [t_grow])
        for c in range(2):
            tt(P, "dve", k.mcol[:, :, c], pcol[:, :, c], adab[:], ALU.add, [k.pb[0], t_adab], [k.t_mcol])
        ts(P, "dve", k.mcol[:, 8:16, :], k.mcol[:, 8:16, :], 1.0, ALU.add, [k.t_mcol], [k.t_mcol])
        dma(P, "sp", k.GROW.ap()[layer:layer + 1, :, :], grow[:], reads=[t_grow], writes=[k.t_growd])
    P.barrier()


def load_ln_gate(P, k, I, st, layer):
    k.gateB = P.sb(st, "gateB", [128, 2, 1024], F32)
    k.t_gateB = P.tile("gateB")
    k.lnB = P.sb(st, "lnB", [128, 2, 1024], F32)
    k.t_lnB = P.tile("lnB")
    dma(P, "sp", k.lnB[:, 0, :], I["lngB"].ap()[:, layer, :], writes=[k.t_lnB])
    dma(P, "sp", k.lnB[:, 1, :], I["lnbB"].ap()[:, layer, :], writes=[k.t_lnB])
    for c in range(2):
        dma(P, "sp", k.gateB[:, c, :], k.GROW.ap()[layer, c:c + 1, :].to_broadcast([128, 1024]), reads=[k.t_growd],
            writes=[k.t_gateB])


def ln_residual(P, k, W, psb, xt, t_xt, which, out_ap, t_out, affine=True):
    fx = ps2(k, psb)
    v, t_v = W["v"], W["t_v"]
    sts, mv, t_s = W["st"], W["mv"], W["t_s"]
    tt(P, "dve", v[:], fx, k.gateB[:, which, :], ALU.mult, [k.pb[psb], k.pb[psb + 1], k.t_gateB], [t_v])
    stt(P, "dve", v[:], xt, ALPHA, v[:], ALU.mult, ALU.add, [t_xt, t_v], [t_v])
    P.op("dve", lambda Eh: Eh.bn_stats(out=sts[:, 0:6], in_=v[:, 0:512]), reads=[t_v], writes=[t_s])
    P.op("dve", lambda Eh: Eh.bn_stats(out=sts[:, 6:12], in_=v[:, 512:1024]), reads=[t_v], writes=[t_s])
    P.op("dve", lambda Eh: Eh.bn_aggr(out=mv[:, 0:2], in_=sts[:, 0:12]), reads=[t_s], writes=[t_s])
    act(P, mv[:, 2:3], mv[:, 1:2], AF.Sqrt, [t_s], [t_s], scale=1.0, bias=W["eps"][:, 0:1])
    P.op("dve", lambda Eh: Eh.reciprocal(out=mv[:, 2:3], in_=mv[:, 2:3]), reads=[t_s], writes=[t_s])
    ts(P, "dve", mv[:, 3:4], mv[:, 0:1], mv[:, 2:3], ALU.mult, [t_s], [t_s], s2=-1.0, op1=ALU.mult)
    if not affine:
        act(P, out_ap, v[:], AF.Identity, [t_v, t_s], [t_out], scale=mv[:, 2:3], bias=mv[:, 3:4])
        return
    act(P, v[:], v[:], AF.Identity, [t_v, t_s], [t_v], scale=mv[:, 2:3], bias=mv[:, 3:4])
    tt(P, "pool", v[:], v[:], k.lnB[:, 0, :], ALU.mult, [t_v, k.t_lnB], [t_v])
    tt(P, "pool", out_ap, v[:], k.lnB[:, 1, :], ALU.add, [t_v, k.t_lnB], [t_out])


def ln_work(P, st, n=2):
    Ws = []
    for i in range(n):
        W = {}
        W["v"] = P.sb(st, f"lnv{i}", [128, 1024], F32)
        W["st"] = P.sb(st, f"lnst{i}", [128, 12], F32)
        W["mv"] = P.sb(st, f"lnmv{i}", [128, 4], F32)
        W["eps"] = P.sb(st, f"lneps{i}", [128, 1], F32)
        W["t_v"], W["t_s"], W["t_e"] = P.tiles(3, "lnw")
        mset(P, "dve", W["eps"][:], LN_EPS, [W["t_e"]])
        Ws.append(W)
    return Ws


def layer0(P, k, I, X1, CTX1, G0, slot=None, do_ctx=True, do_ada=True, ln_affine=True):
    sfx = "" if slot is None else str(slot)
    if do_ada:
        ada(P, k, I, 0)
    with ExitStack() as st:
        wout = P.sb(st, "wout", [128, 16, 1024], BF16)
        t_wout = P.tile("wout")
        for q in range(4):
            dma(P, "pool", wout[:, q * 4:(q + 1) * 4, :],
                I["conv_w_out"].ap().rearrange("(j p) d -> p j d", p=128)[:, q * 4:(q + 1) * 4, :], writes=[t_wout])
        with ExitStack() as st2:
            hxT = P.sb(st2, "hxT", [128, 8, NXH], BF16)
            hcT = P.sb(st2, "hcT", [128, 8, NCX], BF16)
            t_hx = P.tiles(8, "hx")
            t_hc = P.tile("hc")
            cw = P.sb(st2, "cw", [128, 16, 3], F32)
            edge = P.sb(st2, "edge", [128, 2], F32)
            t_cw = P.tile("cw")
            dma(P, "sp", cw[:], I["cw"].ap(), writes=[t_cw])
            dma(P, "sp", edge[:], I["edge" + sfx].ap(), writes=[t_cw])
            with ExitStack() as st3:
                xs = [P.sb(st3, f"xs{i}", [128, NXH], F32) for i in range(2)]
                xcs = P.sb(st3, "xcs", [128, 8, NCX], F32)
                t_xs = P.tiles(2, "xs")
                t_xcs = P.tile("xcs")
                if do_ctx:
                    dma(P, "sp", xcs[:], I["ctxT"].ap().rearrange("(dt p) t -> p dt t", p=128), writes=[t_xcs])
                for dt in range(8):
                    s = dt % 2
                    dma(P, "sp", xs[s][:], I["xT" + sfx].ap()[dt * 128:(dt + 1) * 128, :], writes=[t_xs[s]])
                    act(P, hxT[:, dt, :], xs[s][:], AF.Identity, [t_xs[s], k.t_mcol], [t_hx[dt]],
                        scale=k.mcol[:, 8 + dt, 0:1], bias=k.mcol[:, dt, 0:1])
                    if do_ctx:
                        act(P, hcT[:, dt, :], xcs[:, dt, :], AF.Identity, [t_xcs, k.t_mcol], [t_hc],
                            scale=k.mcol[:, 8 + dt, 1:2], bias=k.mcol[:, dt, 1:2])
            P.barrier()
            with ExitStack() as st3:
                wsl = [P.sb(st3, f"wsl{i}", [128, 8, 4, 128], BF16) for i in range(2)]
                t_wsl = P.tiles(2, "wsl")
                gst = [P.sb(st3, f"gst{i}", [128, NTX], BF16) for i in range(2)]
                t_gst = P.tiles(2, "gst")
                NW = 2
                cgs = [P.sb(st3, f"cgs{i}", [128, 640], F32) for i in range(NW)]
                uu = [P.sb(st3, f"uu{i}", [128, 640], F32) for i in range(NW)]
                yc = [P.sb(st3, f"yc{i}", [128, 512], F32) for i in range(NW)]
                sz = [P.sb(st3, f"sz{i}", [128, 512], F32) for i in range(NW)]
                t1 = [P.sb(st3, f"t1{i}", [128, 512], F32) for i in range(NW)]
                t_cgs, t_uu, t_yc, t_sz, t_t1 = (P.tiles(NW, "w") for _ in range(5))
                wsrc = I["conv_w_in"].ap().rearrange("(dt p) c -> p dt c", p=128)

                def load_w(j):
                    for part in range(4):
                        dma(P, "pool", wsl[j % 2][:, :, part, :],
                            wsrc[:, :, part * 2048 + j * 128: part * 2048 + (j + 1) * 128], writes=[t_wsl[j % 2]])
                load_w(0)
                it = 0
                for j in range(16):
                    if j + 1 < 16:
                        load_w(j + 1)
                    w = wsl[j % 2]
                    tw = t_wsl[j % 2]
                    vert = j >= 8
                    g = gst[j % 2]
                    tg = t_gst[j % 2]
                    for blk in range(9 if do_ctx else 8):
                        ws_ = it % NW
                        it += 1
                        isctx = blk == 8
                        n = NCX if isctx else 512
                        ext = vert and not isctx
                        def src(dt, c0, c1):
                            if isctx:
                                return hcT[:, dt, c0:c1], [t_hc]
                            return hxT[:, dt, c0:c1], [t_hx[dt]]
                        base = 0 if isctx else HALO + blk * 512
                        for part, bank in ((1, 0), (2, 2)):
                            if ext:
                                for dt in range(8):
                                    r, tr = src(dt, base - 64, base + 448)
                                    mm(P, k.ps[:, bank, :], w[:, dt, part, :], r, dt == 0, dt == 7, [tw] + tr, [k.pb[bank]])
                                for dt in range(8):
                                    r, tr = src(dt, base + 448, base + 576)
                                    mm(P, k.ps[:, bank + 1, 0:128], w[:, dt, part, :], r, dt == 0, dt == 7, [tw] + tr,
                                       [k.pb[bank + 1]])
                            else:
                                for dt in range(8):
                                    r, tr = src(dt, base, base + n)
                                    mm(P, k.ps[:, bank, 0:n], w[:, dt, part, :], r, dt == 0, dt == 7, [tw] + tr, [k.pb[bank]])
                        bz, bbg = 5 + 2 * (it % 2), 4 + 2 * (it % 2)
                        for part, bank in ((3, bz), (0, bbg)):
                            for dt in range(8):
                                r, tr = src(dt, base, base + n)
                                mm(P, k.ps[:, bank, 0:n], w[:, dt, part, :], r, dt == 0, dt == 7, [tw] + tr, [k.pb[bank]])
                        ne = 640 if ext else n
                        pcg = ps2(k, 0)[:, 0:ne]
                        pv = ps2(k, 2)[:, 0:ne]
                        cp(P, "act", cgs[ws_][:, 0:ne], pcg, [k.pb[0], k.pb[1]], [t_cgs[ws_]])
                        tt(P, "dve", uu[ws_][:, 0:ne], cgs[ws_][:, 0:ne], pv, ALU.mult, [t_cgs[ws_], k.pb[2], k.pb[3]], [t_uu[ws_]])
                        u_ = uu[ws_]
                        y_ = yc[ws_]
                        tu, ty = t_uu[ws_], t_yc[ws_]
                        if ext:
                            if blk == 0:
                                ts(P, "dve", u_[:, 0:64], u_[:, 0:64], edge[:, 0:1], ALU.mult, [tu, t_cw], [tu])
                            if blk == 7:
                                ts(P, "dve", u_[:, 576:640], u_[:, 576:640], edge[:, 1:2], ALU.mult, [tu, t_cw], [tu])
                            ts(P, "dve", y_[:, 0:512], u_[:, 64:576], cw[:, j, 1:2], ALU.mult, [tu, t_cw], [ty])
                            stt(P, "dve", y_[:, 0:512], u_[:, 0:512], cw[:, j, 0:1], y_[:, 0:512], ALU.mult, ALU.add, [tu, t_cw, ty], [ty])
                            stt(P, "dve", y_[:, 0:512], u_[:, 128:640], cw[:, j, 2:3], y_[:, 0:512], ALU.mult, ALU.add, [tu, t_cw, ty], [ty])
                        else:
                            rl = NCX if isctx else 64
                            u3 = u_[:, 0:n].rearrange("p (r c) -> p r c", c=rl)
                            y3 = y_[:, 0:n].rearrange("p (r c) -> p r c", c=rl)
                            ts(P, "dve", y_[:, 0:n], u_[:, 0:n], cw[:, j, 1:2], ALU.mult, [tu, t_cw], [ty])
                            stt(P, "dve", y3[:, :, 1:rl], u3[:, :, 0:rl - 1], cw[:, j, 0:1], y3[:, :, 1:rl], ALU.mult, ALU.add,
                                [tu, t_cw, ty], [ty])
                            stt(P, "dve", y3[:, :, 0:rl - 1], u3[:, :, 1:rl], cw[:, j, 2:3], y3[:, :, 0:rl - 1], ALU.mult, ALU.add,
                                [tu, t_cw, ty], [ty])
                        act(P, sz[ws_][:, 0:n], k.ps[:, bz, 0:n], AF.Silu, [k.pb[bz]], [t_sz[ws_]])
                        tt(P, "dve", t1[ws_][:, 0:n], k.ps[:, bbg, 0:n], y_[:, 0:n], ALU.mult, [k.pb[bbg], ty], [t_t1[ws_]])
                        c0 = NT if isctx else blk * 512
                        tt(P, "pool", g[:, c0:c0 + n], t1[ws_][:, 0:n], sz[ws_][:, 0:n], ALU.mult, [t_t1[ws_], t_sz[ws_]], [tg])
                    dma(P, "pool", G0.ap()[j * 128:(j + 1) * 128, 0:(NTX if do_ctx else NT)], g[:, 0:(NTX if do_ctx else NT)], reads=[tg])
        P.barrier()
        with ExitStack() as st2:
            gb = [P.sb(st2, f"gb{i}", [128, 16, 512], BF16) for i in range(2)]
            t_gb = P.tiles(2, "gb")
            xt = [P.sb(st2, f"xt{i}", [128, 1024], F32) for i in range(2)]
            t_xt = P.tiles(2, "xt")
            ot = [P.sb(st2, f"ot{i}", [128, 1024], F32) for i in range(2)]
            t_ot = P.tiles(2, "ot")
            Ws = ln_work(P, st2)
            load_ln_gate(P, k, I, st2, 0)
            gsrc = G0.ap().rearrange("(j p) t -> p j t", p=128)

            def load_g(b):
                n = NCX if b == 8 else 512
                dma(P, "sp", gb[b % 2][:, :, 0:n], gsrc[:, :, b * 512:b * 512 + n], writes=[t_gb[b % 2]])
            load_g(0)
            it = 0
            nblk0 = 9 if do_ctx else 8
            for b in range(nblk0):
                if b + 1 < nblk0:
                    load_g(b + 1)
                ntile = 2 if b == 8 else 4
                for tl in range(ntile):
                    s = it % 2
                    it += 1
                    psb = 2 * s
                    isctx = b == 8
                    if isctx:
                        xsrc = I["ctxtok"].ap()[tl * 128:(tl + 1) * 128, :]
                        dst = CTX1.ap()[tl * 128:(tl + 1) * 128, :]
                    else:
                        r0 = b * 512 + tl * 128
                        xsrc = I["xtok" + sfx].ap()[r0:r0 + 128, :]
                        dst = X1.ap()[r0:r0 + 128, :]
                    dma(P, "sp", xt[s][:], xsrc, writes=[t_xt[s]])
                    for h in range(2):
                        for j in range(16):
                            mm(P, k.ps[:, psb + h, :], gb[b % 2][:, j, tl * 128:(tl + 1) * 128], wout[:, j, h * 512:(h + 1) * 512],
                               j == 0, j == 15, [t_gb[b % 2], t_wout], [k.pb[psb + h]])
                    ln_residual(P, k, Ws[s], psb, xt[s][:], t_xt[s], 1 if isctx else 0, ot[s][:], t_ot[s], affine=ln_affine)
                    dma(P, "pool", dst, ot[s][:], reads=[t_ot[s]])
    P.barrier()


def bc_last(ap, shape):
    return ap.unsqueeze(len(shape) - 1).to_broadcast(shape)


def s5_tables(P, k, I, st):
    S = K()
    S.apr = P.sb(st, "apr", [128, 17, 128], F32)
    S.api = P.sb(st, "api", [128, 17, 128], F32)
    S.bbr = P.sb(st, "bbr", [128, 128, 16], F32)
    S.bbi = P.sb(st, "bbi", [128, 128, 16], F32)
    S.a4r = P.sb(st, "a4r", [128, 128], F32)
    S.a4i = P.sb(st, "a4i", [128, 128], F32)
    S.rpr = P.sb(st, "rpr", [128, 8, 128], F32)
    S.rpi = P.sb(st, "rpi", [128, 8, 128], F32)
    S.mgi = P.sb(st, "mgi", [128, 2], F32)
    S.sel = P.sb(st, "sel", [128, 4], F32)
    S.t_tab = P.tile("s5tab")
    tb = S.t_tab
    dma(P, "sp", S.mgi[:], I["mgi"].ap(), writes=[tb])
    dma(P, "sp", S.sel[:], I["sel"].ap(), writes=[tb])
    with ExitStack() as s2:
        lr = P.sb(s2, "lr", [128, 128], F32)
        li = P.sb(s2, "li", [128, 128], F32)
        ls = P.sb(s2, "ls", [128, 128], F32)
        bre = P.sb(s2, "bre", [128, 128, 16], F32)
        bim = P.sb(s2, "bim", [128, 128, 16], F32)
        w = [P.sb(s2, f"zw{i}", [128, 128], F32) for i in range(8)]
        big1 = P.sb(s2, "zb1", [128, 128, 16], F32)
        big2 = P.sb(s2, "zb2", [128, 128, 16], F32)
        tz = P.tile("zoh")
        dma(P, "sp", lr[:], I["lamre_A"].ap(), writes=[tz])
        dma(P, "sp", li[:], I["lamim_A"].ap(), writes=[tz])
        dma(P, "sp", ls[:], I["logstep_A"].ap(), writes=[tz])
        dma(P, "sp", bre[:], I["bre_A"].ap(), writes=[tz])
        dma(P, "sp", bim[:], I["bim_A"].ap(), writes=[tz])
        R, Wt = [tz, tb], [tz, tb]
        dtt, mag, th, cc, ss, t1, t2, t3 = w
        act(P, dtt[:], ls[:], AF.Exp, R, Wt)
        tt(P, "dve", mag[:], lr[:], dtt[:], ALU.mult, R, Wt)
        act(P, mag[:], mag[:], AF.Exp, R, Wt)
        tt(P, "dve", th[:], li[:], dtt[:], ALU.mult, R, Wt)
        act(P, ss[:], th[:], AF.Sin, R, Wt, scale=1.0 / 64.0)
        ts(P, "dve", t1[:], th[:], 1.0 / 64.0, ALU.mult, R, Wt, s2=PI / 2, op1=ALU.add)
        act(P, cc[:], t1[:], AF.Sin, R, Wt)
        for _ in range(6):
            tt(P, "dve", t1[:], cc[:], cc[:], ALU.mult, R, Wt)
            tt(P, "dve", t2[:], ss[:], ss[:], ALU.mult, R, Wt)
            tt(P, "dve", t3[:], cc[:], ss[:], ALU.mult, R, Wt)
            tt(P, "dve", cc[:], t1[:], t2[:], ALU.subtract, R, Wt)
            ts(P, "dve", ss[:], t3[:], 2.0, ALU.mult, R, Wt)
        ar, ai = S.apr[:, 1, :], S.api[:, 1, :]
        tt(P, "dve", ar, mag[:], cc[:], ALU.mult, R, Wt)
        tt(P, "dve", ai, mag[:], ss[:], ALU.mult, R, Wt)
        mset(P, "dve", S.apr[:, 0, :], 1.0, Wt)
        mset(P, "dve", S.api[:, 0, :], 0.0, Wt)
        tt(P, "dve", t1[:], lr[:], lr[:], ALU.mult, R, Wt)
        tt(P, "dve", t2[:], li[:], li[:], ALU.mult, R, Wt)
        tt(P, "dve", t1[:], t1[:], t2[:], ALU.add, R, Wt)
        P.op("dve", lambda Eh: Eh.reciprocal(out=t1[:], in_=t1[:]), reads=R, writes=Wt)
        ts(P, "dve", t2[:], ar, -1.0, ALU.add, R, Wt)
        tt(P, "dve", t3[:], t2[:], lr[:], ALU.mult, R, Wt)
        tt(P, "dve", cc[:], ai, li[:], ALU.mult, R, Wt)
        tt(P, "dve", t3[:], t3[:], cc[:], ALU.add, R, Wt)
        tt(P, "dve", t3[:], t3[:], t1[:], ALU.mult, R, Wt)
        tt(P, "dve", cc[:], ai, lr[:], ALU.mult, R, Wt)
        tt(P, "dve", ss[:], t2[:], li[:], ALU.mult, R, Wt)
        tt(P, "dve", cc[:], cc[:], ss[:], ALU.subtract, R, Wt)
        tt(P, "dve", cc[:], cc[:], t1[:], ALU.mult, R, Wt)
        frb = bc_last(t3[:], [128, 128, 16])
        fib = bc_last(cc[:], [128, 128, 16])
        tt(P, "dve", big1[:], bre[:], frb, ALU.mult, R, Wt)
        tt(P, "dve", big2[:], bim[:], fib, ALU.mult, R, Wt)
        tt(P, "dve", S.bbr[:], big1[:], big2[:], ALU.subtract, R, Wt)
        tt(P, "dve", big1[:], bim[:], frb, ALU.mult, R, Wt)
        tt(P, "dve", big2[:], bre[:], fib, ALU.mult, R, Wt)
        tt(P, "dve", S.bbi[:], big1[:], big2[:], ALU.add, R, Wt)
        for kk in range(1, 16):
            pr, pi_ = S.apr[:, kk, :], S.api[:, kk, :]
            tt(P, "dve", t1[:], pr, ar, ALU.mult, R, Wt)
            tt(P, "dve", t2[:], pi_, ai, ALU.mult, R, Wt)
            tt(P, "dve", S.apr[:, kk + 1, :], t1[:], t2[:], ALU.subtract, R, Wt)
            tt(P, "dve", t1[:], pr, ai, ALU.mult, R, Wt)
            tt(P, "dve", t2[:], pi_, ar, ALU.mult, R, Wt)
            tt(P, "dve", S.api[:, kk + 1, :], t1[:], t2[:], ALU.add, R, Wt)
        cp(P, "dve", S.a4r[:], S.apr[:, 16, :], R, Wt)
        cp(P, "dve", S.a4i[:], S.api[:, 16, :], R, Wt)
        for l_ in range(8):
            cp(P, "dve", S.rpr[:, l_, :], S.a4r[:], R, Wt)
            cp(P, "dve", S.rpi[:, l_, :], S.a4i[:], R, Wt)
            tt(P, "dve", t1[:], S.a4r[:], S.a4r[:], ALU.mult, R, Wt)
            tt(P, "dve", t2[:], S.a4i[:], S.a4i[:], ALU.mult, R, Wt)
            tt(P, "dve", t3[:], S.a4r[:], S.a4i[:], ALU.mult, R, Wt)
            tt(P, "dve", S.a4r[:], t1[:], t2[:], ALU.subtract, R, Wt)
            ts(P, "dve", S.a4i[:], t3[:], 2.0, ALU.mult, R, Wt)
    P.barrier()
    return S


def l1_inproj(P, k, I, X1src, CTX1src, UD, HXD, do_ctx=True, store_hx=True, fold=False):
    with ExitStack() as st:
        wu = P.sb(st, "wu", [128, 8, E2], BF16)
        t_wu = P.tile("wu")
        wsrc = I["ssm_w_in"].ap().rearrange("(dt p) c -> p dt c", p=128)
        for q in range(4):
            dma(P, "pool", wu[:, :, q * 512:(q + 1) * 512], wsrc[:, :, q * 512:(q + 1) * 512], writes=[t_wu])
        xt = [P.sb(st, f"x1t{i}", [128, 1024], F32) for i in range(2)]
        t_xt = P.tiles(2, "x1t")
        hxb = [P.sb(st, f"hxb{i}", [128, 8, 512], BF16) for i in range(2)]
        t_hxb = P.tiles(2, "hxb")
        ub = [P.sb(st, f"ub{i}", [128, 16, 512], BF16) for i in range(2)]
        t_ub = P.tiles(2, "ub")
        usrc = UD.ap().rearrange("(j p) t -> p j t", p=128)
        nblk = 9 if do_ctx else 8
        cnt = [0]

        def prep(b):
            isctx = b == 8
            n = NCX if isctx else 512
            hb, thb = hxb[b % 2], t_hxb[b % 2]
            for tl in range(n // 128):
                s = cnt[0] % 2
                cnt[0] += 1
                src = CTX1src.ap()[tl * 128:(tl + 1) * 128, :] if isctx else X1src.ap()[b * 512 + tl * 128: b * 512 + (tl + 1) * 128, :]
                dma(P, "sp", xt[s][:], src, writes=[t_xt[s]])
                for h in range(2):
                    bank = 2 * s + h
                    for q in range(4):
                        dt = h * 4 + q
                        P.op("pe", lambda Eh, o=k.ps[:, bank, q * 128:(q + 1) * 128], i_=xt[s][:, dt * 128:(dt + 1) * 128]:
                             Eh.transpose(o, i_, k.identf[:]), reads=[t_xt[s], k.t_ident], writes=[k.pb[bank]])
                    for q in range(4):
                        dt = h * 4 + q
                        if fold:
                            act(P, hb[:, dt, tl * 128:(tl + 1) * 128], k.ps[:, bank, q * 128:(q + 1) * 128], AF.Identity,
                                [k.pb[bank], k.t_mcol2], [thb], scale=k.mcol2[:, 8 + dt:9 + dt], bias=k.mcol2[:, dt:dt + 1])
                        else:
                            act(P, hb[:, dt, tl * 128:(tl + 1) * 128], k.ps[:, bank, q * 128:(q + 1) * 128], AF.Identity,
                                [k.pb[bank], k.t_mcol], [thb], scale=k.mcol[:, 8 + dt, (1 if isctx else 0):(2 if isctx else 1)],
                                bias=k.mcol[:, dt, (1 if isctx else 0):(2 if isctx else 1)])
            c0 = NT if isctx else b * 512
            if store_hx:
                dma(P, "act", HXD.ap()[:, :, c0:c0 + n], hb[:, :, 0:n], reads=[thb])

        prep(0)
        for b in range(nblk):
            if b + 1 < nblk:
                prep(b + 1)
            isctx = b == 8
            n = NCX if isctx else 512
            hb, thb = hxb[b % 2], t_hxb[b % 2]
            c0 = NT if isctx else b * 512
            u_, tu = ub[b % 2], t_ub[b % 2]
            for j in range(16):
                bank = 4 + (j % 4)
                for dt in range(8):
                    mm(P, k.ps[:, bank, 0:n], wu[:, dt, j * 128:(j + 1) * 128], hb[:, dt, 0:n], dt == 0, dt == 7,
                       [t_wu, thb], [k.pb[bank]])
                cp(P, "dve", u_[:, j, 0:n], k.ps[:, bank, 0:n], [k.pb[bank]], [tu])
            dma(P, "pool", usrc[:, :, c0:c0 + n], u_[:, :, 0:n], reads=[tu])
    P.barrier()


def build_cmp(P, S, W, src_r, src_i, k0, nk, rM0, sign_neg_part1):
    shp = [128, nk, 4, 16]
    A_r = bc_last(S.apr[:, k0:k0 + nk, rM0:rM0 + 4], shp)
    A_i = bc_last(S.api[:, k0:k0 + nk, rM0:rM0 + 4], shp)
    B_r = src_r[:, rM0:rM0 + 4, :].unsqueeze(1).to_broadcast(shp)
    B_i = src_i[:, rM0:rM0 + 4, :].unsqueeze(1).to_broadcast(shp)
    cmp_, t_cmp = W["cmp"], W["t_cmp"]
    ta, tb_ = W["tmp1"], W["tmp2"]
    R = [S.t_tab, W["t_tmp"]]
    Wr = [W["t_tmp"]]
    tt(P, "dve", ta[:, 0:nk], B_r, A_r, ALU.mult, R, Wr)
    tt(P, "pool", tb_[:, 0:nk], B_i, A_i, ALU.mult, [S.t_tab, W["t_tmp2"]], [W["t_tmp2"]])
    tt(P, "dve", cmp_[:, 0:nk, 0, :, :], ta[:, 0:nk], tb_[:, 0:nk], ALU.subtract, [W["t_tmp"], W["t_tmp2"]], [t_cmp])
    tt(P, "dve", ta[:, 0:nk], B_i, A_r, ALU.mult, R + [W["t_tmp2"]], Wr)
    tt(P, "pool", tb_[:, 0:nk], B_r, A_i, ALU.mult, [S.t_tab, W["t_tmp2"], W["t_tmp"]], [W["t_tmp2"]])
    if sign_neg_part1:
        stt(P, "dve", cmp_[:, 0:nk, 1, :, :], ta[:, 0:nk], -1.0, tb_[:, 0:nk], ALU.mult, ALU.subtract,
            [W["t_tmp"], W["t_tmp2"]], [t_cmp])
    else:
        tt(P, "dve", cmp_[:, 0:nk, 1, :, :], ta[:, 0:nk], tb_[:, 0:nk], ALU.add, [W["t_tmp"], W["t_tmp2"]], [t_cmp])


def pad_cmp(P, S, W, dst, t_dst, nk):
    cmp_, t_cmp = W["cmp"], W["t_cmp"]
    i = 0
    for gi in range(2):
        for m in range(4):
            c0 = (2 * m + gi) * 16
            if gi == 0:
                ts(P, "dve", dst[:, 0:nk, :, m, c0:c0 + 16], cmp_[:, 0:nk, :, m, :], S.mgi[:, gi:gi + 1], ALU.mult,
                   [t_cmp, S.t_tab], [t_dst])
            else:
                act(P, dst[:, 0:nk, :, m, c0:c0 + 16], cmp_[:, 0:nk, :, m, :], AF.Copy, [t_cmp, S.t_tab], [t_dst],
                    scale=S.mgi[:, gi:gi + 1])
            i += 1


def table_work(P, st):
    W = {}
    W["cmp"] = P.sb(st, "cmp", [128, 17, 2, 4, 16], F32)
    W["tmp1"] = P.sb(st, "tmp1", [128, 17, 4, 16], F32)
    W["tmp2"] = P.sb(st, "tmp2", [128, 17, 4, 16], F32)
    W["t_cmp"], W["t_tmp"], W["t_tmp2"] = P.tiles(3, "tw")
    return W


def l1_summaries(P, k, I, S, UD, SD, do_ctx=True):
    with ExitStack() as st:
        W = table_work(P, st)
        bk = P.sb(st, "bkpad", [128, 16, 2, 4, 128], BF16)
        t_bk = P.tile("bk")
        mset(P, "pool", bk[:].rearrange("p a b c d -> p (a b c d)"), 0.0, [t_bk])
        wrt = [P.sb(st, f"wrt{i}", [128, 16, 2, 128], BF16) for i in range(2)]
        t_wrt = P.tiles(2, "wrt")
        uj = [P.sb(st, f"uj{i}", [128, NTX], BF16) for i in range(2)]
        t_uj = P.tiles(2, "uj")
        ssb = [P.sb(st, f"ssb{i}", [128, 4, 2, 256], F32) for i in range(2)]
        t_ssb = P.tiles(2, "ssb")
        ssc = [P.sb(st, f"ssc{i}", [128, 4, 2, 16], F32) for i in range(2)]
        t_ssc = P.tiles(2, "ssc")

        nld = NTX if do_ctx else NT

        def load_u(j):
            dma(P, "sp", uj[j % 2][:, 0:nld], UD.ap()[j * 128:(j + 1) * 128, 0:nld], writes=[t_uj[j % 2]])
        load_u(0)
        it = 0
        for j in range(16):
            if j + 1 < 16:
                load_u(j + 1)
            u_, tu = uj[j % 2], t_uj[j % 2]
            u3 = u_[:, 0:NT].rearrange("p (c t) -> p c t", t=TCH)
            u3c = u_[:, NT:NTX].rearrange("p (c t) -> p c t", t=TCH)
            for r in range(2):
                s = it % 2
                it += 1
                rM0 = r * 64 + 4 * j
                build_cmp(P, S, W, S.bbr, S.bbi, 0, 16, rM0, False)
                pad_cmp(P, S, W, bk, t_bk, 16)
                wr, twr = wrt[s], t_wrt[s]
                wflat = wr[:].rearrange("p a b c -> p (a b c)")
                for rd in range(8):
                    bank = 4 + (rd % 2)
                    for q in range(4):
                        kp = rd * 4 + q
                        kk, part = divmod(kp, 2)
                        for m in range(4):
                            mm(P, k.ps[:, bank, q * 128:(q + 1) * 128], bk[:, kk, part, m, :], k.identb[:], m == 0, m == 3,
                               [t_bk, k.t_ident], [k.pb[bank]])
                    cp(P, "act", wflat[:, rd * 512:(rd + 1) * 512], k.ps[:, bank, :], [k.pb[bank]], [twr])
                for (uv, nch, dst, tdst, dcol) in ((u3, NCHL, ssb[s], t_ssb[s], 0), (u3c, NCHX, ssc[s], t_ssc[s], NCHL))[0:(2 if do_ctx else 1)]:
                    for part in range(2):
                        for kk in range(TCH):
                            sidx = (TCH - 1 - kk) if r == 0 else kk
                            for m in range(4):
                                mm(P, k.ps[:, m, part * 256: part * 256 + nch], wr[32 * m:32 * m + 32, kk, part, :],
                                   uv[32 * m:32 * m + 32, :, sidx], kk == 0, kk == TCH - 1, [twr, tu], [k.pb[m]], tp=(32 * m, 0))
                    for m in range(4):
                        cp(P, "dve" if m % 2 == 0 else "act", dst[:, m, :, :],
                           k.ps[:, m, :].rearrange("p (a b) -> p a b", a=2)[:, :, 0:nch], [k.pb[m]], [tdst])
                    dma(P, "sp", SD[r].ap()[:, 4 * j:4 * j + 4, :, dcol:dcol + nch], dst[:], reads=[tdst])
    P.barrier()


def l1_carry(P, k, S, SD, HD, init, final, do_ctx, do_lat, store):
    CBK = 32
    eng = "dve"
    with ExitStack() as st:
        sblk = [[P.sb(st, f"sblk{r}{i}", [128, 64, 2, CBK], F32) for i in range(2)] for r in range(2)]
        t_sblk = [P.tiles(2, "sblk") for r in range(2)]
        hseq = [P.sb(st, f"hseq{r}", [128, 64, 2, CBK + 1], F32) for r in range(2)]
        t_hseq = P.tiles(2, "hseq")
        hbf = [P.sb(st, f"hbf{r}", [128, 64, 2, CBK], BF16) for r in range(2)]
        t_hbf = P.tiles(2, "hbf")
        tmp = [[P.sb(st, f"ctmp{r}{i}", [128, 64], F32) for i in range(4)] for r in range(2)]
        t_tmp = [P.tiles(4, "ctmp") for r in range(2)]
        blocks = [[], []]
        for r in range(2):
            if do_ctx:
                blocks[r].append((NCHL, NCHX))
            if do_lat:
                lat = [(c0, CBK) for c0 in range(0, NCHL, CBK)]
                blocks[r] += lat if r == 0 else lat[::-1]
        Rr = [S.apr[:, 16, r * 64:(r + 1) * 64] for r in range(2)]
        Ri = [S.api[:, 16, r * 64:(r + 1) * 64] for r in range(2)]

        def load_blk(r, bi):
            c0, nb = blocks[r][bi]
            dma(P, "sp", sblk[r][bi % 2][:, :, :, 0:nb], SD[r].ap()[:, :, :, c0:c0 + nb], writes=[t_sblk[r][bi % 2]])
        nblk = len(blocks[0])
        for r in range(2):
            load_blk(r, 0)
        for bi in range(nblk):
            nb = blocks[0][bi][1]
            for r in range(2):
                if bi + 1 < nblk:
                    load_blk(r, bi + 1)
                hs, ths = hseq[r], t_hseq[r]
                e_in = 0 if r == 0 else nb
                if bi == 0:
                    if init[r] is None:
                        mset(P, eng, hs[:, :, :, e_in], 0.0, [ths])
                    else:
                        cp(P, eng, hs[:, :, 0, e_in], init[r][0], [init[r][2]], [ths])
                        cp(P, eng, hs[:, :, 1, e_in], init[r][1], [init[r][2]], [ths])
                else:
                    pnb = blocks[r][bi - 1][1]
                    e_prev = pnb if r == 0 else 0
                    if e_prev != e_in:
                        cp(P, eng, hs[:, :, :, e_in], hs[:, :, :, e_prev], [ths], [ths])
            for i in range(nb):
                cc = [i, nb - 1 - i]
                ei = [cc[0], cc[1] + 1]
                eo = [cc[0] + 1, cc[1]]
                for r in range(2):
                    hs, ths, tm, ttm = hseq[r], t_hseq[r], tmp[r], t_tmp[r]
                    hr, hi = hs[:, :, 0, ei[r]], hs[:, :, 1, ei[r]]
                    tt(P, eng, tm[0][:], Rr[r], hr, ALU.mult, [S.t_tab, ths], [ttm[0]])
                    tt(P, eng, tm[1][:], Ri[r], hi, ALU.mult, [S.t_tab, ths], [ttm[1]])
                    tt(P, eng, tm[2][:], Ri[r], hr, ALU.mult, [S.t_tab, ths], [ttm[2]])
                    tt(P, eng, tm[3][:], Rr[r], hi, ALU.mult, [S.t_tab, ths], [ttm[3]])
                for r in range(2):
                    tm, ttm = tmp[r], t_tmp[r]
                    tt(P, eng, tm[0][:], tm[0][:], tm[1][:], ALU.subtract, [ttm[0], ttm[1]], [ttm[0]])
                    tt(P, eng, tm[2][:], tm[2][:], tm[3][:], ALU.add, [ttm[2], ttm[3]], [ttm[2]])
                for r in range(2):
                    hs, ths, tm, ttm = hseq[r], t_hseq[r], tmp[r], t_tmp[r]
                    sb_, tsb = sblk[r][bi % 2], t_sblk[r][bi % 2]
                    tt(P, eng, hs[:, :, 0, eo[r]], tm[0][:], sb_[:, :, 0, cc[r]], ALU.add, [ttm[0], tsb], [ths])
                    tt(P, eng, hs[:, :, 1, eo[r]], tm[2][:], sb_[:, :, 1, cc[r]], ALU.add, [ttm[2], tsb], [ths])
            for r in range(2):
                hs, ths = hseq[r], t_hseq[r]
                c0 = blocks[r][bi][0]
                if store and c0 < NCHL:
                    lo = 0 if r == 0 else 1
                    cp(P, "act", hbf[r][:, :, :, 0:nb], hs[:, :, :, lo:lo + nb], [ths], [t_hbf[r]])
                    dma(P, "act", HD[r].ap()[:, :, :, c0:c0 + nb], hbf[r][:, :, :, 0:nb], reads=[t_hbf[r]])
                e_out = nb if r == 0 else 0
                if final[r] is not None and bi == nblk - 1:
                    cp(P, eng, final[r][0], hs[:, :, 0, e_out], [ths], [final[r][2]])
                    cp(P, eng, final[r][1], hs[:, :, 1, e_out], [ths], [final[r][2]])
    P.barrier()


def l1_summaries_multi(P, k, I, S, UDs, SDs):
    with ExitStack() as st:
        W = table_work(P, st)
        bk = P.sb(st, "bkpad", [128, 16, 2, 4, 128], BF16)
        t_bk = P.tile("bk")
        mset(P, "pool", bk[:].rearrange("p a b c d -> p (a b c d)"), 0.0, [t_bk])
        wrt = [[P.sb(st, f"wrt{r}{i}", [128, 16, 2, 128], BF16) for i in range(2)] for r in range(2)]
        t_wrt = [P.tiles(2, "wrt") for r in range(2)]
        uj = [P.sb(st, f"uj{i}", [128, NTX], BF16) for i in range(2)]
        t_uj = P.tiles(2, "uj")
        ssb = [P.sb(st, f"ssb{i}", [128, 4, 2, 256], F32) for i in range(2)]
        t_ssb = P.tiles(2, "ssb")
        ssc = [P.sb(st, f"ssc{i}", [128, 4, 2, 16], F32) for i in range(2)]
        t_ssc = P.tiles(2, "ssc")
        seq = [(j, sl) for j in range(16) for sl in range(4)]

        def load_u(i):
            j, sl = seq[i]
            nld = NTX if sl == 0 else NT
            dma(P, "sp", uj[i % 2][:, 0:nld], UDs[sl].ap()[j * 128:(j + 1) * 128, 0:nld], writes=[t_uj[i % 2]])
        load_u(0)
        it = 0
        for j in range(16):
            for r in range(2):
                rM0 = r * 64 + 4 * j
                build_cmp(P, S, W, S.bbr, S.bbi, 0, 16, rM0, False)
                pad_cmp(P, S, W, bk, t_bk, 16)
                wr, twr = wrt[r][j % 2], t_wrt[r][j % 2]
                wflat = wr[:].rearrange("p a b c -> p (a b c)")
                for rd in range(8):
                    bank = 4 + (rd % 2)
                    for q in range(4):
                        kp = rd * 4 + q
                        kk, part = divmod(kp, 2)
                        for m in range(4):
                            mm(P, k.ps[:, bank, q * 128:(q + 1) * 128], bk[:, kk, part, m, :], k.identb[:], m == 0, m == 3,
                               [t_bk, k.t_ident], [k.pb[bank]])
                    cp(P, "act", wflat[:, rd * 512:(rd + 1) * 512], k.ps[:, bank, :], [k.pb[bank]], [twr])
            for sl in range(4):
                i = j * 4 + sl
                if i + 1 < len(seq):
                    load_u(i + 1)
                u_, tu = uj[i % 2], t_uj[i % 2]
                u3 = u_[:, 0:NT].rearrange("p (c t) -> p c t", t=TCH)
                u3c = u_[:, NT:NTX].rearrange("p (c t) -> p c t", t=TCH)
                for r in range(2):
                    s_ = it % 2
                    it += 1
                    wr, twr = wrt[r][j % 2], t_wrt[r][j % 2]
                    jobs = ((u3, NCHL, ssb[s_], t_ssb[s_], 0), (u3c, NCHX, ssc[s_], t_ssc[s_], NCHL))
                    for (uv, nch, dst, tdst, dcol) in jobs[0:(2 if sl == 0 else 1)]:
                        for part in range(2):
                            for kk in range(TCH):
                                sidx = (TCH - 1 - kk) if r == 0 else kk
                                for m in range(4):
                                    mm(P, k.ps[:, m, part * 256: part * 256 + nch], wr[32 * m:32 * m + 32, kk, part, :],
                                       uv[32 * m:32 * m + 32, :, sidx], kk == 0, kk == TCH - 1, [twr, tu], [k.pb[m]], tp=(32 * m, 0))
                        for m in range(4):
                            cp(P, "dve" if m % 2 == 0 else "act", dst[:, m, :, :],
                               k.ps[:, m, :].rearrange("p (a b) -> p a b", a=2)[:, :, 0:nch], [k.pb[m]], [tdst])
                        dma(P, "sp", SDs[sl][r].ap()[:, 4 * j:4 * j + 4, :, dcol:dcol + nch], dst[:], reads=[tdst])
    P.barrier()


def l1_carry_foreign(P, k, S, SDs, zs, t_zs):
    CBK = 16
    eng = "dve"
    NS = 3
    with ExitStack() as st:
        sblk = [P.sb(st, f"fsblk{r}", [128, NS, 64, 2, CBK], F32) for r in range(2)]
        t_sblk = P.tiles(2, "fsblk")
        hseq = [P.sb(st, f"fhseq{r}", [128, NS, 64, 2, 2], F32) for r in range(2)]
        t_hseq = P.tiles(2, "fhseq")
        tmp = [[P.sb(st, f"fctmp{r}{i}", [128, NS, 64], F32) for i in range(4)] for r in range(2)]
        t_tmp = [P.tiles(4, "fctmp") for r in range(2)]
        shp = [128, NS, 64]
        Rr = [S.apr[:, 16, r * 64:(r + 1) * 64].unsqueeze(1).to_broadcast(shp) for r in range(2)]
        Ri = [S.api[:, 16, r * 64:(r + 1) * 64].unsqueeze(1).to_broadcast(shp) for r in range(2)]
        lat = [c0 for c0 in range(0, NCHL, CBK)]
        blocks = [lat, lat[::-1]]
        for r in range(2):
            mset(P, eng, hseq[r][:, :, :, :, 0], 0.0, [t_hseq[r]])
        cur = 0
        for bi in range(len(lat)):
            for r in range(2):
                c0 = blocks[r][bi]
                for sl in range(NS):
                    dma(P, "sp", sblk[r][:, sl], SDs[sl + 1][r].ap()[:, :, :, c0:c0 + CBK], writes=[t_sblk[r]])
            for i in range(CBK):
                cc = [i, CBK - 1 - i]
                nxt = 1 - cur
                for r in range(2):
                    hs, ths, tm, ttm = hseq[r], t_hseq[r], tmp[r], t_tmp[r]
                    hr, hi = hs[:, :, :, 0, cur], hs[:, :, :, 1, cur]
                    tt(P, eng, tm[0][:], Rr[r], hr, ALU.mult, [S.t_tab, ths], [ttm[0]])
                    tt(P, eng, tm[1][:], Ri[r], hi, ALU.mult, [S.t_tab, ths], [ttm[1]])
                    tt(P, eng, tm[2][:], Ri[r], hr, ALU.mult, [S.t_tab, ths], [ttm[2]])
                    tt(P, eng, tm[3][:], Rr[r], hi, ALU.mult, [S.t_tab, ths], [ttm[3]])
                for r in range(2):
                    tm, ttm = tmp[r], t_tmp[r]
                    tt(P, eng, tm[0][:], tm[0][:], tm[1][:], ALU.subtract, [ttm[0], ttm[1]], [ttm[0]])
                    tt(P, eng, tm[2][:], tm[2][:], tm[3][:], ALU.add, [ttm[2], ttm[3]], [ttm[2]])
                for r in range(2):
                    hs, ths, tm, ttm = hseq[r], t_hseq[r], tmp[r], t_tmp[r]
                    tt(P, eng, hs[:, :, :, 0, nxt], tm[0][:], sblk[r][:, :, :, 0, cc[r]], ALU.add, [ttm[0], t_sblk[r]], [ths])
                    tt(P, eng, hs[:, :, :, 1, nxt], tm[2][:], sblk[r][:, :, :, 1, cc[r]], ALU.add, [ttm[2], t_sblk[r]], [ths])
                cur = nxt
        for r in range(2):
            for part in range(2):
                cp(P, eng, zs[:, 1:4, r, part, :], hseq[r][:, :, :, part, cur], [t_hseq[r]], [t_zs])
    P.barrier()


def l1_zreduce_foreign(P, k, S, SDs, zs, t_zs):
    NS = 3
    with ExitStack() as st:
        XA = [P.sb(st, f"zxa{r}", [128, NS, 4, 2, 256], F32) for r in range(2)]
        XB = [P.sb(st, f"zxb{r}", [128, NS, 4, 2, 128], F32) for r in range(2)]
        tm = [[P.sb(st, f"zt{r}{i}", [128, NS, 4, 128], F32) for i in range(3)] for r in range(2)]
        t_xa, t_xb = P.tiles(2, "zxa"), P.tiles(2, "zxb")
        t_tm = [P.tiles(3, "zt") for r in range(2)]
        eng = "dve"
        for j in range(16):
            for r in range(2):
                for sl in range(NS):
                    dma(P, "sp", XA[r][:, sl], SDs[sl + 1][r].ap()[:, 4 * j:4 * j + 4, :, 0:NCHL], writes=[t_xa[r]])
            for l_ in range(8):
                n = 128 >> l_
                for r in range(2):
                    src, tsrc = (XA[r], t_xa[r]) if l_ % 2 == 0 else (XB[r], t_xb[r])
                    dst, tdst = (XB[r], t_xb[r]) if l_ % 2 == 0 else (XA[r], t_xa[r])
                    shp = [128, NS, 4, n]
                    c0 = r * 64 + 4 * j
                    Rr = S.rpr[:, l_, c0:c0 + 4].unsqueeze(1).unsqueeze(3).to_broadcast(shp)
                    Ri = S.rpi[:, l_, c0:c0 + 4].unsqueeze(1).unsqueeze(3).to_broadcast(shp)
                    ia, ib = (0, 1) if r == 0 else (1, 0)
                    Ar, Ai = src[:, :, :, 0, ia:2 * n:2], src[:, :, :, 1, ia:2 * n:2]
                    Br, Bi = src[:, :, :, 0, ib:2 * n:2], src[:, :, :, 1, ib:2 * n:2]
                    t1, t2, t3 = (tm[r][i][:, :, :, 0:n] for i in range(3))
                    tt(P, eng, t1, Ar, Rr, ALU.mult, [tsrc, S.t_tab], [t_tm[r][0]])
                    tt(P, eng, t2, Ai, Ri, ALU.mult, [tsrc, S.t_tab], [t_tm[r][1]])
                    tt(P, eng, t3, Ai, Rr, ALU.mult, [tsrc, S.t_tab], [t_tm[r][2]])
                for r in range(2):
                    src, tsrc = (XA[r], t_xa[r]) if l_ % 2 == 0 else (XB[r], t_xb[r])
                    shp = [128, NS, 4, n]
                    c0 = r * 64 + 4 * j
                    Ri = S.rpi[:, l_, c0:c0 + 4].unsqueeze(1).unsqueeze(3).to_broadcast(shp)
                    ia = 0 if r == 0 else 1
                    Ar = src[:, :, :, 0, ia:2 * n:2]
                    t1, t2, t3 = (tm[r][i][:, :, :, 0:n] for i in range(3))
                    tt(P, eng, t1, t1, t2, ALU.subtract, [t_tm[r][0], t_tm[r][1]], [t_tm[r][0]])
                    tt(P, eng, t2, Ar, Ri, ALU.mult, [tsrc, S.t_tab, t_tm[r][1]], [t_tm[r][1]])
                for r in range(2):
                    src, tsrc = (XA[r], t_xa[r]) if l_ % 2 == 0 else (XB[r], t_xb[r])
                    dst, tdst = (XB[r], t_xb[r]) if l_ % 2 == 0 else (XA[r], t_xa[r])
                    ib = 1 if r == 0 else 0
                    Br, Bi = src[:, :, :, 0, ib:2 * n:2], src[:, :, :, 1, ib:2 * n:2]
                    t1, t2, t3 = (tm[r][i][:, :, :, 0:n] for i in range(3))
                    tt(P, eng, t3, t3, t2, ALU.add, [t_tm[r][2], t_tm[r][1]], [t_tm[r][2]])
                    tt(P, eng, dst[:, :, :, 0, 0:n], t1, Br, ALU.add, [t_tm[r][0], tsrc], [tdst])
                    tt(P, eng, dst[:, :, :, 1, 0:n], t3, Bi, ALU.add, [t_tm[r][2], tsrc], [tdst])
            for r in range(2):
                for part in range(2):
                    cp(P, "act", zs[:, 1:4, r, part, 4 * j:4 * j + 4], XA[r][:, :, :, part, 0], [t_xa[r]], [t_zs])
    P.barrier()


def l1_summaries_zr(P, k, I, S, UDs, SD, zs, t_zs):
    NS = 3
    with ExitStack() as st:
        W = table_work(P, st)
        bk = P.sb(st, "bkpad", [128, 16, 2, 4, 128], BF16)
        t_bk = P.tile("bk")
        mset(P, "pool", bk[:].rearrange("p a b c d -> p (a b c d)"), 0.0, [t_bk])
        wrt = [P.sb(st, f"wrt{r}", [128, 16, 2, 128], BF16) for r in range(2)]
        t_wrt = P.tiles(2, "wrt")
        uj = [P.sb(st, f"uj{i}", [128, NTX], BF16) for i in range(2)]
        t_uj = P.tiles(2, "uj")
        ssb = P.sb(st, "ssb", [128, 4, 2, 256], F32)
        t_ssb = P.tile("ssb")
        ssc = P.sb(st, "ssc", [128, 4, 2, 16], F32)
        t_ssc = P.tile("ssc")
        XB = [P.sb(st, f"zxb{r}", [128, NS, 4, 2, 128], F32) for r in range(2)]
        XC = [P.sb(st, f"zxc{r}", [128, NS, 4, 2, 64], F32) for r in range(2)]
        t_xb, t_xc = P.tiles(2, "zxb"), P.tiles(2, "zxc")
        tm0 = [[P.sb(st, f"zl0{a}{i}", [128, 4, 128], F32) for i in range(3)] for a in range(2)]
        t_tm0 = [P.tiles(3, "zl0") for a in range(2)]
        _tm1 = [P.sb(st, f"zl1{i}", [128, NS, 4, 64], F32) for i in range(3)]
        _t_tm1 = P.tiles(3, "zl1")
        tm1 = [_tm1, _tm1]
        t_tm1 = [_t_tm1, _t_tm1]
        seq = [(j, sl) for j in range(16) for sl in range(4)]

        def load_u(i):
            j, sl = seq[i]
            nld = NTX if sl == 0 else NT
            dma(P, "sp", uj[i % 2][:, 0:nld], UDs[sl].ap()[j * 128:(j + 1) * 128, 0:nld], writes=[t_uj[i % 2]])

        def cmuladd(eng, dst_r, dst_i, Ar, Ai, Br, Bi, Rr, Ri, t, tt_, rd):
            tt(P, eng, t[0], Ar, Rr, ALU.mult, rd + [S.t_tab], [tt_[0]])
            tt(P, eng, t[1], Ai, Ri, ALU.mult, rd + [S.t_tab], [tt_[1]])
            tt(P, eng, t[2], Ai, Rr, ALU.mult, rd + [S.t_tab], [tt_[2]])
            tt(P, eng, t[0], t[0], t[1], ALU.subtract, [tt_[0], tt_[1]], [tt_[0]])
            tt(P, eng, t[1], Ar, Ri, ALU.mult, rd + [S.t_tab, tt_[1]], [tt_[1]])
            tt(P, eng, t[2], t[2], t[1], ALU.add, [tt_[2], tt_[1]], [tt_[2]])
            return t[0], t[2]

        load_u(0)
        it = 0
        for j in range(16):
            for r in range(2):
                rM0 = r * 64 + 4 * j
                build_cmp(P, S, W, S.bbr, S.bbi, 0, 16, rM0, False)
                pad_cmp(P, S, W, bk, t_bk, 16)
                wflat = wrt[r][:].rearrange("p a b c -> p (a b c)")
                for rd_ in range(8):
                    bank = 4 + (rd_ % 2)
                    for q in range(4):
                        kp = rd_ * 4 + q
                        kk, part = divmod(kp, 2)
                        for m in range(4):
                            mm(P, k.ps[:, bank, q * 128:(q + 1) * 128], bk[:, kk, part, m, :], k.identb[:], m == 0, m == 3,
                               [t_bk, k.t_ident], [k.pb[bank]])
                    cp(P, "act", wflat[:, rd_ * 512:(rd_ + 1) * 512], k.ps[:, bank, :], [k.pb[bank]], [t_wrt[r]])
            for sl in range(4):
                i = j * 4 + sl
                if i + 1 < len(seq):
                    load_u(i + 1)
                u_, tu = uj[i % 2], t_uj[i % 2]
                u3 = u_[:, 0:NT].rearrange("p (c t) -> p c t", t=TCH)
                u3c = u_[:, NT:NTX].rearrange("p (c t) -> p c t", t=TCH)
                for r in range(2):
                    b0 = 4 * (it % 2)
                    a_ = it % 2
                    it += 1
                    wr, twr = wrt[r], t_wrt[r]
                    pbs = [k.pb[b0 + m] for m in range(4)]

                    def smm(uv, nch):
                        for part in range(2):
                            for kk in range(TCH):
                                sidx = (TCH - 1 - kk) if r == 0 else kk
                                for m in range(4):
                                    mm(P, k.ps[:, b0 + m, part * 256: part * 256 + nch], wr[32 * m:32 * m + 32, kk, part, :],
                                       uv[32 * m:32 * m + 32, :, sidx], kk == 0, kk == TCH - 1, [twr, tu], [pbs[m]], tp=(32 * m, 0))
                    smm(u3, NCHL)
                    if sl == 0:
                        for m in range(4):
                            cp(P, "dve" if m % 2 == 0 else "act", ssb[:, m, :, :],
                               k.ps[:, b0 + m, :].rearrange("p (a b) -> p a b", a=2), [pbs[m]], [t_ssb])
                        dma(P, "sp", SD[r].ap()[:, 4 * j:4 * j + 4, :, 0:NCHL], ssb[:], reads=[t_ssb])
                        smm(u3c, NCHX)
                        for m in range(4):
                            cp(P, "dve" if m % 2 == 0 else "act", ssc[:, m, :, :],
                               k.ps[:, b0 + m, :].rearrange("p (a b) -> p a b", a=2)[:, :, 0:NCHX], [pbs[m]], [t_ssc])
                        dma(P, "sp", SD[r].ap()[:, 4 * j:4 * j + 4, :, NCHL:NCHT], ssc[:], reads=[t_ssc])
                    else:
                        n = 128
                        shp = [128, 4, n]
                        c0 = r * 64 + 4 * j
                        Rr = S.rpr[:, 0, c0:c0 + 4].unsqueeze(2).to_broadcast(shp)
                        Ri = S.rpi[:, 0, c0:c0 + 4].unsqueeze(2).to_broadcast(shp)
                        ia, ib = (0, 1) if r == 0 else (1, 0)
                        pv = k.ps[:, b0:b0 + 4, :]
                        Ar, Ai = pv[:, :, ia:256:2], pv[:, :, 256 + ia:512:2]
                        Br, Bi = pv[:, :, ib:256:2], pv[:, :, 256 + ib:512:2]
                        t = [x[:] for x in tm0[a_]]
                        o_r, o_i = cmuladd("dve", None, None, Ar, Ai, Br, Bi, Rr, Ri, t, t_tm0[a_], pbs)
                        tt(P, "dve", XB[r][:, sl - 1, :, 0, :], o_r, Br, ALU.add, [t_tm0[a_][0]] + pbs, [t_xb[r]])
                        tt(P, "dve", XB[r][:, sl - 1, :, 1, :], o_i, Bi, ALU.add, [t_tm0[a_][2]] + pbs, [t_xb[r]])
            for r in range(2):
                for l_ in range(1, 8):
                    n = 128 >> l_
                    src, tsrc = (XB[r], t_xb[r]) if l_ % 2 == 1 else (XC[r], t_xc[r])
                    dst, tdst = (XC[r], t_xc[r]) if l_ % 2 == 1 else (XB[r], t_xb[r])
                    shp = [128, NS, 4, n]
                    c0 = r * 64 + 4 * j
                    Rr = S.rpr[:, l_, c0:c0 + 4].unsqueeze(1).unsqueeze(3).to_broadcast(shp)
                    Ri = S.rpi[:, l_, c0:c0 + 4].unsqueeze(1).unsqueeze(3).to_broadcast(shp)
                    ia, ib = (0, 1) if r == 0 else (1, 0)
                    Ar, Ai = src[:, :, :, 0, ia:2 * n:2], src[:, :, :, 1, ia:2 * n:2]
                    Br, Bi = src[:, :, :, 0, ib:2 * n:2], src[:, :, :, 1, ib:2 * n:2]
                    t = [x[:, :, :, 0:n] for x in tm1[r]]
                    o_r, o_i = cmuladd("dve", None, None, Ar, Ai, Br, Bi, Rr, Ri, t, t_tm1[r], [tsrc])
                    tt(P, "dve", dst[:, :, :, 0, 0:n], o_r, Br, ALU.add, [t_tm1[r][0], tsrc], [tdst])
                    tt(P, "dve", dst[:, :, :, 1, 0:n], o_i, Bi, ALU.add, [t_tm1[r][2], tsrc], [tdst])
            for r in range(2):
                for part in range(2):
                    cp(P, "act", zs[:, 1:4, r, part, 4 * j:4 * j + 4], XC[r][:, :, :, part, 0], [t_xc[r]], [t_zs])
    P.barrier()


def cmul_add(P, eng, out_r, out_i, a_r, a_i, x_r, x_i, z_r, z_i, tm, R, Wt):
    tt(P, eng, tm[0][:], a_r, x_r, ALU.mult, R, Wt)
    tt(P, eng, tm[1][:], a_i, x_i, ALU.mult, R, Wt)
    tt(P, eng, tm[2][:], a_i, x_r, ALU.mult, R, Wt)
    tt(P, eng, tm[3][:], a_r, x_i, ALU.mult, R, Wt)
    tt(P, eng, tm[0][:], tm[0][:], tm[1][:], ALU.subtract, R, Wt)
    tt(P, eng, tm[2][:], tm[2][:], tm[3][:], ALU.add, R, Wt)
    tt(P, eng, out_r, tm[0][:], z_r, ALU.add, R, Wt)
    tt(P, eng, out_i, tm[2][:], z_i, ALU.add, R, Wt)


def l1_combine(P, k, S, C, zall, tz):
    with ExitStack() as st:
        hk = [P.sb(st, f"hk{i}", [128, 2, 64], F32) for i in range(2)]
        tm = [P.sb(st, f"cbt{i}", [128, 64], F32) for i in range(4)]
        R = [tz, S.t_tab, C.t_c]
        Wt = [C.t_c]
        for r in range(2):
            a_r = S.a4r[:, r * 64:(r + 1) * 64]
            a_i = S.a4i[:, r * 64:(r + 1) * 64]
            order = [0, 1, 2, 3] if r == 0 else [3, 2, 1, 0]
            cur_r, cur_i = C.hctx[:, r, 0, :], C.hctx[:, r, 1, :]
            ts(P, "dve", C.hin[:, r, 0, :], cur_r, S.sel[:, order[0]:order[0] + 1], ALU.mult, R, Wt)
            ts(P, "dve", C.hin[:, r, 1, :], cur_i, S.sel[:, order[0]:order[0] + 1], ALU.mult, R, Wt)
            for idx in range(1, 4):
                kprev, kc = order[idx - 1], order[idx]
                nh = hk[idx % 2]
                cmul_add(P, "dve", nh[:, 0, :], nh[:, 1, :], a_r, a_i, cur_r, cur_i, zall[:, kprev, r, 0, :], zall[:, kprev, r, 1, :],
                         tm, R, Wt)
                cur_r, cur_i = nh[:, 0, :], nh[:, 1, :]
                stt(P, "dve", C.hin[:, r, 0, :], cur_r, S.sel[:, kc:kc + 1], C.hin[:, r, 0, :], ALU.mult, ALU.add, R, Wt)
                stt(P, "dve", C.hin[:, r, 1, :], cur_i, S.sel[:, kc:kc + 1], C.hin[:, r, 1, :], ALU.mult, ALU.add, R, Wt)
    P.barrier()


def l1_outputs(P, k, I, S, UD, HD, GD):
    with ExitStack() as st:
        cr = P.sb(st, "cr", [128, 128, 16], F32)
        ci = P.sb(st, "ci", [128, 128, 16], F32)
        dcol = P.sb(st, "dcol", [128, 16], F32)
        t_c = P.tile("cri")
        dma(P, "sp", cr[:], I["cre_A"].ap(), writes=[t_c])
        dma(P, "sp", ci[:], I["cim_A"].ap(), writes=[t_c])
        dma(P, "sp", dcol[:], I["d_col"].ap(), writes=[t_c])
        W = table_work(P, st)
        bigs = [P.sb(st, f"big{r}", [128, 16, 2, 4, 128], BF16) for r in range(2)]
        t_big = P.tiles(2, "big")
        cpads = [P.sb(st, f"cpad{r}", [128, 1, 2, 4, 128], BF16) for r in range(2)]
        t_cpad = P.tiles(2, "cpad")
        for r in range(2):
            mset(P, "pool", bigs[r][:].rearrange("p a b c d -> p (a b c d)"), 0.0, [t_big[r]])
            mset(P, "pool", cpads[r][:].rearrange("p a b c d -> p (a b c d)"), 0.0, [t_cpad[r]])
        kt = [P.sb(st, f"kt{r}", [128, 16, 128], BF16) for r in range(2)]
        t_kt = P.tiles(2, "kt")
        uj = [P.sb(st, "uj4", [128, NT], BF16)] * 2
        t_uj = [P.tile("uj4")] * 2
        utm = P.sb(st, "utm", [128, TCH, NCHL], BF16)
        t_utm = P.tile("utm")
        hj = [[P.sb(st, f"hj{r}", [128, 4, 2, NCHL], BF16)] * 2 for r in range(2)]
        t_hj = [[P.tile("hj")] * 2 for r in range(2)]
        ysb = P.sb(st, "ysb", [128, 2048], F32)
        t_ysb = P.tile("ysb")
        gst = [P.sb(st, "gst4", [128, NT], BF16)] * 2
        t_gst = [P.tile("gst4")] * 2

        def load_j(j):
            for r in range(2):
                dma(P, "sp", hj[r][j % 2][:], HD[r].ap()[:, 4 * j:4 * j + 4, :, 0:NCHL], writes=[t_hj[r][j % 2]])
        for j in range(16):
            u_, tu = uj[j % 2], t_uj[j % 2]
            dma(P, "sp", u_[:], UD.ap()[j * 128:(j + 1) * 128, 0:NT], writes=[tu])
            load_j(j)
            g_, tg = gst[j % 2], t_gst[j % 2]
            cp(P, "act", utm[:], u_[:].rearrange("p (c t) -> p t c", t=TCH), [tu], [t_utm])
            for r in range(2):
                rM0 = r * 64 + 4 * j
                build_cmp(P, S, W, cr, ci, 0, 1, rM0, True)
                pad_cmp(P, S, W, cpads[r], t_cpad[r], 1)
                build_cmp(P, S, W, S.bbr, S.bbi, 0, 16, rM0, False)
                pad_cmp(P, S, W, bigs[r], t_big[r], 16)
                for rd in range(4):
                    bank = 4 + (rd % 2)
                    for q in range(4):
                        kk = rd * 4 + q
                        i_ = 0
                        for m in range(4):
                            for part in range(2):
                                mm(P, k.ps[:, bank, q * 128:(q + 1) * 128], bigs[r][:, kk, part, m, :], cpads[r][:, 0, part, m, :],
                                   i_ == 0, i_ == 7, [t_big[r], t_cpad[r]], [k.pb[bank]])
                                i_ += 1
                    cp(P, "act", kt[r][:, rd * 4:(rd + 1) * 4, :].rearrange("p a b -> p (a b)"), k.ps[:, bank, :], [k.pb[bank]],
                       [t_kt[r]])
            for r in range(2):
                rM0 = r * 64 + 4 * j
                build_cmp(P, S, W, cr, ci, 1, 16, rM0, True)
                pad_cmp(P, S, W, bigs[r], t_big[r], 16)
            for hf in range(2):
                cs = slice(hf * 128, (hf + 1) * 128)
                mlist = []
                for r in range(2):
                    h_, th = hj[r][j % 2], t_hj[r][j % 2]
                    for kk in range(TCH):
                        for b in range(4):
                            if r == 0:
                                tlo, thi = max(4 * b, kk), 4 * b + 4
                                if tlo >= thi:
                                    continue
                                rhs = utm[:, tlo - kk:thi - kk, cs]
                            else:
                                tlo, thi = 4 * b, min(4 * b + 4, TCH - kk)
                                if tlo >= thi:
                                    continue
                                rhs = utm[:, tlo + kk:thi + kk, cs]
                            out = k.ps[:, b, (tlo - 4 * b) * 128:(thi - 4 * b) * 128].rearrange("p (t c) -> p t c", c=128)
                            mlist.append((b, out, kt[r][:, kk, :], rhs, [t_kt[r], t_utm]))
                    for t in range(TCH):
                        b = t // 4
                        tti = (t + 1) if r == 0 else (TCH - t)
                        for m in range(4):
                            for part in range(2):
                                out = k.ps[:, b, (t % 4) * 128:(t % 4 + 1) * 128]
                                mlist.append((b, out, bigs[r][:, tti - 1, part, m, :], h_[:, m, part, cs], [t_big[r], th]))
                lastidx = {}
                for i_, e in enumerate(mlist):
                    lastidx[e[0]] = i_
                seen = set()
                for i_, (b, out, lhsT, rhs, rds) in enumerate(mlist):
                    mm(P, out, lhsT, rhs, b not in seen, lastidx[b] == i_, rds, [k.pb[b]])
                    seen.add(b)
                pflat = k.ps[:, 0:4, :].rearrange("p a b -> p (a b)").rearrange("p (t c) -> p c t", c=128)
                cp(P, "act", ysb[:].rearrange("p (c t) -> p c t", t=TCH), pflat, [k.pb[0], k.pb[1], k.pb[2], k.pb[3]], [t_ysb])
                stt(P, "dve", ysb[:], u_[:, hf * 2048:(hf + 1) * 2048], dcol[:, j:j + 1], ysb[:], ALU.mult, ALU.add,
                    [tu, t_c, t_ysb], [t_ysb])
                act(P, g_[:, hf * 2048:(hf + 1) * 2048], ysb[:], AF.Gelu_apprx_tanh, [t_ysb], [tg])
            dma(P, "pool", GD.ap()[j * 128:(j + 1) * 128, :], g_[:], reads=[tg])
    P.barrier()


def l1_glu(P, k, I, GD, G2D):
    with ExitStack() as st:
        wg = P.sb(st, "wg", [128, 16, E2], BF16)
        t_wg = P.tile("wg")
        wsrc = I["ssm_w_glu"].ap().rearrange("(j p) c -> p j c", p=128)
        for q in range(8):
            dma(P, "pool", wg[:, q * 2:(q + 1) * 2, :], wsrc[:, q * 2:(q + 1) * 2, :], writes=[t_wg])
        bg = P.sb(st, "bglu", [128, 16], F32)
        t_bg = P.tile("bglu")
        dma(P, "sp", bg[:], I["bglu_col"].ap(), writes=[t_bg])
        gb = [P.sb(st, f"ggb{i}", [128, 16, 512], BF16) for i in range(2)]
        t_gb = P.tiles(2, "ggb")
        g2 = [P.sb(st, f"gg2{i}", [128, 16, 512], BF16) for i in range(2)]
        t_g2 = P.tiles(2, "gg2")
        sg = [P.sb(st, f"sg{i}", [128, 512], F32) for i in range(2)]
        t_sg = P.tiles(2, "sg")
        gsrc = GD.ap().rearrange("(j p) t -> p j t", p=128)
        gdst = G2D.ap().rearrange("(j p) t -> p j t", p=128)

        def load_g(b):
            dma(P, "sp", gb[b % 2][:], gsrc[:, :, b * 512:(b + 1) * 512], writes=[t_gb[b % 2]])
        load_g(0)
        it = 0
        for b in range(8):
            if b + 1 < 8:
                load_g(b + 1)
            g_, tg = gb[b % 2], t_gb[b % 2]
            o_, to = g2[b % 2], t_g2[b % 2]
            for jo in range(16):
                s = it % 2
                it += 1
                bank = s
                for j in range(16):
                    mm(P, k.ps[:, bank, :], wg[:, j, jo * 128:(jo + 1) * 128], g_[:, j, :], j == 0, j == 15, [t_wg, tg], [k.pb[bank]])
                act(P, sg[s][:], k.ps[:, bank, :], AF.Sigmoid, [k.pb[bank], t_bg], [t_sg[s]], scale=1.0, bias=bg[:, jo:jo + 1])
                tt(P, "dve", o_[:, jo, :], g_[:, jo, :], sg[s][:], ALU.mult, [tg, t_sg[s]], [to])
            dma(P, "pool", gdst[:, :, b * 512:(b + 1) * 512], o_[:], reads=[to])
    P.barrier()


def l1_out(P, k, I, X1src, HXD, G2D, OUT):
    with ExitStack() as st:
        wz = P.sb(st, "wz", [128, 8, E2], BF16)
        t_wz = P.tile("wz")
        wsrc = I["ssm_w_in"].ap().rearrange("(dt p) c -> p dt c", p=128)
        for q in range(4):
            dma(P, "pool", wz[:, :, q * 512:(q + 1) * 512], wsrc[:, :, E2 + q * 512:E2 + (q + 1) * 512], writes=[t_wz])
        wo = P.sb(st, "wo1", [128, 16, 1024], BF16)
        t_wo = P.tile("wo1")
        for q in range(4):
            dma(P, "pool", wo[:, q * 4:(q + 1) * 4, :],
                I["ssm_w_out"].ap().rearrange("(j p) d -> p j d", p=128)[:, q * 4:(q + 1) * 4, :], writes=[t_wo])
        hxb = [P.sb(st, f"hxb5{i}", [128, 8, 512], BF16) for i in range(2)]
        t_hxb = P.tiles(2, "hxb5")
        g2 = [P.sb(st, f"g25{i}", [128, 16, 512], BF16) for i in range(2)]
        t_g2 = P.tiles(2, "g25")
        gat = P.sb(st, "gat", [128, 16, 512], BF16)
        t_gat = P.tile("gat")
        sz = [P.sb(st, f"sz5{i}", [128, 512], F32) for i in range(2)]
        t_sz = P.tiles(2, "sz5")
        xt = [P.sb(st, f"xt5{i}", [128, 1024], F32) for i in range(2)]
        t_xt = P.tiles(2, "xt5")
        ot = [P.sb(st, f"ot5{i}", [128, 1024], F32) for i in range(2)]
        t_ot = P.tiles(2, "ot5")
        Ws = ln_work(P, st)
        load_ln_gate(P, k, I, st, 1)
        gsrc = G2D.ap().rearrange("(j p) t -> p j t", p=128)

        def load_b(b):
            dma(P, "sp", hxb[b % 2][:], HXD.ap()[:, :, b * 512:(b + 1) * 512], writes=[t_hxb[b % 2]])
            dma(P, "sp", g2[b % 2][:], gsrc[:, :, b * 512:(b + 1) * 512], writes=[t_g2[b % 2]])
        load_b(0)
        it = 0
        it2 = 0
        for b in range(8):
            if b + 1 < 8:
                load_b(b + 1)
            hb, thb = hxb[b % 2], t_hxb[b % 2]
            g_, tg = g2[b % 2], t_g2[b % 2]
            for j in range(16):
                s = it % 2
                it += 1
                bank = 4 + s
                for dt in range(8):
                    mm(P, k.ps[:, bank, :], wz[:, dt, j * 128:(j + 1) * 128], hb[:, dt, :], dt == 0, dt == 7, [t_wz, thb], [k.pb[bank]])
                act(P, sz[s][:], k.ps[:, bank, :], AF.Silu, [k.pb[bank]], [t_sz[s]])
                tt(P, "dve", gat[:, j, :], g_[:, j, :], sz[s][:], ALU.mult, [tg, t_sz[s]], [t_gat])
            for tl in range(4):
                s = it2 % 2
                it2 += 1
                psb = 2 * s
                r0 = b * 512 + tl * 128
                dma(P, "sp", xt[s][:], X1src.ap()[r0:r0 + 128, :], writes=[t_xt[s]])
                for h in range(2):
                    for j in range(16):
                        mm(P, k.ps[:, psb + h, :], gat[:, j, tl * 128:(tl + 1) * 128], wo[:, j, h * 512:(h + 1) * 512],
                           j == 0, j == 15, [t_gat, t_wo], [k.pb[psb + h]])
                ln_residual(P, k, Ws[s], psb, xt[s][:], t_xt[s], 0, ot[s][:], t_ot[s])
                dma(P, "pool", OUT.ap()[r0:r0 + 128, :], ot[s][:], reads=[t_ot[s]])
    P.barrier()


def _core_inputs_common(inp, core):
    b, kk = core // 4, core % 4
    t0 = kk * NT
    x = inp["x"]
    f32 = np.float32
    d = {}
    xh = np.zeros((NXH, D), f32)
    lo, hi = t0 - HALO, t0 + NT + HALO
    slo, shi = max(lo, 0), min(hi, x.shape[1])
    xh[slo - lo:shi - lo] = x[b, slo:shi]
    d["xT"] = np.ascontiguousarray(xh.T)
    d["xtok"] = np.ascontiguousarray(x[b, t0:t0 + NT])
    d["ctxT"] = np.ascontiguousarray(inp["ctx"][b].T)
    d["ctxtok"] = np.ascontiguousarray(inp["ctx"][b])
    cv = np.stack([inp["c"][b].reshape(8, 128).T, inp["c_ctx"].reshape(8, 128).T], axis=-1)
    d["cvec"] = np.ascontiguousarray(cv.astype(f32))
    edge = np.ones((128, 2), f32)
    if kk == 0:
        edge[:, 0] = 0.0
    if kk == 3:
        edge[:, 1] = 0.0
    d["edge"] = edge
    sel = np.zeros((128, 4), f32)
    sel[:, kk] = 1.0
    d["sel"] = sel
    return d


def _shared_inputs(inp):
    f32 = np.float32
    s = {}
    s["ident"] = np.eye(128, dtype=f32)
    s["ada_w"] = np.ascontiguousarray(inp["ada_w"])
    ab = inp["ada_b"]
    s["adab_col"] = np.ascontiguousarray(ab.reshape(2, 24, 128).transpose(2, 0, 1))
    s["adab_grow"] = np.ascontiguousarray(ab[None, :, 2048:3072])
    s["lngB"] = np.ascontiguousarray(np.broadcast_to(inp["ln_g"][None], (128, 2, D)))
    s["lncol0"] = np.ascontiguousarray(np.stack([inp["ln_g"][0].reshape(8, 128).T, inp["ln_b"][0].reshape(8, 128).T], axis=1))
    s["lnbB"] = np.ascontiguousarray(np.broadcast_to(inp["ln_b"][None], (128, 2, D)))
    s["conv_w_in"] = np.ascontiguousarray(inp["conv_w_in"][0])
    s["conv_w_out"] = np.ascontiguousarray(inp["conv_w_out"][0])
    s["cw"] = np.ascontiguousarray(inp["conv_w"][0].reshape(3, 16, 128).transpose(2, 1, 0))
    return s


L0_INPUTS = {
    "xT": [D, NXH], "xtok": [NT, D], "ctxT": [D, NCX], "ctxtok": [NCX, D], "cvec": [128, 8, 2],
    "edge": [128, 2], "ident": [128, 128], "ada_w": [2, D, 3 * D], "adab_col": [128, 2, 24],
    "adab_grow": [1, 2, D], "lngB": [128, 2, D], "lnbB": [128, 2, D],
    "conv_w_in": [D, 8192], "conv_w_out": [E2, D], "cw": [128, 16, 3],
}
L1_INPUTS = {
    "cvec": [128, 8, 2], "ident": [128, 128], "ada_w": [2, D, 3 * D], "adab_col": [128, 2, 24],
    "adab_grow": [1, 2, D], "lngB": [128, 2, D], "lnbB": [128, 2, D], "sel": [128, 4], "mgi": [128, 2],
    "ssm_w_in": [D, 2 * E2], "ssm_w_glu": [E2, E2], "ssm_w_out": [E2, D], "d_col": [128, 16], "bglu_col": [128, 16],
    "lamre_A": [128, 128], "lamim_A": [128, 128], "logstep_A": [128, 128],
    "bre_A": [128, 128, 16], "bim_A": [128, 128, 16], "cre_A": [128, 128, 16], "cim_A": [128, 128, 16],
}


def _shared_inputs_l1(inp):
    f32 = np.float32
    s = {}
    s["ssm_w_in"] = np.ascontiguousarray(inp["ssm_w_in"][0])
    s["ssm_w_glu"] = np.ascontiguousarray(inp["ssm_w_glu"][0])
    s["ssm_w_out"] = np.ascontiguousarray(inp["ssm_w_out"][0])
    s["d_col"] = np.ascontiguousarray(inp["ssm_d"][0].reshape(16, 128).T)
    s["bglu_col"] = np.ascontiguousarray(inp["ssm_b_glu"][0].reshape(16, 128).T)

    def lamA(a):
        return np.ascontiguousarray(a.reshape(2, 64, 2, 64).transpose(2, 3, 0, 1).reshape(128, 128))
    s["lamre_A"] = lamA(inp["ssm_lam_re"][0])
    s["lamim_A"] = lamA(inp["ssm_lam_im"][0])
    ls = inp["ssm_log_step"][0].reshape(2, 64, 2).transpose(2, 0, 1)
    s["logstep_A"] = np.ascontiguousarray(np.broadcast_to(ls[:, None], (2, 64, 2, 64)).reshape(128, 128))

    def bA(a):
        return np.ascontiguousarray(a.reshape(2, 64, 2, 64, 16).transpose(2, 3, 0, 1, 4).reshape(128, 128, 16))

    def cA(a):
        return np.ascontiguousarray(a.reshape(2, 64, 2, 16, 64).transpose(2, 4, 0, 1, 3).reshape(128, 128, 16))
    s["bre_A"] = bA(inp["ssm_b_re"][0])
    s["bim_A"] = bA(inp["ssm_b_im"][0])
    s["cre_A"] = cA(inp["ssm_c_re"][0])
    s["cim_A"] = cA(inp["ssm_c_im"][0])
    mgi = np.zeros((128, 2), f32)
    mgi[:64, 0] = 1.0
    mgi[64:, 1] = 1.0
    s["mgi"] = mgi
    return s


def carry_state(P, st):
    C = K()
    C.hctx = P.sb(st, "hctx", [128, 2, 2, 64], F32)
    C.hin = P.sb(st, "hin", [128, 2, 2, 64], F32)
    C.z = P.sb(st, "zloc", [128, 2, 2, 64], F32)
    C.t_c = P.tile("cstate")
    return C


def layer1(P, k, I, mode, X1src, CTX1src, ZOUT, ZALL, OUT, X1F=None):
    dbg = "ExternalOutput" if DEBUG else "Internal"
    UD = P.dram("ud", [E2, NTX], BF16, kind=dbg)
    HXD = P.dram("hxd", [128, 8, NTX], BF16)
    SD = [P.dram(f"sd{r}", [128, 64, 2, NCHT], F32) for r in range(2)]
    HD = [P.dram(f"hd{r}", [128, 64, 2, NCHL], BF16, kind=dbg) for r in range(2)]
    GD = P.dram("gd", [E2, NT], BF16, kind=dbg)
    G2D = P.dram("g2d", [E2, NT], BF16)
    ada(P, k, I, 1)
    with ExitStack() as st:
        S = s5_tables(P, k, I, st)
        C = carry_state(P, st)
        zall = P.sb(st, "zall", [128, 4, 2, 2, 64], F32)
        tz = P.tile("zall")
        l1_inproj(P, k, I, X1src, CTX1src, UD, HXD)
        if mode in ("A", "B"):
            l1_summaries(P, k, I, S, UD, SD)
        if mode == "A":
            fin = [(C.z[:, r, 0, :], C.z[:, r, 1, :], C.t_c) for r in range(2)]
            l1_carry(P, k, S, SD, HD, [None, None], fin, False, True, False)
            dma(P, "sp", ZOUT.ap(), C.z[:], reads=[C.t_c])
            P.barrier()
        if mode == "FUSED":
            UDF = [P.dram(f"udf{i}", [E2, NT], BF16) for i in range(3)]
            SDF = [[P.dram(f"sdf{i}_{r}", [128, 64, 2, NCHL], F32) for r in range(2)] for i in range(3)]
            zs = P.sb(st, "zs", [128, 4, 2, 2, 64], F32)
            perm = P.sb(st, "perm", [128, 16], F32)
            t_zs = P.tile("zs")
            dma(P, "sp", perm[:], I["perm"].ap(), writes=[t_zs])
            mset(P, "dve", zs[:].rearrange("p a b c d -> p (a b c d)"), 0.0, [t_zs])
            lncol = P.sb(st, "lncol", [128, 2, 8], F32)
            k.mcol2 = P.sb(st, "mcol2", [128, 16], F32)
            k.t_mcol2 = P.tile("mcol2")
            dma(P, "sp", lncol[:], I["lncol0"].ap(), writes=[k.t_mcol2])
            tt(P, "dve", k.mcol2[:, 8:16], lncol[:, 0, :], k.mcol[:, 8:16, 0], ALU.mult, [k.t_mcol, k.t_mcol2], [k.t_mcol2])
            tt(P, "dve", k.mcol2[:, 0:8], lncol[:, 1, :], k.mcol[:, 8:16, 0], ALU.mult, [k.t_mcol, k.t_mcol2], [k.t_mcol2])
            tt(P, "dve", k.mcol2[:, 0:8], k.mcol2[:, 0:8], k.mcol[:, 0:8, 0], ALU.add, [k.t_mcol, k.t_mcol2], [k.t_mcol2])
            for slot in range(1, 4):
                l1_inproj(P, k, I, X1F[slot - 1], None, UDF[slot - 1], HXD, do_ctx=False, store_hx=False, fold=True)
            if MERGED_ZR:
                l1_summaries_zr(P, k, I, S, [UD] + UDF, SD, zs, t_zs)
            else:
                l1_summaries_multi(P, k, I, S, [UD] + UDF, [SD] + SDF)
                l1_zreduce_foreign(P, k, S, [SD] + SDF, zs, t_zs)
            zf = zall[:].rearrange("p a b c d -> p a (b c d)")
            zsf = zs[:].rearrange("p a b c d -> p a (b c d)")
            for j in range(4):
                for sl in range(4):
                    if sl == 0:
                        ts(P, "dve", zf[:, j, :], zsf[:, sl, :], perm[:, sl * 4 + j:sl * 4 + j + 1], ALU.mult, [t_zs, tz], [tz])
                    else:
                        stt(P, "dve", zf[:, j, :], zsf[:, sl, :], perm[:, sl * 4 + j:sl * 4 + j + 1], zf[:, j, :], ALU.mult, ALU.add,
                            [t_zs, tz], [tz])
            P.barrier()
        if mode == "B":
            dma(P, "sp", zall[:], ZALL.ap(), writes=[tz])
        if mode in ("B", "FUSED"):
            fin = [(C.hctx[:, r, 0, :], C.hctx[:, r, 1, :], C.t_c) for r in range(2)]
            l1_carry(P, k, S, SD, HD, [None, None], fin, True, False, False)
            l1_combine(P, k, S, C, zall, tz)
            ini = [(C.hin[:, r, 0, :], C.hin[:, r, 1, :], C.t_c) for r in range(2)]
            l1_carry(P, k, S, SD, HD, ini, [None, None], False, True, True)
            l1_outputs(P, k, I, S, UD, HD, GD)
    if mode in ("B", "FUSED"):
        l1_glu(P, k, I, GD, G2D)
        l1_out(P, k, I, X1src, HXD, G2D, OUT)


FUSED_INPUTS = dict(L1_INPUTS)
FUSED_INPUTS.update({kk_: v for kk_, v in L0_INPUTS.items() if kk_ not in ("xT", "xtok", "edge")})
for _s in range(4):
    FUSED_INPUTS[f"xT{_s}"] = [D, NXH]
    FUSED_INPUTS[f"xtok{_s}"] = [NT, D]
    FUSED_INPUTS[f"edge{_s}"] = [128, 2]
FUSED_INPUTS["perm"] = [128, 16]
FUSED_INPUTS["lncol0"] = [128, 2, 8]


def build(mode):
    P = Prog()
    k = K()
    I = {}
    if mode == "L0":
        for nm, shp in L0_INPUTS.items():
            I[nm] = P.dram(nm, shp, F32, kind="ExternalInput")
        X1 = P.dram("x1_out", [NT, D], F32, kind="ExternalOutput")
        CTX1 = P.dram("ctx1_out", [NCX, D], F32, kind="ExternalOutput")
        G0 = P.dram("g0", [E2, NTX], BF16)
        setup_globals(P, k, I)
        layer0(P, k, I, X1, CTX1, G0)
    elif mode in ("A", "B"):
        for nm, shp in L1_INPUTS.items():
            I[nm] = P.dram(nm, shp, F32, kind="ExternalInput")
        X1 = P.dram("x1_in", [NT, D], F32, kind="ExternalInput")
        CTX1 = P.dram("ctx1_in", [NCX, D], F32, kind="ExternalInput")
        ZOUT = ZALL = OUT = None
        if mode == "A":
            ZOUT = P.dram("z_out", [128, 2, 2, 64], F32, kind="ExternalOutput")
        else:
            ZALL = P.dram("z_all", [128, 4, 2, 2, 64], F32, kind="ExternalInput")
            OUT = P.dram("out", [NT, D], F32, kind="ExternalOutput")
        setup_globals(P, k, I)
        layer1(P, k, I, mode, X1, CTX1, ZOUT, ZALL, OUT)
    elif mode == "FUSED":
        for nm, shp in FUSED_INPUTS.items():
            I[nm] = P.dram(nm, shp, F32, kind="ExternalInput")
        OUT = P.dram("out", [NT, D], F32, kind="ExternalOutput")
        X1 = P.dram("x1s", [NT, D], F32)
        X1F = [P.dram(f"x1f{i}", [NT, D], F32) for i in range(3)]
        CTX1 = P.dram("ctx1s", [NCX, D], F32)
        G0 = P.dram("g0", [E2, NTX], BF16)
        setup_globals(P, k, I)
        for slot in range(4):
            layer0(P, k, I, X1 if slot == 0 else X1F[slot - 1], CTX1, G0, slot=slot, do_ctx=(slot == 0), do_ada=(slot == 0),
                   ln_affine=(slot == 0))
        layer1(P, k, I, "FUSED", X1, CTX1, None, None, OUT, X1F=X1F)
    P.emit()
    return P


def _slot_inputs(inp, core):
    b, kk = core // 4, core % 4
    f32 = np.float32
    x = inp["x"]
    order = [kk] + [j for j in range(4) if j != kk]
    d = {}
    perm = np.zeros((128, 16), f32)
    for sl, pos in enumerate(order):
        t0 = pos * NT
        xh = np.zeros((NXH, D), f32)
        lo, hi = t0 - HALO, t0 + NT + HALO
        slo, shi = max(lo, 0), min(hi, x.shape[1])
        xh[slo - lo:shi - lo] = x[b, slo:shi]
        d[f"xT{sl}"] = np.ascontiguousarray(xh.T)
        d[f"xtok{sl}"] = np.ascontiguousarray(x[b, t0:t0 + NT])
        edge = np.ones((128, 2), f32)
        if pos == 0:
            edge[:, 0] = 0.0
        if pos == 3:
            edge[:, 1] = 0.0
        d[f"edge{sl}"] = edge
        perm[:, sl * 4 + pos] = 1.0
    d["perm"] = perm
    return d


def run_fused(inp):
    P = build("FUSED")
    sh = _shared_inputs(inp)
    sh.update(_shared_inputs_l1(inp))
    maps = []
    for core in range(8):
        d = _core_inputs_common(inp, core)
        d.update(_slot_inputs(inp, core))
        maps.append({nm: (d[nm] if nm in d else sh[nm]) for nm in FUSED_INPUTS})
    res = run_bass_kernel_spmd(P.nc, maps, core_ids=list(range(8)))
    return res.results


def run_L0(inp):
    P = build("L0")
    sh = _shared_inputs(inp)
    maps = []
    for core in range(8):
        d = _core_inputs_common(inp, core)
        maps.append({nm: (d[nm] if nm in d else sh[nm]) for nm in L0_INPUTS})
    res = run_bass_kernel_spmd(P.nc, maps, core_ids=list(range(8)))
    return res.results


def run_L1(inp, mode, x1s, ctx1s, zalls=None):
    P = build(mode)
    sh = _shared_inputs(inp)
    sh.update(_shared_inputs_l1(inp))
    maps = []
    for core in range(8):
        d = _core_inputs_common(inp, core)
        m = {nm: (d[nm] if nm in d else sh[nm]) for nm in L1_INPUTS}
        m["x1_in"] = x1s[core]
        m["ctx1_in"] = ctx1s[core]
        if mode == "B":
            m["z_all"] = zalls[core // 4]
        maps.append(m)
    res = run_bass_kernel_spmd(P.nc, maps, core_ids=list(range(8)))
    return res.results


def kernel_unfused(**inputs):
    inp = {k_: np.asarray(v, dtype=np.float32) for k_, v in inputs.items()}
    r0 = run_L0(inp)
    x1s = [np.ascontiguousarray(r0[c]["x1_out"]) for c in range(8)]
    ctx1s = [np.ascontiguousarray(r0[c]["ctx1_out"]) for c in range(8)]
    ra = run_L1(inp, "A", x1s, ctx1s)
    zalls = [np.ascontiguousarray(np.stack([ra[b * 4 + kk]["z_out"] for kk in range(4)], axis=1)) for b in range(2)]
    rb = run_L1(inp, "B", x1s, ctx1s, zalls)
    out = np.empty((2, 4 * NT, D), np.float32)
    for c in range(8):
        out[c // 4, (c % 4) * NT:(c % 4 + 1) * NT] = rb[c]["out"]
    return out


def kernel(**inputs):
    inp = {k_: np.asarray(v, dtype=np.float32) for k_, v in inputs.items()}
    rf = run_fused(inp)
    out = np.empty((2, 4 * NT, D), np.float32)
    for c in range(8):
        out[c // 4, (c % 4) * NT:(c % 4 + 1) * NT] = rf[c]["out"]
    return out
```
